# Optimizing a Trainium2 kernel written in Bass

```python
import math
import jax, jax.numpy as jnp
from jax import lax
import numpy as np

D_MODEL = 1024
BATCH = 8
SEQ = 8192
DEPTH = 2
DEC_BATCH = 16
DEC_SEQ = 2048
PAST_LEN = 128

GRID_W = 64
Q_BLOCK = 128
HEAD_DIM = 64
EPS = 1e-6

HY_WIDTH = 512
HY_ORDER = 2
SHORT_K = 3
FILT_EMB = 33
FILT_ORDER = 64
FILT_INNER = 2
HY_FAST_DECAY = 0.3
HY_SLOW_DECAY = 1.5
HY_TARGET = 1e-2
HY_SHIFT = 0.0

GQA_HEADS = 8
GQA_KV_HEADS = 2
ROPE_THETA = 10000.0

DIFF_HEADS = 4
REL_BUCKETS = 32
REL_MAX_DIST = 128

N_BRANCH = 3
COL_SIZES = (
    (HY_ORDER + 1) * HY_WIDTH,
    HY_WIDTH,
    GQA_HEADS * HEAD_DIM,
    GQA_KV_HEADS * HEAD_DIM,
    GQA_KV_HEADS * HEAD_DIM,
    GQA_HEADS * HEAD_DIM,
    DIFF_HEADS * 2 * HEAD_DIM,
    DIFF_HEADS * 2 * HEAD_DIM,
    DIFF_HEADS * 2 * HEAD_DIM,
    DIFF_HEADS * 2 * HEAD_DIM,
)
IN_COLS = sum(COL_SIZES)
GQA_WIDTH = GQA_HEADS * HEAD_DIM
DIFF_WIDTH = DIFF_HEADS * 2 * HEAD_DIM

kernel_name = 'hybrid_hyena_gqa_diffattn_encoder'


def rms_norm(x, g):
    xf = x.astype(jnp.float32)
    y = xf * lax.rsqrt(jnp.mean(xf * xf, axis=-1, keepdims=True) + EPS)
    return (y * g.astype(jnp.float32)).astype(x.dtype)


def split_columns(u):
    parts, start = [], 0
    for size in COL_SIZES:
        parts.append(u[..., start:start + size])
        start += size
    return parts


def axial_rope_tables(L):
    rows = L // GRID_W
    row = jnp.broadcast_to(jnp.arange(rows, dtype=jnp.float32)[:, None], (rows, GRID_W)).reshape(L)
    col = jnp.broadcast_to(jnp.arange(GRID_W, dtype=jnp.float32)[None, :], (rows, GRID_W)).reshape(L)
    n_freq = HEAD_DIM // 4
    inv_freq = ROPE_THETA ** (-jnp.arange(n_freq, dtype=jnp.float32) / n_freq)
    ang = jnp.concatenate([row[:, None] * inv_freq, col[:, None] * inv_freq], axis=-1)
    return jnp.cos(ang), jnp.sin(ang)


def apply_rope(x, cos, sin):
    half = x.shape[-1] // 2
    x1, x2 = x[..., :half], x[..., half:]
    c, s = cos[None, :, None, :], sin[None, :, None, :]
    return jnp.concatenate([x1 * c - x2 * s, x2 * c + x1 * s], axis=-1).astype(x.dtype)


def t5_bucket(rel):
    nb = REL_BUCKETS // 2
    max_exact = nb // 2
    ret = (rel > 0).astype(jnp.int32) * nb
    n = jnp.abs(rel)
    nf = jnp.maximum(n, 1).astype(jnp.float32)
    large = max_exact + (jnp.log(nf / max_exact) / math.log(REL_MAX_DIST / max_exact)
                         * (nb - max_exact)).astype(jnp.int32)
    large = jnp.minimum(large, nb - 1)
    return ret + jnp.where(n < max_exact, n, large)


def hyena_positions(L):
    t01 = jnp.linspace(0.0, 1.0, L, dtype=jnp.float32)[:, None]
    bands = (FILT_EMB - 1) // 2
    w = 2.0 * math.pi * jnp.arange(L, dtype=jnp.float32)[:, None] / L
    f = jnp.linspace(1e-4, bands - 1, bands, dtype=jnp.float32)[None, :]
    z = jnp.concatenate([t01, jnp.cos(f * w), -jnp.sin(f * w)], axis=-1)
    max_decay = math.log(HY_TARGET) / HY_FAST_DECAY
    min_decay = math.log(HY_TARGET) / HY_SLOW_DECAY
    deltas = jnp.linspace(min_decay, max_decay, HY_WIDTH, dtype=jnp.float32)
    window = jnp.exp(-t01 * jnp.abs(deltas)[None, :]) + HY_SHIFT
    return z, window


def centred_short_conv(u, w, b):
    L = u.shape[1]
    pad = SHORT_K // 2
    up = jnp.pad(u, ((0, 0), (pad, pad), (0, 0)))
    out = up[:, 0:L] * w[0]
    for j in range(1, SHORT_K):
        out = out + up[:, j:j + L] * w[j]
    return out + b


def hyena_filter(z, window, w1, b1, w2, b2, wout, freq):
    a = jnp.sin(freq * (z @ w1 + b1))
    for i in range(FILT_INNER):
        a = jnp.sin(freq * (a @ w2[i] + b2[i]))
    hf = (a @ wout).reshape(-1, 2, HY_WIDTH) * window[:, None, :]
    fwd, bwd = hf[:, 0], hf[:, 1]
    return jnp.concatenate([fwd, jnp.zeros_like(fwd[:1]), bwd[:0:-1]], axis=0).astype(jnp.float32)


def bidir_fftconv(u, c):
    L = u.shape[1]
    uf = jnp.fft.rfft(u.astype(jnp.float32), n=2 * L, axis=1)
    cf = jnp.fft.rfft(c, n=2 * L, axis=0)
    y = jnp.fft.irfft(uf * cf[None], n=2 * L, axis=1)[:, :L]
    return y.astype(u.dtype)


def gqa_attention(q, k, v):
    B, L, H, D = q.shape
    G = H // GQA_KV_HEADS
    nb = L // Q_BLOCK
    qb = q.reshape(B, nb, Q_BLOCK, GQA_KV_HEADS, G, D).transpose(1, 0, 2, 3, 4, 5)
    scale = D ** -0.5

    def block(qi):
        s = jnp.einsum('bqkgd,bskd->bkgqs', qi, k).astype(jnp.float32) * scale
        p = jax.nn.softmax(s, axis=-1).astype(v.dtype)
        return jnp.einsum('bkgqs,bskd->bqkgd', p, v)

    o = lax.map(block, qb)
    return o.transpose(1, 0, 2, 3, 4, 5).reshape(B, L, H * D)


def diff_attention(q, k, v, rel_bias, lam, lam_init, subln_g):
    B, L, H, _, D = q.shape
    nb = L // Q_BLOCK
    qb = q.reshape(B, nb, Q_BLOCK, H, 2, D).transpose(1, 0, 2, 3, 4, 5)
    starts = jnp.arange(nb, dtype=jnp.int32) * Q_BLOCK
    kpos = jnp.arange(L, dtype=jnp.int32)
    scale = D ** -0.5

    def block(args):
        qi, q0 = args
        qpos = q0 + jnp.arange(Q_BLOCK, dtype=jnp.int32)
        bucket = t5_bucket(kpos[None, :] - qpos[:, None])
        bias = jnp.take(rel_bias, bucket, axis=0).transpose(2, 0, 1).astype(jnp.float32)
        s = jnp.einsum('bqhcd,bshcd->bhcqs', qi, k).astype(jnp.float32) * scale + bias[None, :, None]
        p = jax.nn.softmax(s, axis=-1)
        a = p[:, :, 0] - lam * p[:, :, 1]
        return jnp.einsum('bhqs,bshe->bqhe', a.astype(v.dtype), v)

    o = lax.map(block, (qb, starts))
    o = o.transpose(1, 0, 2, 3, 4).reshape(B, L, H, 2 * D)
    o = rms_norm(o, subln_g) * (1.0 - lam_init)
    return o.reshape(B, L, H * 2 * D)


def mixer_layer(x, l, p, rope_cos, rope_sin, z, window, rel_bias):
    B, L, _ = x.shape
    h = rms_norm(x, p['norm_g'][l])
    u = h @ p['w_in'][l]
    (hy_u, hy_gate, gq_q, gq_k, gq_v, gq_gate, df_q, df_k, df_v, df_gate) = split_columns(u)

    hy = centred_short_conv(hy_u, p['hy_conv_w'][l], p['hy_conv_b'][l])
    x0, x1, hv = hy[..., :HY_WIDTH], hy[..., HY_WIDTH:2 * HY_WIDTH], hy[..., 2 * HY_WIDTH:]
    filt = hyena_filter(z, window, p['hy_f_w1'][l], p['hy_f_b1'][l], p['hy_f_w2'][l],
                        p['hy_f_b2'][l], p['hy_f_wout'][l], p['hy_f_freq'][l])
    hv = hv * x1
    hv = bidir_fftconv(hv, filt) + hv * p['hy_bias'][l]
    y_hy = (hv * x0) * jax.nn.silu(hy_gate)

    q = gq_q.reshape(B, L, GQA_HEADS, HEAD_DIM)
    k = gq_k.reshape(B, L, GQA_KV_HEADS, HEAD_DIM)
    v = gq_v.reshape(B, L, GQA_KV_HEADS, HEAD_DIM)
    q = apply_rope(rms_norm(q, p['q_norm_g'][l]), rope_cos, rope_sin)
    k = apply_rope(rms_norm(k, p['k_norm_g'][l]), rope_cos, rope_sin)
    y_gq = gqa_attention(q, k, v) * jax.nn.silu(gq_gate)

    dq = df_q.reshape(B, L, DIFF_HEADS, 2, HEAD_DIM)
    dk = df_k.reshape(B, L, DIFF_HEADS, 2, HEAD_DIM)
    dv = df_v.reshape(B, L, DIFF_HEADS, 2 * HEAD_DIM)
    lam_init = 0.8 - 0.6 * math.exp(-0.3 * l)
    lam = (jnp.exp(jnp.sum(p['lam_q1'][l].astype(jnp.float32) * p['lam_k1'][l].astype(jnp.float32)))
           - jnp.exp(jnp.sum(p['lam_q2'][l].astype(jnp.float32) * p['lam_k2'][l].astype(jnp.float32)))
           + lam_init)
    y_df = diff_attention(dq, dk, dv, rel_bias, lam, lam_init, p['diff_subln_g'][l]) * jax.nn.silu(df_gate)

    gates = jax.nn.sigmoid((h @ p['w_merge'][l] + p['b_merge'][l]).astype(jnp.float32))
    gates = gates.reshape(B, L, N_BRANCH, D_MODEL).astype(x.dtype)
    merged = (gates[:, :, 0] * (y_hy @ p['w_branch_hy'][l])
              + gates[:, :, 1] * (y_gq @ p['w_branch_gqa'][l])
              + gates[:, :, 2] * (y_df @ p['w_branch_diff'][l]))
    return merged @ p['w_out'][l]


def encoder_trunk(x, p, rel_bias, final_g):
    L = x.shape[1]
    rope_cos, rope_sin = axial_rope_tables(L)
    z, window = hyena_positions(L)
    for l in range(DEPTH):
        x = x + mixer_layer(x, l, p, rope_cos, rope_sin, z, window, rel_bias)
    return rms_norm(x, final_g)


def setup_inputs(seed: int = 0) -> dict:
    key = jax.random.key(seed)
    ks = iter(list(jax.random.split(key, 40)))

    def nrm(shape, scale):
        return jax.random.normal(next(ks), shape, jnp.float32) * scale

    def gain(shape, base=1.0):
        return base + 0.02 * jax.random.normal(next(ks), shape, jnp.float32)

    d = {}
    d['x_prompt'] = nrm((BATCH, SEQ, D_MODEL), 1.0)
    d['x_sample'] = nrm((DEC_BATCH, DEC_SEQ, D_MODEL), 1.0)
    d['rel_bias'] = nrm((REL_BUCKETS, DIFF_HEADS), 0.5)
    d['norm_g'] = gain((DEPTH, D_MODEL))
    d['w_in'] = nrm((DEPTH, D_MODEL, IN_COLS), D_MODEL ** -0.5)
    d['hy_conv_w'] = nrm((DEPTH, SHORT_K, (HY_ORDER + 1) * HY_WIDTH), SHORT_K ** -0.5)
    d['hy_conv_b'] = nrm((DEPTH, (HY_ORDER + 1) * HY_WIDTH), 0.02)
    d['hy_f_w1'] = nrm((DEPTH, FILT_EMB, FILT_ORDER), FILT_EMB ** -0.5)
    d['hy_f_b1'] = nrm((DEPTH, FILT_ORDER), 0.1)
    d['hy_f_w2'] = nrm((DEPTH, FILT_INNER, FILT_ORDER, FILT_ORDER), FILT_ORDER ** -0.5)
    d['hy_f_b2'] = nrm((DEPTH, FILT_INNER, FILT_ORDER), 0.1)
    d['hy_f_wout'] = nrm((DEPTH, FILT_ORDER, 2 * HY_WIDTH), 0.05 * FILT_ORDER ** -0.5)
    d['hy_f_freq'] = gain((DEPTH, FILT_ORDER))
    d['hy_bias'] = nrm((DEPTH, HY_WIDTH), 1.0)
    d['q_norm_g'] = gain((DEPTH, HEAD_DIM))
    d['k_norm_g'] = gain((DEPTH, HEAD_DIM))
    d['lam_q1'] = nrm((DEPTH, HEAD_DIM), 0.1)
    d['lam_k1'] = nrm((DEPTH, HEAD_DIM), 0.1)
    d['lam_q2'] = nrm((DEPTH, HEAD_DIM), 0.1)
    d['lam_k2'] = nrm((DEPTH, HEAD_DIM), 0.1)
    d['diff_subln_g'] = gain((DEPTH, 2 * HEAD_DIM))
    d['w_branch_hy'] = nrm((DEPTH, HY_WIDTH, D_MODEL), HY_WIDTH ** -0.5)
    d['w_branch_gqa'] = nrm((DEPTH, GQA_WIDTH, D_MODEL), GQA_WIDTH ** -0.5)
    d['w_branch_diff'] = nrm((DEPTH, DIFF_WIDTH, D_MODEL), DIFF_WIDTH ** -0.5)
    d['w_merge'] = nrm((DEPTH, D_MODEL, N_BRANCH * D_MODEL), D_MODEL ** -0.5)
    d['b_merge'] = nrm((DEPTH, N_BRANCH * D_MODEL), 0.02)
    d['w_out'] = nrm((DEPTH, D_MODEL, D_MODEL), D_MODEL ** -0.5)
    d['final_g'] = gain((D_MODEL,))
    return d


def reference(x_prompt, x_sample, rel_bias, norm_g, w_in, hy_conv_w, hy_conv_b, hy_f_w1, hy_f_b1,
              hy_f_w2, hy_f_b2, hy_f_wout, hy_f_freq, hy_bias, q_norm_g, k_norm_g, lam_q1, lam_k1,
              lam_q2, lam_k2, diff_subln_g, w_branch_hy, w_branch_gqa, w_branch_diff, w_merge, b_merge,
              w_out, final_g):
    p = dict(norm_g=norm_g, w_in=w_in, hy_conv_w=hy_conv_w, hy_conv_b=hy_conv_b, hy_f_w1=hy_f_w1,
             hy_f_b1=hy_f_b1, hy_f_w2=hy_f_w2, hy_f_b2=hy_f_b2, hy_f_wout=hy_f_wout, hy_f_freq=hy_f_freq,
             hy_bias=hy_bias, q_norm_g=q_norm_g, k_norm_g=k_norm_g, lam_q1=lam_q1, lam_k1=lam_k1,
             lam_q2=lam_q2, lam_k2=lam_k2, diff_subln_g=diff_subln_g, w_branch_hy=w_branch_hy,
             w_branch_gqa=w_branch_gqa, w_branch_diff=w_branch_diff, w_merge=w_merge, b_merge=b_merge,
             w_out=w_out)
    y_prompt = encoder_trunk(x_prompt, p, rel_bias, final_g)
    y_sample = encoder_trunk(x_sample, p, rel_bias, final_g)
    return (y_prompt, y_sample)
```

```python
import math
from contextlib import ExitStack

import numpy as np
import ml_dtypes

import concourse.bass as bass
import concourse.mybir as mybir
from concourse.bass_utils import run_bass_kernel_spmd

F32 = mybir.dt.float32
BF16 = mybir.dt.bfloat16
AF = mybir.ActivationFunctionType
ALU = mybir.AluOpType
AX = mybir.AxisListType

D = 1024
NCOL = 5376
HW = 512
EPS = 1e-6
DEPTH = 2
NCORES = 8
PI = math.pi

SEQS = [8192, 2048, 2048]
DEBUG_OUT = set()
STOP_AFTER = None

ENGS = ["pe", "act", "dve", "pool", "sp"]
BLK = {"pe": "tensor", "act": "scalar", "dve": "vector", "pool": "gpsimd", "sp": "sync"}


def fft_dims(L):
    n = 2 * L
    n1 = 1
    while n1 * n1 < n:
        n1 *= 2
    assert n1 * n1 == n, "2L must be a power of 4"
    return n1, n1


class DSem:
    def __init__(self, h, glob=False):
        self.h = h
        self.val = 0
        self.glob = glob


class Phase:
    def __init__(self, nc, name):
        self.nc = nc
        self.name = name
        self.es = ExitStack()
        self.q = {e: [] for e in ENGS}
        self.cnt = {e: 0 for e in ENGS}
        self.csem = {e: self.es.enter_context(nc.semaphore(f"{name}_{e}")) for e in ENGS}
        self.seen = {e: {} for e in ENGS}
        self.dsems = []
        self.gused = []
        self.n = 0

    def sb(self, name, shape, dt):
        return self.es.enter_context(self.nc.sbuf_tensor(f"{self.name}_{name}", list(shape), dt))

    def ps(self, name, shape, dt=F32):
        return self.es.enter_context(self.nc.psum_tensor(f"{self.name}_{name}", list(shape), dt))

    def dsem(self, name):
        d = DSem(self.es.enter_context(self.nc.semaphore(f"{self.name}_d_{name}")))
        self.dsems.append(d)
        return d

    def _waits(self, eng, deps):
        w = []
        for t in deps:
            if t is None:
                continue
            kind, key, val = t
            if kind == "c":
                if key == "pe" and eng == "pe":
                    continue
                h = self.csem[key]
                k = ("c", key)
            else:
                h = key.h
                k = ("d", id(key))
            if self.seen[eng].get(k, 0) >= val:
                continue
            self.seen[eng][k] = val
            w.append((h, val))
        return w

    def op(self, eng, fn, deps=()):
        w = self._waits(eng, deps)
        self.cnt[eng] += 1
        self.q[eng].append((w, fn, self.csem[eng], 1))
        self.n += 1
        return ("c", eng, self.cnt[eng])

    def dma(self, eng, out, in_, ds, deps=(), slow=False):
        assert (eng == "pool") == ds.glob, "pool-queue DMAs must use a global DSem (and only they)"
        if ds.glob and ds not in self.gused:
            self.gused.append(ds)
        w = self._waits(eng, deps)
        ds.val += 16
        if slow:
            fn = lambda e: e.dma_start(out=out, in_=in_, allow_slow_non_contiguous=True)
        else:
            fn = lambda e: e.dma_start(out=out, in_=in_)
        self.q[eng].append((w, fn, ds.h, 16))
        self.n += 1
        return ("d", ds, ds.val)

    def run(self):
        fin = []
        for d in self.dsems + self.gused:
            if d.val > 0 and self.seen["sp"].get(("d", id(d)), 0) < d.val:
                fin.append((d.h, d.val))
        q = self.q

        def emit(eng_name):
            def f(e):
                for (w, fn, sem, inc) in q[eng_name]:
                    for (h, v) in w:
                        e.wait_ge(h, v)
                    ins = fn(e)
                    ins.then_inc(sem, inc)
                if eng_name == "sp":
                    for (h, v) in fin:
                        e.wait_ge(h, v)
            return f

        with self.nc.Block() as block:
            for en in ENGS:
                if q[en] or (en == "sp" and fin):
                    getattr(block, BLK[en])(emit(en))
        allsems = [self.csem[e] for e in ENGS] + [d.h for d in self.dsems]

        def clr(e):
            for h in allsems:
                e.sem_clear(h)

        with self.nc.Block() as block:
            block.sync(clr)
        self.es.close()


_CONSTS = {}


def t5_bucket_np(rel):
    nb = 16
    max_exact = 8
    ret = (rel > 0).astype(np.int32) * nb
    n = np.abs(rel)
    nf = np.maximum(n, 1).astype(np.float32)
    large = max_exact + (np.log(nf / np.float32(max_exact)) / np.float32(math.log(128 / max_exact))
                         * np.float32(nb - max_exact)).astype(np.int32)
    large = np.minimum(large, nb - 1)
    return ret + np.where(n < max_exact, n, large)


def host_consts():
    key = tuple(SEQS)
    if key in _CONSTS:
        return _CONSTS[key]
    bf = ml_dtypes.bfloat16
    c = {}
    c["ident"] = np.eye(128, dtype=np.float32).astype(bf)
    ps = np.zeros((128, 128), np.float32)
    for m in range(128):
        if m % 64 < 32:
            ps[m + 32, m] = -1.0
        else:
            ps[m - 32, m] = 1.0
    c["pswap"] = ps.astype(bf)
    ob = np.zeros((128, 128), np.float32)
    ob[:64, :64] = 1.0 / 64
    ob[64:, 64:] = 1.0 / 64
    c["onesblk"] = ob.astype(bf)
    c["ones128"] = np.full((128, 128), 1.0 / 128, np.float32).astype(bf)
    c["ones1"] = np.ones((128, 128), np.float32).astype(bf)
    Lmax = max(SEQS)
    t = np.arange(Lmax)
    row = (t // 64).astype(np.float32)
    col = (t % 64).astype(np.float32)
    n_freq = 16
    inv = (np.float32(10000.0) ** (-np.arange(n_freq, dtype=np.float32) / np.float32(n_freq))).astype(np.float32)
    ang = np.concatenate([row[:, None] * inv, col[:, None] * inv], axis=-1).astype(np.float32)
    cs = np.cos(ang).astype(np.float32).T
    sn = np.sin(ang).astype(np.float32).T
    c["ropec"] = np.ascontiguousarray(np.tile(cs, (4, 1)))
    c["ropes"] = np.ascontiguousarray(np.tile(sn, (4, 1)))
    e = np.arange(1280)
    bk = t5_bucket_np((e - 639).astype(np.int32))
    oh = np.zeros((32, 1280), np.float32)
    oh[bk, e] = 1.0
    c["t5oh"] = oh
    for L in sorted(set(SEQS)):
        t01 = np.linspace(0.0, 1.0, L, dtype=np.float32)[:, None]
        bands = 16
        w = (np.float32(2.0 * math.pi) * np.arange(L, dtype=np.float32)[:, None] / np.float32(L)).astype(np.float32)
        f = np.linspace(1e-4, bands - 1, bands, dtype=np.float32)[None, :]
        z = np.concatenate([t01, np.cos(f * w), -np.sin(f * w)], axis=-1).astype(np.float32)
        c[f"zT{L}"] = np.ascontiguousarray(z.T)
        max_decay = math.log(1e-2) / 0.3
        min_decay = math.log(1e-2) / 1.5
        deltas = np.linspace(min_decay, max_decay, HW, dtype=np.float32)
        win = np.exp(-t01 * np.abs(deltas)[None, :]).astype(np.float32)
        wb = win.copy()
        wb[0, :] = 0.0
        c[f"win{L}"] = np.ascontiguousarray(np.stack([win, wb], 0))
        N1, N2 = fft_dims(L)
        N = N1 * N2
        n1 = np.arange(N1 // 2)[:, None, None]
        n2 = np.arange(N2)[None, :, None]
        k1 = np.arange(N1)[None, None, :]
        ph = (k1 * (N2 * n1 + n2)) % N
        th = 2.0 * np.pi * ph.astype(np.float64) / N
        f1r = np.cos(th).astype(np.float32).astype(bf)
        f1i = (-np.sin(th)).astype(np.float32).astype(bf)
        if N1 // 2 >= 32:
            f1r = np.ascontiguousarray(np.concatenate([f1r[:, 0::2, :], f1r[:, 1::2, :]], axis=0))
            f1i = np.ascontiguousarray(np.concatenate([f1i[:, 0::2, :], f1i[:, 1::2, :]], axis=0))
        c[f"f1r{L}"] = f1r
        c[f"f1i{L}"] = f1i
        a = np.arange(N2)[:, None]
        b = np.arange(N2)[None, :]
        th2 = 2.0 * np.pi * ((a * b) % N2).astype(np.float64) / N2
        c[f"f2{L}"] = np.stack([np.cos(th2), np.sin(th2), -np.sin(th2), -np.cos(th2)], 1).astype(np.float32).astype(bf)
        k1 = np.arange(N1)[:, None, None]
        n2 = np.arange(N2)[None, :, None]
        n1 = np.arange(N1 // 2)[None, None, :]
        ph = (k1 * (N2 * n1 + n2)) % N
        th = 2.0 * np.pi * ph.astype(np.float64) / N
        c[f"i2c{L}"] = (np.cos(th) / N).astype(np.float32).astype(bf)
        c[f"i2s{L}"] = (-np.sin(th) / N).astype(np.float32).astype(bf)
    _CONSTS[key] = c
    return c


PARAM_NAMES = ["rel_bias", "norm_g", "w_in", "hy_conv_w", "hy_conv_b", "hy_f_w1", "hy_f_b1", "hy_f_w2",
               "hy_f_b2", "hy_f_wout", "hy_f_freq", "hy_bias", "q_norm_g", "k_norm_g", "lam_q1", "lam_k1",
               "lam_q2", "lam_k2", "diff_subln_g", "w_branch_hy", "w_branch_gqa", "w_branch_diff",
               "w_merge", "b_merge", "w_out", "final_g"]

PARAM_SHAPES = {
    "rel_bias": (32, 4), "norm_g": (DEPTH, D), "w_in": (DEPTH, D, NCOL), "hy_conv_w": (DEPTH, 3, 1536),
    "hy_conv_b": (DEPTH, 1536), "hy_f_w1": (DEPTH, 33, 64), "hy_f_b1": (DEPTH, 64),
    "hy_f_w2": (DEPTH, 2, 64, 64), "hy_f_b2": (DEPTH, 2, 64), "hy_f_wout": (DEPTH, 64, 1024),
    "hy_f_freq": (DEPTH, 64), "hy_bias": (DEPTH, HW), "q_norm_g": (DEPTH, 64), "k_norm_g": (DEPTH, 64),
    "lam_q1": (DEPTH, 64), "lam_k1": (DEPTH, 64), "lam_q2": (DEPTH, 64), "lam_k2": (DEPTH, 64),
    "diff_subln_g": (DEPTH, 128), "w_branch_hy": (DEPTH, HW, D), "w_branch_gqa": (DEPTH, HW, D),
    "w_branch_diff": (DEPTH, HW, D), "w_merge": (DEPTH, D, 3 * D), "b_merge": (DEPTH, 3 * D),
    "w_out": (DEPTH, D, D), "final_g": (D,),
}


def bcast_rows(ap_1d_or_row, nparts):
    a = ap_1d_or_row
    return bass.AP(tensor=a.tensor, offset=a.offset, ap=[[0, nparts]] + [list(x) for x in a.ap[-1:]])


class Prog:
    def __init__(self):
        self.nc = bass.Bass("TRN2", target_bir_lowering=False)
        nc = self.nc
        self.T = sum(SEQS)
        self.offs = [sum(SEQS[:i]) for i in range(len(SEQS))]
        self.Ls = sorted(set(SEQS), reverse=True)
        self.consts = host_consts()
        self.d = {}
        self.d["x"] = nc.dram_tensor("x", [self.T, D], F32, kind="ExternalInput").ap()
        for k in PARAM_NAMES:
            self.d[k] = nc.dram_tensor(k, list(PARAM_SHAPES[k]), F32, kind="ExternalInput").ap()
        for k, v in self.consts.items():
            dt = BF16 if v.dtype == ml_dtypes.bfloat16 else F32
            self.d["c_" + k] = nc.dram_tensor("c_" + k, list(v.shape), dt, kind="ExternalInput").ap()
        self.d["y"] = nc.dram_tensor("y", [self.T, D], F32, kind="ExternalOutput").ap()
        self.ges = ExitStack()
        self.g = {}
        self.gsem = [DSem(self.ges.enter_context(nc.semaphore(f"gpool{i}")), glob=True) for i in range(4)]

    def scr(self, name, shape, dt):
        kind = "ExternalOutput" if name in DEBUG_OUT else "Internal"
        a = self.nc.dram_tensor(name, list(shape), dt, kind=kind).ap()
        self.d[name] = a
        return a

    def gsb(self, name, shape, dt):
        t = self.ges.enter_context(self.nc.sbuf_tensor("g_" + name, list(shape), dt))
        self.g[name] = t
        return t


def phase_prep(P):
    nc, d, g = P.nc, P.d, P.g
    for nm in ["ident", "pswap", "onesblk", "ones128", "ones1"]:
        P.gsb(nm, [128, 128], BF16)
    P.gsb("wstrip", [128, 4, 1152], BF16)
    P.gsb("farb", [128, 4, 2], F32)
    P.gsb("lamt", [128, DEPTH, 2], F32)
    P.gsb("eps", [128, 1], F32)
    P.gsb("onesf", [128, 128], F32)
    bvr = P.scr("BVR", [4, 1280], F32)
    ph = Phase(nc, "prep")
    ld = ph.dsem("ld")
    toks = []
    for nm in ["ident", "pswap", "onesblk", "ones128", "ones1"]:
        toks.append(ph.dma("sp", g[nm][:], d["c_" + nm][:], ld))
    t_eps = ph.op("pool", lambda e: e.memset(g["eps"][:], EPS))
    ph.op("pool", lambda e: e.memset(g["onesf"][:], 1.0))
    rb = ph.sb("rb", [32, 4], F32)
    oh = ph.sb("oh", [32, 1280], F32)
    bv = ph.sb("bv", [4, 1280], F32)
    pb = ph.ps("pb", [4, 3, 512], F32)
    l2 = ph.dsem("l2")
    ph.dma("sp", rb[:], d["rel_bias"][:], l2)
    t_oh = ph.dma("sp", oh[:], d["c_t5oh"][:], l2)
    widths = [512, 512, 256]
    tcp = []
    for i, wd in enumerate(widths):
        tm = ph.op("pe", lambda e, i=i, wd=wd: e.matmul(pb[:, i, 0:wd], lhsT=rb[:], rhs=oh[:, i * 512:i * 512 + wd],
                                                         start=True, stop=True), deps=[t_oh])
        tcp.append(ph.op("dve", lambda e, i=i, wd=wd: e.tensor_copy(out=bv[:, i * 512:i * 512 + wd], in_=pb[:, i, 0:wd]),
                         deps=[tm]))
    st = ph.dsem("st")
    t_st = ph.dma("sp", bvr[:], bv[:], st, deps=tcp)
    l3 = ph.dsem("l3")
    wrev = ph.sb("wrev", [128, 4, 1152], F32)
    for h in range(4):
        src = bass.AP(tensor=bvr.tensor, offset=h * 1280, ap=[[1, 128], [1, 1152]])
        t_wr = ph.dma("sp", wrev[:, h, :], src, l3, deps=[t_st])
        for side, e0 in enumerate([0, 1278]):
            src = bass.AP(tensor=bvr.tensor, offset=h * 1280 + e0, ap=[[0, 128], [1, 1]])
            t_wr = ph.dma("sp", g["farb"][:, h, side:side + 1], src, l3, deps=[t_st])
    for h in range(4):
        a = wrev[:, h, :]
        rev = bass.AP(tensor=a.tensor, offset=a.offset + 1151, ap=[list(a.ap[0]), [-1, 1152]])
        ph.op("dve", lambda e, h=h, rev=rev: e.tensor_scalar(out=g["wstrip"][:, h, :], in0=rev, scalar1=8.0, scalar2=None, op0=ALU.mult),
              deps=[t_wr])
    lv = ph.sb("lv", [128, DEPTH, 4, 64], F32)
    l4 = ph.dsem("l4")
    t_lv = None
    for l in range(DEPTH):
        for j, nm in enumerate(["lam_q1", "lam_k1", "lam_q2", "lam_k2"]):
            t_lv = ph.dma("sp", lv[:, l, j, :], bcast_rows(d[nm][l:l + 1, :], 128), l4)
    junk = ph.sb("junk", [128, 64], F32)
    ss = ph.sb("ss", [128, DEPTH, 2], F32)
    ee = ph.sb("ee", [128, DEPTH, 2], F32)
    df = ph.sb("df", [128, DEPTH], F32)
    for l in range(DEPTH):
        lam_init = 0.8 - 0.6 * math.exp(-0.3 * l)
        ts = []
        prev = None
        for j in range(2):
            tmul = ph.op("dve", lambda e, l=l, j=j: e.tensor_tensor(out=junk[:], in0=lv[:, l, 2 * j, :], in1=lv[:, l, 2 * j + 1, :],
                                                                 op=ALU.mult), deps=[t_lv, prev])
            prev = ph.op("dve", lambda e, l=l, j=j: e.tensor_reduce(out=ss[:, l, j:j + 1], in_=junk[:], axis=AX.X, op=ALU.add),
                         deps=[tmul])
            ts.append(prev)
        te = ph.op("act", lambda e, l=l: e.activation(out=ee[:, l, :], in_=ss[:, l, :], func=AF.Exp), deps=ts)
        t1 = ph.op("dve", lambda e, l=l: e.tensor_tensor(out=df[:, l:l + 1], in0=ee[:, l, 1:2], in1=ee[:, l, 0:1],
                                                         op=ALU.subtract), deps=[te, prev])
        ph.op("dve", lambda e, l=l, li=lam_init: e.tensor_scalar(out=g["lamt"][:, l, 0:1], in0=df[:, l:l + 1],
                                                                  scalar1=-li, scalar2=None, op0=ALU.add), deps=[t1])
    ph.run()


def fft_stage1(ph, P, L, srcs, combos, Ad, tagp):
    nc, d = P.nc, P.d
    N1, N2 = fft_dims(L)
    H1 = N1 // 2
    G = FG
    paired = H1 >= 32
    NP_ = 2 * H1 if paired else H1
    GH = G // 2 if paired else G
    ncol = N2 // 2 if paired else N2
    tr = ph.sb(tagp + "tr", [NP_, ncol, N1], BF16)
    ti = ph.sb(tagp + "ti", [NP_, ncol, N1], BF16)
    ldt = ph.dsem(tagp + "ldt")
    ph.dma("sp", tr[:], d[f"c_f1r{L}"][:], ldt)
    t_tab = ph.dma("sp", ti[:], d[f"c_f1i{L}"][:], ldt)
    ns = len(srcs)
    NB = 2
    xs = [ph.sb(f"{tagp}x{b}", [NP_, ns, GH, HW], BF16) for b in range(NB)]
    xsem = [ph.dsem(f"{tagp}xs{b}") for b in range(NB)]
    no = len(combos)
    stg = [ph.sb(f"{tagp}st{b}", [N1, no, 2, G, HW], BF16) for b in range(NB)]
    ssem = [ph.dsem(f"{tagp}ss{b}") for b in range(NB)]
    NSL = 4
    pp = ph.ps(tagp + "pp", [N1, NSL, HW], F32)
    ngrp = N2 // G
    x_free = [None] * NB
    st_free = [None] * NB
    pp_free = [None] * NSL
    x_ld = [None] * NB

    def load(gi):
        b = gi % NB
        t = None
        for si, s in enumerate(srcs):
            v = s.rearrange("(a n) c -> a n c", n=N2)
            if paired:
                for hf in range(2):
                    src = v[0:H1, gi * G + hf:(gi + 1) * G:2, :]
                    t = ph.dma("sp", xs[b][hf * H1:(hf + 1) * H1, si, :, :], src, xsem[b], deps=[x_free[b]])
            else:
                t = ph.dma("sp", xs[b][:, si, :, :], v[0:H1, gi * G:(gi + 1) * G, :], xsem[b], deps=[x_free[b]])
        x_ld[b] = t

    load(0)
    cnt = 0
    for gi in range(ngrp):
        b = gi % NB
        if gi + 1 < ngrp:
            load(gi + 1)
        evs = []
        last_mm = None
        for ml in range(GH):
            for o, combo in enumerate(combos):
                for ri, tab in enumerate([tr, ti]):
                    halves = [0, 1] if paired else [0]
                    slots = []
                    for hf in halves:
                        slots.append(cnt % NSL)
                        cnt += 1
                    col = (gi * G) // (2 if paired else 1) + ml
                    for ci, si in enumerate(combo):
                        for hf in halves:
                            slot = slots[hf]
                            p0, p1 = hf * H1, (hf + 1) * H1
                            kw = dict(tile_position=(p0, 0)) if paired else {}
                            last_mm = ph.op("pe", lambda e, slot=slot, tab=tab, col=col, b=b, si=si, ml=ml, ci=ci, nci=len(combo), p0=p0, p1=p1, kw=kw:
                                            e.matmul(pp[:, slot, :], lhsT=tab[p0:p1, col, :], rhs=xs[b][p0:p1, si, ml, :],
                                                     start=(ci == 0), stop=(ci == nci - 1), **kw),
                                            deps=[t_tab, x_ld[b], pp_free[slot]])
                    for hf in halves:
                        slot = slots[hf]
                        n2l = (2 * ml + hf) if paired else ml
                        eng = "act" if (cnt + hf) % 2 else "dve"
                        if eng == "act":
                            ev = ph.op("act", lambda e, slot=slot, b=b, o=o, ri=ri, n2l=n2l:
                                       e.activation(out=stg[b][:, o, ri, n2l, :], in_=pp[:, slot, :], func=AF.Copy),
                                       deps=[last_mm, st_free[b]])
                        else:
                            ev = ph.op("dve", lambda e, slot=slot, b=b, o=o, ri=ri, n2l=n2l:
                                       e.tensor_copy(out=stg[b][:, o, ri, n2l, :], in_=pp[:, slot, :]),
                                       deps=[last_mm, st_free[b]])
                        pp_free[slot] = ev
                        evs.append(ev)
        x_free[b] = last_mm
        t = None
        for o in range(no):
            for ri in range(2):
                dst = Ad[o][ri][gi * G:(gi + 1) * G, :, :].rearrange("n k c -> k n c")
                t = ph.dma("sp", dst, stg[b][:, o, ri, :, :], ssem[b], deps=evs)
        st_free[b] = t


def fft_stage2_tables(ph, P, L, tagp):
    d = P.d
    N1, N2 = fft_dims(L)
    f2 = ph.sb(tagp + "f2", [N2, 4, N2], BF16)
    ldt = ph.dsem(tagp + "ldf2")
    t = ph.dma("sp", f2[:], d[f"c_f2{L}"][:], ldt)
    return f2, t


FG = 4


def load_ktiles(ph, srcs, k0, G, dst, sem, deps, N2):
    t = None
    for si, s in enumerate(srcs):
        t = ph.dma("sp", dst[:, si, :, :], s[:, k0:k0 + G, :], sem, deps=deps)
    return t


def phase_filter(P, l, L):
    nc, d, g = P.nc, P.d, P.g
    N1, N2 = fft_dims(L)
    key = f"{L}"
    if ("FIL" + key) not in d:
        P.scr("FIL" + key, [3, L, HW], BF16)
        for o in range(2):
            for ri in range(2):
                P.scr(f"AF{key}_{o}{ri}", [N2, N1, HW], BF16)
        for ri in range(2):
            P.scr(f"H{key}_{ri}", [N2, N1, HW], BF16)
    FIL = d["FIL" + key]
    AFd = [[d[f"AF{key}_{o}{ri}"] for ri in range(2)] for o in range(2)]
    Hd = [d[f"H{key}_{ri}"] for ri in range(2)]

    ph = Phase(nc, f"fm{l}_{L}")
    w1 = ph.sb("w1", [33, 64], F32)
    w2 = ph.sb("w2", [64, 2, 64], F32)
    wo = ph.sb("wo", [64, 1024], F32)
    fr = ph.sb("fr", [64, 1], F32)
    bb = ph.sb("bb", [64, 3], F32)
    fb = ph.sb("fb", [64, 3], F32)
    zt = ph.sb("zt", [33, L], F32)
    ld = ph.dsem("ld")
    ph.dma("sp", w1[:], d["hy_f_w1"][l], ld)
    for i in range(2):
        ph.dma("sp", w2[:, i, :], d["hy_f_w2"][l, i], ld)
    ph.dma("sp", wo[:], d["hy_f_wout"][l], ld)
    ph.dma("sp", fr[:], d["hy_f_freq"][l:l + 1, :].rearrange("o f -> f o"), ld, slow=True)
    ph.dma("sp", bb[:, 0:1], d["hy_f_b1"][l:l + 1, :].rearrange("o f -> f o"), ld, slow=True)
    for i in range(2):
        ph.dma("sp", bb[:, 1 + i:2 + i], d["hy_f_b2"][l, i:i + 1, :].rearrange("o f -> f o"), ld, slow=True)
    t_ld = ph.dma("sp", zt[:], d[f"c_zT{L}"][:], ld)
    t_fb = ph.op("dve", lambda e: e.tensor_scalar(out=fb[:], in0=bb[:], scalar1=fr[:, 0:1], scalar2=None, op0=ALU.mult),
                 deps=[t_ld])
    pa = ph.ps("pa", [64, 4, 512], F32)
    pf = ph.ps("pf", [128, 4, 512], F32)
    ya = [[ph.sb(f"ya{p}_{i}", [64, 512], F32) for i in range(2)] for p in range(2)]
    aa = [[ph.sb(f"aa{p}_{i}", [64, 512], F32) for i in range(2)] for p in range(2)]
    m1 = [ph.sb(f"m1_{p}", [64, 512], F32) for p in range(2)]
    m2 = [ph.sb(f"m2_{p}", [64, 512], F32) for p in range(2)]
    NBW = 2
    wn = [ph.sb(f"wn{b}", [128, 2, 512], F32) for b in range(NBW)]
    wsem = [ph.dsem(f"ws{b}") for b in range(NBW)]
    fo = [ph.sb(f"fo{b}", [128, 3, 512], BF16) for b in range(NBW)]
    fsem = [ph.dsem(f"fs{b}") for b in range(NBW)]
    wn_free = [None] * NBW
    fo_free = [None] * NBW
    pa_free = [None] * 4
    pf_free = [None] * 4
    a_read = [[None, None], [None, None]]
    ya_read = [[None, None], [None, None]]
    m_read = [None, None]
    ntile = L // 512
    cnt = {"sub": 0, "pfc": 0}

    def tile_stages(j, par):
        cols = slice(j * 512, (j + 1) * 512)
        stt = {"prev_a": None}

        def layer_stage(layer):
            def run():
                pi = par * 2 + layer % 2
                if layer == 0:
                    tm = ph.op("pe", lambda e: e.matmul(pa[:, pi, :], lhsT=w1[:], rhs=zt[:, cols], start=True, stop=True),
                               deps=[t_ld, pa_free[pi]])
                else:
                    src = aa[par][(layer - 1) % 2]
                    tm = ph.op("pe", lambda e: e.matmul(pa[:, pi, :], lhsT=w2[:, layer - 1, :], rhs=src[:], start=True, stop=True),
                               deps=[t_ld, pa_free[pi], stt["prev_a"]])
                    a_read[par][(layer - 1) % 2] = tm
                yb = ya[par][layer % 2]
                t1 = ph.op("dve", lambda e: e.tensor_scalar(out=yb[:], in0=pa[:, pi, :], scalar1=fr[:, 0:1],
                                                            scalar2=fb[:, layer:layer + 1], op0=ALU.mult, op1=ALU.add),
                           deps=[tm, t_fb, ya_read[par][layer % 2]])
                pa_free[pi] = t1
                ta = ph.op("dve", lambda e: e.tensor_scalar(out=m1[par][:], in0=yb[:], scalar1=-PI, scalar2=2 * PI, op0=ALU.is_lt, op1=ALU.mult),
                           deps=[t1, m_read[par]])
                tb = ph.op("dve", lambda e: e.tensor_scalar(out=m2[par][:], in0=yb[:], scalar1=PI, scalar2=-2 * PI, op0=ALU.is_gt, op1=ALU.mult),
                           deps=[t1])
                tc = ph.op("dve", lambda e: e.tensor_tensor(out=yb[:], in0=yb[:], in1=m1[par][:], op=ALU.add), deps=[ta, tb])
                t2 = ph.op("dve", lambda e: e.tensor_tensor(out=yb[:], in0=yb[:], in1=m2[par][:], op=ALU.add), deps=[tc])
                m_read[par] = t2
                ab = aa[par][layer % 2]
                stt["prev_a"] = ph.op("act", lambda e: e.activation(out=ab[:], in_=yb[:], func=AF.Sin),
                                      deps=[t2, a_read[par][layer % 2]])
                ya_read[par][layer % 2] = stt["prev_a"]
            return run

        def tm_stage(s):
            def run():
                a3 = aa[par][0]
                b = cnt["sub"] % NBW
                cnt["sub"] += 1
                t0 = j * 512 + s * 128
                t_w = ph.dma("sp", wn[b][:], d[f"c_win{L}"][0:2, t0:t0 + 128, :].rearrange("w t c -> t w c"), wsem[b], deps=[wn_free[b]])
                outs = []
                for half in range(2):
                    slot = cnt["pfc"] % 4
                    cnt["pfc"] += 1
                    tm = ph.op("pe", lambda e, slot=slot, half=half: e.matmul(pf[:, slot, :], lhsT=a3[:, s * 128:(s + 1) * 128],
                                                                              rhs=wo[:, half * 512:(half + 1) * 512], start=True, stop=True),
                               deps=[stt["prev_a"], pf_free[slot]])
                    a_read[par][0] = tm
                    te = ph.op("dve", lambda e, slot=slot, half=half: e.tensor_tensor(out=fo[b][:, half, :], in0=pf[:, slot, :],
                                                                                      in1=wn[b][:, half, :], op=ALU.mult),
                               deps=[tm, t_w, fo_free[b]])
                    pf_free[slot] = te
                    outs.append(te)
                wn_free[b] = outs[-1]
                tn = ph.op("act", lambda e: e.activation(out=fo[b][:, 2, :], in_=fo[b][:, 1, :], func=AF.Copy, scale=-1.0),
                           deps=[outs[-1], fo_free[b]])
                fo_free[b] = ph.dma("sp", FIL[:, t0:t0 + 128, :].rearrange("w t c -> t w c"), fo[b][:], fsem[b], deps=[outs[0], tn])
            return run

        return [layer_stage(0), layer_stage(1), layer_stage(2)] + [tm_stage(s) for s in range(4)]

    for j0 in range(0, ntile, 2):
        lists = [tile_stages(j, j - j0) for j in range(j0, min(j0 + 2, ntile))]
        for k in range(7):
            for lst in lists:
                lst[k]()
    ph.run()

    ph = Phase(nc, f"ff1{l}_{L}")
    fft_stage1(ph, P, L, [FIL[0], FIL[1], FIL[2]], [[0, 1], [0, 2]], AFd, "a")
    ph.run()

    ph = Phase(nc, f"ff2{l}_{L}")
    f2, t_f2 = fft_stage2_tables(ph, P, L, "b")
    hb = ph.sb("hb", [128, HW], F32)
    lb = ph.dsem("lb")
    t_hb = ph.dma("sp", hb[:], bcast_rows(d["hy_bias"][l:l + 1, :], 128), lb)
    G = FG
    NB = 2
    xin = [ph.sb(f"xin{b}", [N2, 4, G, HW], BF16) for b in range(NB)]
    xsem = [ph.dsem(f"xs{b}") for b in range(NB)]
    hst = [ph.sb(f"hst{b}", [N2, 2, G, HW], BF16) for b in range(NB)]
    hsem = [ph.dsem(f"hs{b}") for b in range(NB)]
    pp = ph.ps("pp", [N2, 4, HW], F32)
    x_free = [None] * NB
    h_free = [None] * NB
    pp_free = [None] * 4
    x_ld = [None] * NB
    srcs = [AFd[0][0], AFd[0][1], AFd[1][0], AFd[1][1]]
    ngrp = N1 // G
    x_ld[0] = load_ktiles(ph, srcs, 0, G, xin[0], xsem[0], [], N2)
    cnt = 0
    for gi in range(ngrp):
        b = gi % NB
        if gi + 1 < ngrp:
            nb_ = (gi + 1) % NB
            x_ld[nb_] = load_ktiles(ph, srcs, (gi + 1) * G, G, xin[nb_], xsem[nb_], [x_free[nb_]], N2)
        evs = []
        last = None
        for kl in range(G):
            for ri, (ia, ib, tb) in enumerate([(0, 1, 1), (3, 2, 2)]):
                slot = cnt % 4
                cnt += 1
                ph.op("pe", lambda e, slot=slot, b=b, ia=ia, kl=kl: e.matmul(pp[:, slot, :], lhsT=f2[:, 0, :], rhs=xin[b][:, ia, kl, :],
                                                                           start=True, stop=False),
                      deps=[t_f2, x_ld[b], pp_free[slot]])
                last = ph.op("pe", lambda e, slot=slot, b=b, ib=ib, kl=kl, tb=tb: e.matmul(pp[:, slot, :], lhsT=f2[:, tb, :], rhs=xin[b][:, ib, kl, :],
                                                                                         start=False, stop=True))
                if ri == 0:
                    ev = ph.op("dve", lambda e, slot=slot, b=b, kl=kl: e.tensor_tensor(out=hst[b][:, 0, kl, :], in0=pp[:, slot, :], in1=hb[0:N2, :], op=ALU.add),
                               deps=[last, t_hb, h_free[b]])
                else:
                    ev = ph.op("act", lambda e, slot=slot, b=b, kl=kl: e.activation(out=hst[b][:, 1, kl, :], in_=pp[:, slot, :], func=AF.Copy),
                               deps=[last, h_free[b]])
                pp_free[slot] = ev
                evs.append(ev)
        x_free[b] = last
        t = None
        for ri in range(2):
            t = ph.dma("sp", Hd[ri][:, gi * G:(gi + 1) * G, :], hst[b][:, ri, :, :], hsem[b], deps=evs)
        h_free[b] = t
    ph.run()


def ensure_act_scratch(P):
    d, T = P.d, P.T
    if "UH" in d:
        return
    P.scr("UH", [2048, T], BF16)
    P.scr("GQ", [512, T], BF16)
    P.scr("GK2", [2, 128, T], BF16)
    P.scr("GV", [2, T, 64], BF16)
    P.scr("GS", [512, T], BF16)
    P.scr("DQ", [512, T], BF16)
    P.scr("DK", [512, T], BF16)
    P.scr("DV", [T, 512], BF16)
    P.scr("DS", [512, T], BF16)
    P.scr("HT", [1024, T], BF16)
    P.scr("XR", [T, D], F32)
    P.scr("HV", [T, HW], BF16)
    P.scr("GG", [512, T], BF16)
    P.scr("YC", [512, T], BF16)
    P.scr("YG", [512, T], BF16)
    P.scr("YD", [512, T], BF16)


def seq_of(P, t0):
    for i, o in enumerate(P.offs):
        if o <= t0 < o + SEQS[i]:
            return i, t0 - o
    raise ValueError


def phase_A(P, l):
    nc, d, g = P.nc, P.d, P.g
    T = P.T
    ensure_act_scratch(P)
    xsrc = d["x"] if l == 0 else d["XR"]
    ph = Phase(nc, f"A{l}")
    Wb = ph.sb("Wb", [128, 8, NCOL], BF16)
    gam = ph.sb("gam", [128, D], F32)
    qg = ph.sb("qg", [128, 2], F32)
    wl = P.gsem[0]
    t_w = None
    for k in range(8):
        t_w = ph.dma("pool", Wb[:, k, :], d["w_in"][l, k * 128:(k + 1) * 128, :], wl)
    cl = ph.dsem("cl")
    ph.dma("sp", gam[:], bcast_rows(d["norm_g"][l:l + 1, :], 128), cl)
    for hh in range(2):
        ph.dma("sp", qg[hh * 64:(hh + 1) * 64, 0:1], d["q_norm_g"][l:l + 1, :].rearrange("o f -> f o"), cl, slow=True)
        t_c = ph.dma("sp", qg[hh * 64:(hh + 1) * 64, 1:2], d["k_norm_g"][l:l + 1, :].rearrange("o f -> f o"), cl, slow=True)

    NXB = 3
    xb = [ph.sb(f"xb{i}", [128, D], F32) for i in range(NXB)]
    xsem = [ph.dsem(f"xs{i}") for i in range(NXB)]
    xb_free = [None] * NXB
    hb = [ph.sb(f"hb{i}", [128, D], BF16) for i in range(4)]
    hb_free = [None] * 4
    hb_rdy = [None] * 4
    junk = ph.sb("junk", [128, D], F32)
    ssq = ph.sb("ssq", [128, 4], F32)
    hT = [ph.sb(f"hT{i}", [128, 8, 512], BF16) for i in range(2)]
    hT_free = [None] * 2
    hT_st = [None] * 2
    hT_rdy = [None] * 2
    htsem = [ph.dsem(f"hts{i}") for i in range(2)]
    cs = [ph.sb(f"cs{i}", [128, 2, 512], F32) for i in range(2)]
    cssem = [ph.dsem(f"css{i}") for i in range(2)]
    cs_free = [None] * 2
    cs_ld = [None] * 2
    tp = ph.ps("tp", [128, 2, 8, 128], BF16)
    tp_free = [None] * 2
    NS = 6
    pg = ph.ps("pg", [128, NS, 512], F32)
    slot_free = [None] * NS
    st = {"slot": 0, "sub": 0, "og": 0, "tpi": 0, "qs": 0}
    NOG = 5
    og = [ph.sb(f"og{i}", [128, 4, 512], BF16) for i in range(NOG)]
    ogsem = [ph.dsem(f"ogs{i}") for i in range(NOG)]
    og_free = [None] * NOG
    vs = [ph.sb(f"vs{i}", [128, 640], BF16) for i in range(2)]
    vssem = [ph.dsem(f"vss{i}") for i in range(2)]
    vs_free = [None] * 2
    sqb = [ph.sb(f"sqb{i}", [128, 512], BF16) for i in range(2)]
    rt = [ph.sb(f"rt{i}", [128, 512], F32) for i in range(2)]
    qn = [ph.sb(f"qn{i}", [128, 512], BF16) for i in range(2)]
    t1b = [ph.sb(f"t1b{i}", [128, 512], F32) for i in range(2)]
    t2b = [ph.sb(f"t2b{i}", [128, 512], F32) for i in range(2)]
    lastuse = [dict() for _ in range(2)]
    ssm = ph.sb("ssm", [128, 4], F32)
    rsd = ph.sb("rsd", [128, 4], F32)
    ss_free = [None] * 4

    def getslot():
        s = st["slot"] % NS
        st["slot"] += 1
        return s

    ntile = T // 512

    def norm_part1(tt):
        b = tt % 2
        t0 = tt * 512
        si, pos = seq_of(P, t0)
        cs_ld[b] = None
        ph.dma("sp", cs[b][:, 0, :], d["c_ropec"][:, pos:pos + 512], cssem[b], deps=[cs_free[b]])
        cs_ld[b] = ph.dma("sp", cs[b][:, 1, :], d["c_ropes"][:, pos:pos + 512], cssem[b], deps=[cs_free[b]])
        for s in range(4):
            i = st["sub"] % NXB
            st["sub"] += 1
            j = s
            q4 = s
            r0 = t0 + s * 128
            t_x = ph.dma("sp", xb[i][:], xsrc[r0:r0 + 128, :], xsem[i], deps=[xb_free[i]])
            t_ss = ph.op("act", lambda e, i=i, q4=q4: e.activation(out=junk[:], in_=xb[i][:], func=AF.Square,
                                                                  accum_out=ssq[:, q4:q4 + 1]), deps=[t_x, ss_free[q4]])
            t_sd = ph.op("act", lambda e, q4=q4: e.activation(out=ssm[:, q4:q4 + 1], in_=ssq[:, q4:q4 + 1], func=AF.Sqrt,
                                                             bias=g["eps"][:], scale=1.0 / D), deps=[t_ss])
            t_r = ph.op("dve", lambda e, q4=q4: e.reciprocal(out=rsd[:, q4:q4 + 1], in_=ssm[:, q4:q4 + 1]), deps=[t_sd])
            t_h = ph.op("dve", lambda e, i=i, j=j, q4=q4: e.scalar_tensor_tensor(
                out=hb[j][:], in0=xb[i][:], scalar=rsd[:, q4:q4 + 1], in1=gam[:], op0=ALU.mult, op1=ALU.mult),
                deps=[t_r, t_c, hb_free[j]])
            ss_free[q4] = t_h
            xb_free[i] = t_h
            hb_rdy[j] = t_h

    def norm_part2(tt):
        b = tt % 2
        t0 = tt * 512
        evs = []
        for s in range(4):
            j = s
            tpi = st["tpi"] % 2
            st["tpi"] += 1
            last = None
            for k in range(8):
                last = ph.op("pe", lambda e, tpi=tpi, k=k, j=j: e.transpose(out=tp[:, tpi, k, :], in_=hb[j][:, k * 128:(k + 1) * 128],
                                                                           identity=g["ident"][:]),
                             deps=[hb_rdy[j], tp_free[tpi]])
            hb_free[j] = last
            ev = ph.op("act", lambda e, tpi=tpi, b=b, s=s: e.activation(out=hT[b][:, :, s * 128:(s + 1) * 128], in_=tp[:, tpi, :, :], func=AF.Copy),
                       deps=[last, hT_free[b], hT_st[b]])
            tp_free[tpi] = ev
            evs.append(ev)
        hT_rdy[b] = evs[-1]
        hT_st[b] = ph.dma("sp", d["HT"][:, t0:t0 + 512].rearrange("(k p) t -> p k t", p=128), hT[b][:], htsem[b], deps=evs)

    def grp4(c0, kind, name):
        return [(c0 + j, kind, name, j) for j in range(4)]
    plan = []
    plan += [(16, "qk", "GQ", 0)] + grp4(0, "copy", "UH0") + [(20, "qk", "GK", 0)] + grp4(4, "copy", "UH1")
    plan += [(17, "qk", "GQ", 1)] + grp4(8, "copy", "UH2") + grp4(26, "copy", "DQ")
    plan += [(18, "qk", "GQ", 2)] + grp4(30, "copy", "DK") + grp4(12, "silu", "UH3")
    plan += [(19, "qk", "GQ", 3)] + grp4(22, "silu", "GS") + grp4(38, "silu", "DS")
    dests = {"UH0": (d["UH"], 0), "UH1": (d["UH"], 512), "UH2": (d["UH"], 1024), "UH3": (d["UH"], 1536),
             "DQ": (d["DQ"], 0), "DK": (d["DK"], 0), "GS": (d["GS"], 0), "DS": (d["DS"], 0), "GQ": (d["GQ"], 0)}

    norm_part1(0)
    norm_part2(0)
    for tt in range(ntile):
        b = tt % 2
        t0 = tt * 512
        cur = {}
        gevs = {}
        last_pe = None
        deferred = []
        for ci, (c, kind, grp, j) in enumerate(plan):
            if ci == 5 and tt + 1 < ntile:
                norm_part1(tt + 1)
            if ci == 24 and tt + 1 < ntile:
                norm_part2(tt + 1)
            while deferred and deferred[0][0] <= ci:
                deferred.pop(0)[1]()
            if grp not in cur:
                if grp == "GQ":
                    cur[grp] = 3
                elif grp == "GK":
                    cur[grp] = 4
                else:
                    cur[grp] = st["og"] % 3
                    st["og"] += 1
                gevs[grp] = []
            cur_og = cur[grp]
            grp_evs = gevs[grp]
            su = getslot()
            for k in range(8):
                last_pe = ph.op("pe", lambda e, su=su, k=k, c=c, b=b: e.matmul(pg[:, su, :], lhsT=Wb[:, k, c * 128:(c + 1) * 128], rhs=hT[b][:, k, :],
                                                                             start=(k == 0), stop=(k == 7)),
                                deps=[t_w, hT_rdy[b], slot_free[su]])
            mm = last_pe
            o = cur_og
            if kind == "copy":
                ev = ph.op("act", lambda e, su=su, o=o, j=j: e.activation(out=og[o][:, j, :], in_=pg[:, su, :], func=AF.Copy),
                           deps=[mm, og_free[o]])
                slot_free[su] = ev
            elif kind == "silu":
                ev = ph.op("act", lambda e, su=su, o=o, j=j: e.activation(out=og[o][:, j, :], in_=pg[:, su, :], func=AF.Silu),
                           deps=[mm, og_free[o]])
                slot_free[su] = ev
            else:
                q = st["qs"] % 2
                st["qs"] += 1
                lu = lastuse[q]
                gcol = 0 if grp == "GQ" else 1
                t_sq = ph.op("act", lambda e, su=su, q=q: e.activation(out=sqb[q][:], in_=pg[:, su, :], func=AF.Square),
                             deps=[mm, lu.get("sqb")])
                box = {}

                def stepA(su=su, q=q, lu=lu, gcol=gcol, t_sq=t_sq, box=box):
                    sm = getslot()
                    t_ms = ph.op("pe", lambda e: e.matmul(pg[:, sm, :], lhsT=g["onesblk"][:], rhs=sqb[q][:], start=True, stop=True),
                                 deps=[t_sq, slot_free[sm]])
                    lu["sqb"] = t_ms
                    t_sd = ph.op("act", lambda e: e.activation(out=rt[q][:], in_=pg[:, sm, :], func=AF.Sqrt, bias=g["eps"][:], scale=1.0),
                                 deps=[t_ms, lu.get("rt")])
                    slot_free[sm] = t_sd
                    t_rs = ph.op("dve", lambda e: e.reciprocal(out=rt[q][:], in_=rt[q][:]), deps=[t_sd])
                    t_qn = ph.op("dve", lambda e: e.scalar_tensor_tensor(
                        out=qn[q][:], in0=pg[:, su, :], scalar=qg[:, gcol:gcol + 1], in1=rt[q][:], op0=ALU.mult, op1=ALU.mult),
                        deps=[t_rs, t_c, lu.get("qn")])
                    slot_free[su] = t_qn
                    lu["rt"] = t_qn
                    box["t_qn"] = t_qn

                def stepB(q=q, lu=lu, b=b, o=o, j=j, grp=grp, box=box, grp_evs=grp_evs, t0=t0):
                    t_qn = box["t_qn"]
                    sw = getslot()
                    t_sw = ph.op("pe", lambda e: e.matmul(pg[:, sw, :], lhsT=g["pswap"][:], rhs=qn[q][:], start=True, stop=True),
                                 deps=[t_qn, slot_free[sw]])
                    t_1 = ph.op("dve", lambda e: e.tensor_tensor(out=t1b[q][:], in0=qn[q][:], in1=cs[b][:, 0, :], op=ALU.mult),
                                deps=[t_qn, cs_ld[b], lu.get("t1b")])
                    t_2 = ph.op("dve", lambda e: e.tensor_tensor(out=t2b[q][:], in0=pg[:, sw, :], in1=cs[b][:, 1, :], op=ALU.mult),
                                deps=[t_sw, cs_ld[b], lu.get("t2b")])
                    slot_free[sw] = t_2
                    lu["qn"] = t_2
                    cs_free[b] = t_2
                    ev = ph.op("pool", lambda e: e.tensor_tensor(out=og[o][:, j, :], in0=t1b[q][:], in1=t2b[q][:], op=ALU.add),
                               deps=[t_1, t_2, og_free[o]])
                    lu["t1b"] = ev
                    lu["t2b"] = ev
                    grp_evs.append(ev)
                    if grp == "GK":
                        tk = None
                        for kv in range(2):
                            for dup in range(2):
                                tk = ph.dma("sp", d["GK2"][kv, dup * 64:(dup + 1) * 64, t0:t0 + 512], og[o][kv * 64:(kv + 1) * 64, 0, :], ogsem[o], deps=grp_evs)
                        og_free[o] = tk
                    elif j == 3:
                        dst, r0 = dests[grp]
                        og_free[o] = ph.dma("sp", dst[r0:r0 + 512, t0:t0 + 512].rearrange("(j p) t -> p j t", p=128), og[o][:], ogsem[o], deps=grp_evs)

                deferred.append((ci + 2, stepA))
                deferred.append((ci + 6, stepB))
                deferred.sort(key=lambda x: x[0])
                continue
            grp_evs.append(ev)
            if grp == "GK":
                tk = None
                for kv in range(2):
                    for dup in range(2):
                        tk = ph.dma("sp", d["GK2"][kv, dup * 64:(dup + 1) * 64, t0:t0 + 512], og[o][kv * 64:(kv + 1) * 64, 0, :], ogsem[o], deps=grp_evs)
                og_free[o] = tk
            elif j == 3:
                dst, r0 = dests[grp]
                og_free[o] = ph.dma("sp", dst[r0:r0 + 512, t0:t0 + 512].rearrange("(j p) t -> p j t", p=128), og[o][:], ogsem[o], deps=grp_evs)
        while deferred and deferred[0][0] <= len(plan) + 1:
            deferred.pop(0)[1]()
        for s in range(4):
            vb = (tt * 4 + s) % 2
            s1 = getslot()
            for k in range(8):
                last_pe = ph.op("pe", lambda e, s1=s1, k=k, s=s, b=b: e.matmul(pg[:, s1, 0:128], lhsT=hT[b][:, k, s * 128:(s + 1) * 128], rhs=Wb[:, k, 2688:2816],
                                                                             start=(k == 0), stop=(k == 7)),
                                deps=[t_w, hT_rdy[b], slot_free[s1]])
            e1 = ph.op("act", lambda e, s1=s1, vb=vb: e.activation(out=vs[vb][:, 0:128], in_=pg[:, s1, 0:128], func=AF.Copy),
                       deps=[last_pe, vs_free[vb]])
            slot_free[s1] = e1
            s2 = getslot()
            for k in range(8):
                last_pe = ph.op("pe", lambda e, s2=s2, k=k, s=s, b=b: e.matmul(pg[:, s2, :], lhsT=hT[b][:, k, s * 128:(s + 1) * 128], rhs=Wb[:, k, 4352:4864],
                                                                             start=(k == 0), stop=(k == 7)),
                                deps=[slot_free[s2]])
            e2 = ph.op("dve", lambda e, s2=s2, vb=vb: e.tensor_copy(out=vs[vb][:, 128:640], in_=pg[:, s2, :]),
                       deps=[last_pe, vs_free[vb]])
            slot_free[s2] = e2
            r0 = t0 + s * 128
            for kv in range(2):
                ph.dma("sp", d["GV"][kv, r0:r0 + 128, :], vs[vb][:, kv * 64:(kv + 1) * 64], vssem[vb], deps=[e1])
            vs_free[vb] = ph.dma("sp", d["DV"][r0:r0 + 128, :], vs[vb][:, 128:640], vssem[vb], deps=[e2])
        hT_free[b] = last_pe
        while deferred:
            deferred.pop(0)[1]()
    ph.run()


def phase_attn(P, l, mode):
    nc, d, g = P.nc, P.d, P.g
    isD = mode == "D"
    ph = Phase(nc, f"{mode}{l}")
    Lmax = max(SEQS)
    VW = 128 if isD else 64
    NKB = 2
    Kb = [ph.sb(f"K{i}", [128, Lmax], BF16) for i in range(NKB)]
    Vb = [ph.sb(f"V{i}", [128, Lmax // 128, VW], BF16) for i in range(NKB)]
    ksem = [ph.dsem(f"ks{i}") for i in range(NKB)]
    k_free = [None] * NKB
    k_ld = [None] * NKB
    NQB = 3
    Qb = [ph.sb(f"Q{i}", [128, 512], BF16) for i in range(NQB)]
    Gb = [ph.sb(f"Gt{i}", [128, 512], BF16) for i in range(NQB)]
    qsem = [ph.dsem(f"qs{i}") for i in range(NQB)]
    q_free = [None] * NQB
    q_ld = [None] * NQB
    NP = 4
    p_s = [ph.sb(f"p{i}", [128, 2, 512], BF16) for i in range(NP)]
    p_free = [None] * NP
    ps_s = ph.ps("s", [128, 2, 2, 512], F32)
    s_free = [None] * 2
    if isD:
        acc = ph.ps("acc", [128, 4, 512], F32)
        NACC = 1
        gsub = ph.sb("gsub", [128, 1], F32)
        gs0 = ph.sb("gs0", [128, 1], F32)
        cl = ph.dsem("cl")
        t_g0 = ph.dma("sp", gs0[:], d["diff_subln_g"][l:l + 1, :].rearrange("o f -> f o"), cl, slow=True)
        lam_init = 0.8 - 0.6 * math.exp(-0.3 * l)
        t_gs = ph.op("dve", lambda e: e.tensor_scalar(out=gsub[:], in0=gs0[:], scalar1=1.0 - lam_init, scalar2=None, op0=ALU.mult),
                     deps=[t_g0])
        sqd = [ph.sb(f"sqd{i}", [128, 512], BF16) for i in range(2)]
        dcp = [ph.sb(f"dcp{i}", [64, 512], F32) for i in range(2)]
        obA = [ph.sb(f"obA{i}", [128, 512], F32) for i in range(2)]
        obB = [ph.sb(f"obB{i}", [128, 512], F32) for i in range(2)]
        rbD = [ph.sb(f"rbD{i}", [128, 512], F32) for i in range(2)]
    else:
        acc = ph.ps("acc", [128, 2, 2, 512], F32)
        NACC = 2
    acc_free = [None] * NACC
    den_free = [None]
    p_rd = [[] for _ in range(NP)]
    if isD:
        pq = [ph.sb(f"pq{i}", [128, 2, 512], BF16) for i in range(2)]
        pq_free = [None, None]
    rb1 = ph.sb("rb1", [128, 512], F32)
    ob1 = ph.sb("ob1", [128, 512], F32)
    NOS = 2
    ost = [ph.sb(f"ost{i}", [128, 512], BF16) for i in range(NOS)]
    osem = [ph.dsem(f"os{i}") for i in range(NOS)]
    o_free = [None] * NOS
    ones = g["ones1"]

    groups = []
    kvsets = []
    for si, L in enumerate(SEQS):
        nsets = 4 if isD else 2
        for a in range(nsets):
            kvsets.append((si, a))
            subs = [a] if isD else [2 * a, 2 * a + 1]
            for hp in subs:
                for qj in range(L // 512):
                    groups.append(dict(si=si, a=a, hp=hp, qj=qj, L=L, off=P.offs[si], ks=len(kvsets) - 1))
    Ksrc = d["DK"] if isD else None
    Qsrc = d["DQ"] if isD else d["GQ"]
    Ssrc = d["DS"] if isD else d["GS"]
    Ydst = d["YD"] if isD else d["YG"]

    def load_kv(ksi):
        si, a = kvsets[ksi]
        L, off = SEQS[si], P.offs[si]
        b = ksi % NKB
        if isD:
            ph.dma("sp", Kb[b][:, 0:L], d["DK"][a * 128:(a + 1) * 128, off:off + L], ksem[b], deps=[k_free[b]])
            src = d["DV"][off:off + L, a * 128:(a + 1) * 128].rearrange("(c p) e -> p c e", p=128)
        else:
            ph.dma("sp", Kb[b][:, 0:L], d["GK2"][a, :, off:off + L], ksem[b], deps=[k_free[b]])
            src = d["GV"][a, off:off + L, :].rearrange("(c p) e -> p c e", p=128)
        k_ld[b] = ph.dma("sp", Vb[b][:, 0:L // 128, :], src, ksem[b], deps=[k_free[b]])

    def load_q(gi):
        grp = groups[gi]
        b = gi % NQB
        r0 = grp["hp"] * 128
        c0 = grp["off"] + grp["qj"] * 512
        ph.dma("sp", Qb[b][:], Qsrc[r0:r0 + 128, c0:c0 + 512], qsem[b], deps=[q_free[b]])
        q_ld[b] = ph.dma("sp", Gb[b][:], Ssrc[r0:r0 + 128, c0:c0 + 512], qsem[b], deps=[q_free[b]])

    tiles = []
    for gi, grp in enumerate(groups):
        nkc = grp["L"] // 128
        for kc in range(nkc):
            tiles.append((gi, kc, nkc))
    nt = len(tiles)
    qk_tok = [None] * nt
    exp_tok = [None] * nt
    state = {"last_av": None}

    def emit_qk(i):
        gi, kc, nkc = tiles[i]
        grp = groups[gi]
        if kc == 0:
            flush_group(gi - NQB)
        kb = grp["ks"] % NKB
        qb = gi % NQB
        sb_ = i % 2
        near = False
        if isD:
            o = 128 * kc - 512 * grp["qj"]
            near = -256 < o < 640
        ph.op("pe", lambda e: e.matmul(ps_s[:, sb_, 0, :], lhsT=Kb[kb][0:64, kc * 128:(kc + 1) * 128], rhs=Qb[qb][0:64, :],
                                       start=True, stop=not near, tile_position=(0, 0)),
              deps=[k_ld[kb], q_ld[qb], s_free[sb_]])
        qk_tok[i] = ph.op("pe", lambda e: e.matmul(ps_s[:, sb_, 1, :], lhsT=Kb[kb][64:128, kc * 128:(kc + 1) * 128], rhs=Qb[qb][64:128, :],
                                                   start=True, stop=not near, tile_position=(64, 0)))
        if near:
            brhs = g["wstrip"][:, grp["a"], 512 - o:1024 - o]
            ph.op("pe", lambda e: e.matmul(ps_s[:, sb_, 0, :], lhsT=g["ident"][:], rhs=brhs, start=False, stop=True))
            qk_tok[i] = ph.op("pe", lambda e: e.matmul(ps_s[:, sb_, 1, :], lhsT=g["ident"][:], rhs=brhs, start=False, stop=True))

    def emit_exp(i):
        gi, kc, nkc = tiles[i]
        grp = groups[gi]
        sb_ = i % 2
        pb = i % NP
        if isD:
            h = grp["a"]
            o = 128 * kc - 512 * grp["qj"]
            if o <= -256 or o >= 640:
                side = 0 if o <= -256 else 1
                exp_tok[i] = ph.op("act", lambda e: e.activation(out=p_s[pb][:], in_=ps_s[:, sb_, :, :], func=AF.Exp,
                                                                bias=g["farb"][:, h, side:side + 1], scale=0.125),
                                   deps=[qk_tok[i], p_free[pb]] + p_rd[pb])
                s_free[sb_] = exp_tok[i]
            else:
                exp_tok[i] = ph.op("act", lambda e: e.activation(out=p_s[pb][:], in_=ps_s[:, sb_, :, :], func=AF.Exp, scale=0.125),
                                   deps=[qk_tok[i], p_free[pb]] + p_rd[pb])
                s_free[sb_] = exp_tok[i]
        else:
            exp_tok[i] = ph.op("act", lambda e: e.activation(out=p_s[pb][:], in_=ps_s[:, sb_, :, :], func=AF.Exp, scale=0.125),
                               deps=[qk_tok[i], p_free[pb]] + p_rd[pb])
            s_free[sb_] = exp_tok[i]

    def emit_den_pair(i):
        gi, kc, nkc = tiles[i]
        assert kc % 2 == 1
        r = (i // 2) % 2
        pa_, pb2 = (i - 1) % NP, i % NP
        t_add = ph.op("dve", lambda e: e.tensor_tensor(out=pq[r][:], in0=p_s[pa_][:], in1=p_s[pb2][:], op=ALU.add),
                      deps=[exp_tok[i - 1], exp_tok[i], pq_free[r]])
        p_rd[pa_] = [t_add]
        p_rd[pb2] = [t_add]
        first, last = kc == 1, kc == nkc - 1
        ph.op("pe", lambda e: e.matmul(acc[0:32, 2, :], lhsT=ones[:, 0:32], rhs=pq[r][:, 0, :], start=first, stop=last,
                                       tile_position=(0, 0)), deps=[t_add, den_free[0] if first else None])
        t = ph.op("pe", lambda e: e.matmul(acc[32:64, 2, :], lhsT=ones[:, 0:32], rhs=pq[r][:, 1, :], start=first, stop=last,
                                           tile_position=(0, 32)))
        pq_free[r] = t
        return t

    def emit_av(i):
        gi, kc, nkc = tiles[i]
        grp = groups[gi]
        kb = grp["ks"] % NKB
        pb = i % NP
        a_ = gi % NACC
        first, last = kc == 0, kc == nkc - 1
        deps = [exp_tok[i], acc_free[a_] if first else None]
        if isD:
            if kc >= 2 and kc % 2 == 0:
                emit_den_pair(i - 1)
            ph.op("pe", lambda e: e.matmul(acc[:, 0, :], lhsT=Vb[kb][:, kc, :], rhs=p_s[pb][:, 0, :], start=first, stop=last), deps=deps)
            t = ph.op("pe", lambda e: e.matmul(acc[:, 1, :], lhsT=Vb[kb][:, kc, :], rhs=p_s[pb][:, 1, :], start=first, stop=last))
            p_free[pb] = t
            if last:
                t = emit_den_pair(i)
        else:
            ph.op("pe", lambda e: e.matmul(acc[0:64, a_, 0, :], lhsT=Vb[kb][:, kc, :], rhs=p_s[pb][:, 0, :], start=first, stop=last,
                                           tile_position=(0, 0)), deps=deps)
            ph.op("pe", lambda e: e.matmul(acc[64:128, a_, 0, :], lhsT=Vb[kb][:, kc, :], rhs=p_s[pb][:, 1, :], start=first, stop=last,
                                           tile_position=(0, 64)))
            ph.op("pe", lambda e: e.matmul(acc[0:64, a_, 1, :], lhsT=ones[:, 0:64], rhs=p_s[pb][:, 0, :], start=first, stop=last,
                                           tile_position=(0, 0)))
            t = ph.op("pe", lambda e: e.matmul(acc[64:128, a_, 1, :], lhsT=ones[:, 0:64], rhs=p_s[pb][:, 1, :], start=first, stop=last,
                                               tile_position=(0, 64)))
        if not isD:
            p_free[pb] = t
        state["last_av"] = t
        return t

    pending = {}
    es_last = [None, None]
    sp_state = {"free": None}

    def flush_group(gq):
        for (_due, fn) in pending.pop(gq, []):
            fn()

    def run_due(i):
        for gq in sorted(pending.keys()):
            lst = pending[gq]
            while lst and lst[0][0] <= i:
                lst.pop(0)[1]()
            if not lst:
                pending.pop(gq)

    def finish_group(gi, t11):
        grp = groups[gi]
        qb = gi % NQB
        osl = gi % NOS
        r0 = grp["hp"] * 128
        c0 = grp["off"] + grp["qj"] * 512
        q_free[qb] = t11
        o_free[osl] = ph.dma("sp", Ydst[r0:r0 + 128, c0:c0 + 512], ost[osl][:], osem[osl], deps=[t11])
        if gi + NQB < len(groups):
            load_q(gi + NQB)

    def epilogue(gi, t_last, i_tile):
        grp = groups[gi]
        a_ = gi % NACC
        qb = gi % NQB
        osl = gi % NOS
        if isD:
            es_ = gi % 2
            flush_group(gi - 2)
            oA, oB, dc, sq_, rD = obA[es_], obB[es_], dcp[es_], sqd[es_], rbD[es_]
            c1 = ph.op("dve", lambda e: e.tensor_copy(out=oA[:], in_=acc[:, 0, :]), deps=[t_last, es_last[es_]])
            c2 = ph.op("dve", lambda e: e.tensor_copy(out=oB[:], in_=acc[:, 1, :]), deps=[t_last])
            c3 = ph.op("dve", lambda e: e.tensor_copy(out=dc[:], in_=acc[0:64, 2, :]), deps=[t_last])
            acc_free[a_] = c3
            den_free[0] = c3
            tr = ph.op("dve", lambda e: e.reciprocal(out=dc[:], in_=dc[:]), deps=[c3])
            stt = {}

            def step1():
                b1 = ph.op("pe", lambda e: e.matmul(acc[:, 3, :], lhsT=g["onesf"][0:1, :], rhs=dc[0:1, :], start=True, stop=True),
                           deps=[tr, sp_state["free"]])
                stt["o1"] = ph.op("dve", lambda e: e.tensor_tensor(out=oA[:], in0=oA[:], in1=acc[:, 3, :], op=ALU.mult), deps=[b1, c1])
                sp_state["free"] = stt["o1"]

            def step2():
                b2 = ph.op("pe", lambda e: e.matmul(acc[:, 3, :], lhsT=g["onesf"][32:33, :], rhs=dc[32:33, :], start=True, stop=True),
                           deps=[tr, sp_state["free"]])
                o2 = ph.op("dve", lambda e: e.tensor_tensor(out=oB[:], in0=oB[:], in1=acc[:, 3, :], op=ALU.mult), deps=[b2, c2])
                sp_state["free"] = o2
                t5 = ph.op("dve", lambda e: e.scalar_tensor_tensor(out=oA[:], in0=oB[:], scalar=g["lamt"][:, l, 0:1], in1=oA[:],
                                                                  op0=ALU.mult, op1=ALU.add), deps=[o2, stt["o1"]])
                stt["t5"] = t5
                stt["t6"] = ph.op("act", lambda e: e.activation(out=sq_[:], in_=oA[:], func=AF.Square), deps=[t5])

            def step3():
                t7 = ph.op("pe", lambda e: e.matmul(acc[:, 3, :], lhsT=g["ones128"][:], rhs=sq_[:], start=True, stop=True),
                           deps=[stt["t6"], sp_state["free"]])
                t8 = ph.op("act", lambda e: e.activation(out=rD[:], in_=acc[:, 3, :], func=AF.Sqrt, bias=g["eps"][:], scale=1.0), deps=[t7])
                sp_state["free"] = t8
                t9 = ph.op("dve", lambda e: e.reciprocal(out=rD[:], in_=rD[:]), deps=[t8])
                t10 = ph.op("dve", lambda e: e.scalar_tensor_tensor(out=oA[:], in0=oA[:], scalar=gsub[:, 0:1], in1=rD[:],
                                                                   op0=ALU.mult, op1=ALU.mult), deps=[t9, t_gs, stt["t5"]])
                t11 = ph.op("dve", lambda e: e.tensor_tensor(out=ost[osl][:], in0=oA[:], in1=Gb[qb][:], op=ALU.mult),
                            deps=[t10, q_ld[qb], o_free[osl]])
                es_last[es_] = t11
                finish_group(gi, t11)

            pending[gi] = [(i_tile + 5, step1), (i_tile + 7, step2), (i_tile + 10, step3)]
        else:
            t1 = ph.op("dve", lambda e: e.reciprocal(out=rb1[:], in_=acc[:, a_, 1, :]), deps=[t_last])
            t3 = ph.op("dve", lambda e: e.tensor_tensor(out=ob1[:], in0=acc[:, a_, 0, :], in1=rb1[:], op=ALU.mult), deps=[t1])
            acc_free[a_] = t3
            t11 = ph.op("dve", lambda e: e.tensor_tensor(out=ost[osl][:], in0=ob1[:], in1=Gb[qb][:], op=ALU.mult),
                        deps=[t3, q_ld[qb], o_free[osl]])
            finish_group(gi, t11)

    load_kv(0)
    for gq in range(min(NQB, len(groups))):
        load_q(gq)
    emit_qk(0)
    for i in range(nt):
        gi, kc, nkc = tiles[i]
        grp = groups[gi]
        if kc == 0:
            if (gi == 0 or groups[gi - 1]["ks"] != grp["ks"]) and grp["ks"] + 1 < len(kvsets):
                load_kv(grp["ks"] + 1)
        emit_exp(i)
        if i + 1 < nt:
            emit_qk(i + 1)
        t = emit_av(i)
        run_due(i)
        if kc == nkc - 1:
            if gi + 1 >= len(groups) or groups[gi + 1]["ks"] != grp["ks"]:
                k_free[grp["ks"] % NKB] = t
            epilogue(gi, t, i)
    for gq in sorted(pending.keys()):
        flush_group(gq)
    ph.run()


def phase_H1(P, l):
    nc, d, g = P.nc, P.d, P.g
    ph = Phase(nc, f"H1{l}")
    cw = ph.sb("cw", [128, 12, 4], F32)
    cl = ph.dsem("cl")
    for j in range(3):
        ph.dma("sp", cw[:, :, j:j + 1], d["hy_conv_w"][l, j:j + 1, :].rearrange("o (c p) -> p c o", p=128), cl, slow=True)
    t_cw = ph.dma("sp", cw[:, :, 3:4], d["hy_conv_b"][l:l + 1, :].rearrange("o (c p) -> p c o", p=128), cl, slow=True)
    BWmax = min(2048, max(SEQS))
    NB = 2
    U = [[ph.sb(f"U{b}_{j}", [128, BWmax + 2], BF16) for j in range(3)] for b in range(NB)]
    SG = [ph.sb(f"SG{b}", [128, BWmax], BF16) for b in range(NB)]
    usem = [ph.dsem(f"us{b}") for b in range(NB)]
    u_free = [None] * NB
    x1c = [ph.sb(f"x1c{i}", [128, 512], F32) for i in range(2)]
    x1c_free = [None] * 2
    hvb = ph.sb("hvb", [128, BWmax], BF16)
    gb = ph.sb("gb", [128, BWmax], BF16)
    gsem = ph.dsem("gs")
    hvT = ph.sb("hvT", [128, BWmax // 128, 128], BF16)
    hsem = ph.dsem("hs")
    tpp = ph.ps("tpp", [128, BWmax // 128, 128], BF16)
    pc = ph.ps("pc", [128, 2, 3, 512], F32)
    pc_free = [[None] * 3 for _ in range(2)]
    dg = ph.sb("dg", [128, 12, 3, 128], BF16)
    t_dg = None
    for ch in range(12):
        for j in range(3):
            t_dg = ph.op("dve", lambda e, ch=ch, j=j: e.tensor_scalar(out=dg[:, ch, j, :], in0=g["ident"][:], scalar1=cw[:, ch, j:j + 1],
                                                                     scalar2=None, op0=ALU.mult), deps=[t_cw])
    blocks = []
    for si, L in enumerate(SEQS):
        BW = min(2048, L)
        for cc in range(4):
            for c0 in range(0, L, BW):
                blocks.append((si, L, P.offs[si], cc, c0, BW))
    ld_tok = [None] * NB
    ms_tok = [None] * NB

    def load(bi):
        si, L, off, cc, c0, BW = blocks[bi]
        b = bi % NB
        lo = 1 if c0 == 0 else 0
        hi = BW + 1 if c0 + BW == L else BW + 2
        mt = None
        for j in range(3):
            row0 = (j * 4 + cc) * 128
            if lo == 1:
                mt = ph.op("pool", lambda e, b=b, j=j: e.memset(U[b][j][:, 0:1], 0.0), deps=[u_free[b]])
            if hi == BW + 1:
                mt = ph.op("pool", lambda e, b=b, j=j, BW=BW: e.memset(U[b][j][:, BW + 1:BW + 2], 0.0), deps=[u_free[b]])
            ph.dma("sp", U[b][j][:, lo:hi], d["UH"][row0:row0 + 128, off + c0 - 1 + lo:off + c0 - 1 + hi], usem[b], deps=[u_free[b]])
        row0 = (12 + cc) * 128
        ld_tok[b] = ph.dma("sp", SG[b][:, 0:BW], d["UH"][row0:row0 + 128, off + c0:off + c0 + BW], usem[b], deps=[u_free[b]])
        ms_tok[b] = mt

    g_st = None
    h_st = None
    tp_free = None
    load(0)
    for bi, (si, L, off, cc, c0, BW) in enumerate(blocks):
        b = bi % NB
        if bi + 1 < len(blocks):
            load(bi + 1)
        t_hv = None
        t_g = None
        for ct in range(BW // 512):
            pb_ = (bi * 4 + ct) % 2
            c0c = ct * 512
            mm = []
            for j in range(3):
                ch = j * 4 + cc
                for tap in range(3):
                    t = ph.op("pe", lambda e, pb_=pb_, j=j, ch=ch, tap=tap, b=b, c0c=c0c: e.matmul(
                        pc[:, pb_, j, :], lhsT=dg[:, ch, tap, :], rhs=U[b][j][:, c0c + tap:c0c + tap + 512], start=(tap == 0), stop=(tap == 2)),
                        deps=[t_dg, ld_tok[b], ms_tok[b], pc_free[pb_][j]])
                mm.append(t)
            xi = (bi * 4 + ct) % 2
            t_x1 = ph.op("act", lambda e, pb_=pb_, xi=xi, cc=cc: e.activation(out=x1c[xi][:], in_=pc[:, pb_, 1, :], func=AF.Identity,
                                                                             bias=cw[:, 4 + cc, 3:4], scale=1.0),
                         deps=[mm[1], x1c_free[xi]])
            pc_free[pb_][1] = t_x1
            t_hv = ph.op("dve", lambda e, pb_=pb_, xi=xi, cc=cc, c0c=c0c: e.scalar_tensor_tensor(
                out=hvb[:, c0c:c0c + 512], in0=pc[:, pb_, 2, :], scalar=cw[:, 8 + cc, 3:4], in1=x1c[xi][:], op0=ALU.add, op1=ALU.mult),
                deps=[mm[2], t_x1, tp_free])
            pc_free[pb_][2] = t_hv
            x1c_free[xi] = t_hv
            t_g = ph.op("dve", lambda e, pb_=pb_, cc=cc, b=b, c0c=c0c: e.scalar_tensor_tensor(
                out=gb[:, c0c:c0c + 512], in0=pc[:, pb_, 0, :], scalar=cw[:, cc, 3:4], in1=SG[b][:, c0c:c0c + 512], op0=ALU.add, op1=ALU.mult),
                deps=[mm[0], g_st])
            pc_free[pb_][0] = t_g
        u_free[b] = t_g
        g_st = ph.dma("sp", d["GG"][cc * 128:(cc + 1) * 128, off + c0:off + c0 + BW], gb[:, 0:BW], gsem, deps=[t_g])
        ns = BW // 128
        last = None
        for s in range(ns):
            last = ph.op("pe", lambda e, s=s: e.transpose(out=tpp[:, s, :], in_=hvb[:, s * 128:(s + 1) * 128], identity=g["ident"][:]),
                         deps=[t_hv, h_ev if s == 0 and bi > 0 else None])
        tp_free = last
        h_ev = ph.op("act", lambda e, ns=ns: e.activation(out=hvT[:, 0:ns, :], in_=tpp[:, 0:ns, :], func=AF.Copy), deps=[last, h_st])
        h_st = ph.dma("sp", d["HV"][off + c0:off + c0 + BW, cc * 128:(cc + 1) * 128].rearrange("(s p) c -> p s c", p=128),
                      hvT[:, 0:ns, :], hsem, deps=[h_ev])
    ph.run()


def phase_H2(P, l, si):
    nc, d, g = P.nc, P.d, P.g
    L, off = SEQS[si], P.offs[si]
    N1, N2 = fft_dims(L)
    H1 = N1 // 2
    key = f"{L}"
    if f"AD{key}_0" not in d:
        for ri in range(2):
            P.scr(f"AD{key}_{ri}", [N2, N1, HW], BF16)
            P.scr(f"DD{key}_{ri}", [N1, N2, HW], BF16)
    AD = [d[f"AD{key}_{ri}"] for ri in range(2)]
    DD = [d[f"DD{key}_{ri}"] for ri in range(2)]
    Hd = [d[f"H{key}_{ri}"] for ri in range(2)]

    ph = Phase(nc, f"h2a{l}_{si}")
    fft_stage1(ph, P, L, [d["HV"][off:off + L, :]], [[0]], [AD], "a")
    ph.run()

    ph = Phase(nc, f"h2b{l}_{si}")
    f2, t_f2 = fft_stage2_tables(ph, P, L, "b")
    G = FG
    NB = 2
    xin = [ph.sb(f"xin{b}", [N2, 2, G, HW], BF16) for b in range(NB)]
    hin = [ph.sb(f"hin{b}", [N2, 2, G, HW], BF16) for b in range(NB)]
    xsem = [ph.dsem(f"xs{b}") for b in range(NB)]
    dst = [ph.sb(f"dst{b}", [N2, 2, G, HW], BF16) for b in range(NB)]
    dsem_ = [ph.dsem(f"ds{b}") for b in range(NB)]
    NT = 3
    tq = [ph.sb(f"tq{q}", [N2, 4, HW], BF16) for q in range(NT)]
    y_free = [None] * NT
    NS = 8
    pp = ph.ps("pp", [N2, NS, HW], F32)
    pp_free = [None] * NS
    x_free = [None] * NB
    d_free = [None] * NB
    x_ld = [None] * NB
    st = {"slot": 0, "q": 0}

    def getslot():
        s = st["slot"] % NS
        st["slot"] += 1
        return s

    def load(gi):
        b = gi % NB
        k0 = gi * G
        for ri in range(2):
            ph.dma("sp", xin[b][:, ri, :, :], AD[ri][:, k0:k0 + G, :], xsem[b], deps=[x_free[b]])
        for ri in range(2):
            x_ld[b] = ph.dma("sp", hin[b][:, ri, :, :], Hd[ri][:, k0:k0 + G, :], xsem[b], deps=[x_free[b]])

    ngrp = N1 // G
    items = [(gi, kl) for gi in range(ngrp) for kl in range(G)]
    f2tok = {}

    def emit_f2(i):
        gi, kl = items[i]
        b = gi % NB
        sr, si_ = getslot(), getslot()
        ph.op("pe", lambda e: e.matmul(pp[:, sr, :], lhsT=f2[:, 0, :], rhs=xin[b][:, 0, kl, :], start=True, stop=False),
              deps=[t_f2, x_ld[b], pp_free[sr]])
        t_br = ph.op("pe", lambda e: e.matmul(pp[:, sr, :], lhsT=f2[:, 1, :], rhs=xin[b][:, 1, kl, :], start=False, stop=True))
        ph.op("pe", lambda e: e.matmul(pp[:, si_, :], lhsT=f2[:, 0, :], rhs=xin[b][:, 1, kl, :], start=True, stop=False),
              deps=[pp_free[si_]])
        t_bi = ph.op("pe", lambda e: e.matmul(pp[:, si_, :], lhsT=f2[:, 2, :], rhs=xin[b][:, 0, kl, :], start=False, stop=True))
        f2tok[i] = (sr, si_, t_br, t_bi)

    load(0)
    emit_f2(0)
    evs = []
    for i, (gi, kl) in enumerate(items):
        b = gi % NB
        if kl == 0:
            evs = []
            if gi + 1 < ngrp:
                load(gi + 1)
        if i + 1 < len(items):
            emit_f2(i + 1)
        sr, si_, t_br, t_bi = f2tok.pop(i)
        q = i % NT
        m1 = ph.op("dve", lambda e, q=q, sr=sr, b=b, kl=kl: e.tensor_tensor(out=tq[q][:, 0, :], in0=pp[:, sr, :], in1=hin[b][:, 0, kl, :], op=ALU.mult),
                   deps=[t_br, x_ld[b], y_free[q]])
        m2 = ph.op("dve", lambda e, q=q, si_=si_, b=b, kl=kl: e.tensor_tensor(out=tq[q][:, 1, :], in0=pp[:, si_, :], in1=hin[b][:, 1, kl, :], op=ALU.mult),
                   deps=[t_bi])
        m3 = ph.op("dve", lambda e, q=q, sr=sr, b=b, kl=kl: e.tensor_tensor(out=tq[q][:, 2, :], in0=pp[:, sr, :], in1=hin[b][:, 1, kl, :], op=ALU.mult))
        m4 = ph.op("dve", lambda e, q=q, si_=si_, b=b, kl=kl: e.tensor_tensor(out=tq[q][:, 3, :], in0=pp[:, si_, :], in1=hin[b][:, 0, kl, :], op=ALU.mult))
        pp_free[sr] = m3
        pp_free[si_] = m4
        dr, di = getslot(), getslot()
        for n_, (tbl, src) in enumerate([(0, 0), (3, 1), (2, 2), (2, 3)]):
            t_dr = ph.op("pe", lambda e, dr=dr, q=q, tbl=tbl, src=src, n_=n_: e.matmul(pp[:, dr, :], lhsT=f2[:, tbl, :], rhs=tq[q][:, src, :],
                                                                                   start=(n_ == 0), stop=(n_ == 3)),
                         deps=[m4, pp_free[dr]] if n_ == 0 else [])
        for n_, (tbl, src) in enumerate([(0, 2), (0, 3), (1, 0), (2, 1)]):
            t_di = ph.op("pe", lambda e, di=di, q=q, tbl=tbl, src=src, n_=n_: e.matmul(pp[:, di, :], lhsT=f2[:, tbl, :], rhs=tq[q][:, src, :],
                                                                                   start=(n_ == 0), stop=(n_ == 3)),
                         deps=[pp_free[di]] if n_ == 0 else [])
        y_free[q] = t_di
        e1 = ph.op("act", lambda e, dr=dr, b=b, kl=kl: e.activation(out=dst[b][:, 0, kl, :], in_=pp[:, dr, :], func=AF.Copy),
                   deps=[t_dr, d_free[b]])
        e2 = ph.op("act", lambda e, di=di, b=b, kl=kl: e.activation(out=dst[b][:, 1, kl, :], in_=pp[:, di, :], func=AF.Copy),
                   deps=[t_di, d_free[b]])
        pp_free[dr] = e1
        pp_free[di] = e2
        evs += [e1, e2]
        if kl == G - 1:
            x_free[b] = m4
            t = None
            for ri in range(2):
                t = ph.dma("sp", DD[ri][gi * G:(gi + 1) * G, :, :].rearrange("k n c -> n k c"), dst[b][:, ri, :, :], dsem_[b], deps=evs)
            d_free[b] = t
    ph.run()

    ph = Phase(nc, f"h2c{l}_{si}")
    ic = ph.sb("ic", [N1, N2, H1], BF16)
    isn = ph.sb("isn", [N1, N2, H1], BF16)
    ldt = ph.dsem("ldt")
    ph.dma("sp", ic[:], d[f"c_i2c{L}"][:], ldt)
    t_tab = ph.dma("sp", isn[:], d[f"c_i2s{L}"][:], ldt)
    yc = ph.sb("yc", [128, 4, L], BF16)
    dd = [ph.sb(f"dd{b}", [N1, 2, G, HW], BF16) for b in range(NB)]
    ddsem = [ph.dsem(f"dds{b}") for b in range(NB)]
    dd_free = [None] * NB
    dd_ld = [None] * NB
    NZ = 4
    pz = ph.ps("pz", [128, NZ, 512], F32)
    pz_free = [None] * NZ
    zc = 0

    def load2(gi):
        b = gi % NB
        for ri in range(2):
            dd_ld[b] = ph.dma("sp", dd[b][:, ri, :, :], DD[ri][:, gi * G:(gi + 1) * G, :], ddsem[b], deps=[dd_free[b]])

    ngrp = N2 // G
    load2(0)
    evs = []
    for gi in range(ngrp):
        b = gi % NB
        if gi + 1 < ngrp:
            load2(gi + 1)
        last = None
        for cc in range(4):
            z = zc % NZ
            zc += 1
            for n2l in range(G):
                n2 = gi * G + n2l
                ph.op("pe", lambda e, z=z, n2l=n2l, n2=n2, b=b, cc=cc: e.matmul(pz[:, z, n2l * H1:(n2l + 1) * H1], lhsT=dd[b][:, 0, n2l, cc * 128:(cc + 1) * 128],
                                                                              rhs=ic[:, n2, :], start=True, stop=False),
                      deps=[t_tab, dd_ld[b], pz_free[z]])
                last = ph.op("pe", lambda e, z=z, n2l=n2l, n2=n2, b=b, cc=cc: e.matmul(pz[:, z, n2l * H1:(n2l + 1) * H1], lhsT=dd[b][:, 1, n2l, cc * 128:(cc + 1) * 128],
                                                                                     rhs=isn[:, n2, :], start=False, stop=True))
            dstv = yc[:, cc, :].rearrange("p (a n) -> p n a", n=N2)[:, gi * G:(gi + 1) * G, :]
            if cc % 2 == 0:
                ev = ph.op("act", lambda e, z=z, dstv=dstv: e.activation(out=dstv, in_=pz[:, z, 0:G * H1].rearrange("p (g a) -> p g a", a=H1), func=AF.Copy), deps=[last])
            else:
                ev = ph.op("dve", lambda e, z=z, dstv=dstv: e.tensor_copy(out=dstv, in_=pz[:, z, 0:G * H1].rearrange("p (g a) -> p g a", a=H1)), deps=[last])
            pz_free[z] = ev
            evs.append(ev)
        dd_free[b] = last
    ysem = ph.dsem("ys")
    for cc in range(4):
        ph.dma("sp", d["YC"][cc * 128:(cc + 1) * 128, off:off + L], yc[:, cc, :], ysem, deps=evs[-8:])
    ph.run()


def phase_M(P, l):
    nc, d, g = P.nc, P.d, P.g
    T = P.T
    last_layer = l == DEPTH - 1
    xsrc = d["x"] if l == 0 else d["XR"]
    xdst = d["y"] if last_layer else d["XR"]
    ph = Phase(nc, f"M{l}")
    Wm = ph.sb("Wm", [128, 8, 3 * D], BF16)
    Wbr = ph.sb("Wbr", [128, 3, 4, D], BF16)
    Wo = ph.sb("Wo", [128, 8, D], BF16)
    bm = ph.sb("bm", [128, 24], F32)
    fg = ph.sb("fg", [128, D], F32)
    wl = P.gsem[0]
    t_w = None
    for k in range(8):
        t_w = ph.dma("pool", Wm[:, k, :], d["w_merge"][l, k * 128:(k + 1) * 128, :], wl)
    for j, nm in enumerate(["w_branch_hy", "w_branch_gqa", "w_branch_diff"]):
        for k in range(4):
            t_w = ph.dma("pool", Wbr[:, j, k, :], d[nm][l, k * 128:(k + 1) * 128, :], wl)
    for k in range(8):
        t_w = ph.dma("pool", Wo[:, k, :], d["w_out"][l, k * 128:(k + 1) * 128, :], wl)
    cl = ph.dsem("cl")
    ph.dma("sp", bm[:], d["b_merge"][l:l + 1, :].rearrange("o (j p) -> p (o j)", p=128), cl, slow=True)
    t_c = ph.dma("sp", fg[:], bcast_rows(d["final_g"].rearrange("(o f) -> o f", o=1), 128), cl)

    NB = 2
    hT = [ph.sb(f"hT{b}", [128, 8, 512], BF16) for b in range(NB)]
    Y = [ph.sb(f"Y{b}", [128, 4, 4, 512], BF16) for b in range(NB)]
    isem = [ph.dsem(f"is{b}") for b in range(NB)]
    in_free = [None] * NB
    in_ld = [None] * NB
    xt = [ph.sb(f"xt{s}", [128, D], F32) for s in range(4)]
    xsem = [ph.dsem(f"xs{s}") for s in range(4)]
    x_free = [None] * 4
    x_ld = [None] * 4
    gt = [ph.sb(f"gt{j}", [128, 512], F32) for j in range(3)]
    gt_free = [None] * 3
    acc = ph.sb("acc", [128, 512], F32)
    tm1 = ph.sb("tm1", [128, 512], F32)
    tm2 = ph.sb("tm2", [128, 512], F32)
    mg = ph.sb("mg", [128, 8, 512], BF16)
    mg_free = None
    xo = [ph.sb(f"xo{i}", [128, D], F32) for i in range(2)]
    xosem = [ph.dsem(f"xos{i}") for i in range(2)]
    xo_free = [None] * 2
    junk = ph.sb("junk", [128, D], BF16)
    ssq = ph.sb("ssq", [128, 2], F32)
    NS = 7
    pg = ph.ps("pg", [128, NS, 512], F32)
    slot_free = [None] * NS
    st = {"slot": 0, "xo": 0}

    def getslot():
        s = st["slot"] % NS
        st["slot"] += 1
        return s

    ntile = T // 512

    def load_in(tt):
        b = tt % NB
        t0 = tt * 512
        ph.dma("sp", hT[b][:], d["HT"][:, t0:t0 + 512].rearrange("(k p) t -> p k t", p=128), isem[b], deps=[in_free[b]])
        for j, nm in enumerate(["YC", "GG", "YG", "YD"]):
            in_ld[b] = ph.dma("sp", Y[b][:, j, :, :], d[nm][:, t0:t0 + 512].rearrange("(k p) t -> p k t", p=128), isem[b], deps=[in_free[b]])

    load_in(0)
    a_rd = None
    for tt in range(ntile):
        b = tt % NB
        t0 = tt * 512
        if tt + 1 < ntile:
            load_in(tt + 1)
        for s in range(4):
            x_ld[s] = ph.dma("sp", xt[s][:], xsrc[t0 + s * 128:t0 + (s + 1) * 128, :], xsem[s], deps=[x_free[s]])
        t_yh = ph.op("dve", lambda e, b=b: e.tensor_tensor(out=Y[b][:, 0, :, :], in0=Y[b][:, 0, :, :], in1=Y[b][:, 1, :, :], op=ALU.mult),
                     deps=[in_ld[b]])
        ysel = [0, 2, 3]
        last_pe = None
        for m in range(8):
            tg = []
            for j in range(3):
                su = getslot()
                for k in range(8):
                    last_pe = ph.op("pe", lambda e, su=su, k=k, j=j, m=m, b=b: e.matmul(pg[:, su, :], lhsT=Wm[:, k, j * D + m * 128:j * D + (m + 1) * 128],
                                                                                      rhs=hT[b][:, k, :], start=(k == 0), stop=(k == 7)),
                                    deps=[t_w, in_ld[b], slot_free[su]])
                t = ph.op("act", lambda e, su=su, j=j, m=m: e.activation(out=gt[j][:], in_=pg[:, su, :], func=AF.Sigmoid,
                                                                       bias=bm[:, j * 8 + m:j * 8 + m + 1], scale=1.0),
                          deps=[last_pe, gt_free[j], t_c])
                slot_free[su] = t
                tg.append(t)
            tb = []
            sb_ = []
            for j in range(3):
                su = getslot()
                sb_.append(su)
                for k in range(4):
                    last_pe = ph.op("pe", lambda e, su=su, k=k, j=j, m=m, b=b: e.matmul(pg[:, su, :], lhsT=Wbr[:, j, k, m * 128:(m + 1) * 128],
                                                                                      rhs=Y[b][:, ysel[j], k, :], start=(k == 0), stop=(k == 3)),
                                    deps=[t_yh, slot_free[su]])
                tb.append(last_pe)
            d0 = ph.op("dve", lambda e, su=sb_[0]: e.tensor_tensor(out=acc[:], in0=pg[:, su, :], in1=gt[0][:], op=ALU.mult),
                       deps=[tb[0], tg[0], a_rd])
            d1 = ph.op("dve", lambda e, su=sb_[1]: e.tensor_tensor(out=tm1[:], in0=pg[:, su, :], in1=gt[1][:], op=ALU.mult),
                       deps=[tb[1], tg[1]])
            d2 = ph.op("dve", lambda e, su=sb_[2]: e.tensor_tensor(out=tm2[:], in0=pg[:, su, :], in1=gt[2][:], op=ALU.mult),
                       deps=[tb[2], tg[2]])
            slot_free[sb_[0]] = d0
            slot_free[sb_[1]] = d1
            slot_free[sb_[2]] = d2
            gt_free[0] = d0
            gt_free[1] = d1
            gt_free[2] = d2
            p1 = ph.op("pool", lambda e: e.tensor_tensor(out=acc[:], in0=acc[:], in1=tm1[:], op=ALU.add), deps=[d0, d1])
            p2 = ph.op("pool", lambda e, m=m: e.tensor_tensor(out=mg[:, m, :], in0=acc[:], in1=tm2[:], op=ALU.add), deps=[p1, d2, mg_free])
            a_rd = p2
        in_free[b] = last_pe
        mg_rdy = a_rd
        for s in range(4):
            xi = st["xo"] % 2
            st["xo"] += 1
            tr = []
            for hh in range(2):
                su = getslot()
                for k in range(8):
                    last_pe = ph.op("pe", lambda e, su=su, k=k, s=s, hh=hh: e.matmul(pg[:, su, :], lhsT=mg[:, k, s * 128:(s + 1) * 128],
                                                                                   rhs=Wo[:, k, hh * 512:(hh + 1) * 512], start=(k == 0), stop=(k == 7)),
                                    deps=[mg_rdy, slot_free[su]])
                t = ph.op("dve", lambda e, su=su, s=s, hh=hh, xi=xi: e.tensor_tensor(out=xo[xi][:, hh * 512:(hh + 1) * 512], in0=pg[:, su, :],
                                                                                   in1=xt[s][:, hh * 512:(hh + 1) * 512], op=ALU.add),
                          deps=[last_pe, x_ld[s], xo_free[xi]])
                slot_free[su] = t
                tr.append(t)
            x_free[s] = tr[-1]
            r0 = t0 + s * 128
            if last_layer:
                c = xi
                t_ss = ph.op("act", lambda e, xi=xi, c=c: e.activation(out=junk[:], in_=xo[xi][:], func=AF.Square, accum_out=ssq[:, c:c + 1]),
                             deps=tr)
                t_sd = ph.op("act", lambda e, c=c: e.activation(out=ssq[:, c:c + 1], in_=ssq[:, c:c + 1], func=AF.Sqrt, bias=g["eps"][:], scale=1.0 / D),
                             deps=[t_ss])
                t_r = ph.op("dve", lambda e, c=c: e.reciprocal(out=ssq[:, c:c + 1], in_=ssq[:, c:c + 1]), deps=[t_sd])
                t_y = ph.op("dve", lambda e, xi=xi, c=c: e.scalar_tensor_tensor(out=xo[xi][:], in0=xo[xi][:], scalar=ssq[:, c:c + 1], in1=fg[:],
                                                                             op0=ALU.mult, op1=ALU.mult), deps=[t_r, t_c])
                xo_free[xi] = ph.dma("sp", xdst[r0:r0 + 128, :], xo[xi][:], xosem[xi], deps=[t_y])
            else:
                xo_free[xi] = ph.dma("sp", xdst[r0:r0 + 128, :], xo[xi][:], xosem[xi], deps=tr)
        mg_free = last_pe
    ph.run()


def build(stop=None):
    P = Prog()
    stop = stop or STOP_AFTER
    phase_prep(P)
    if stop == "prep":
        return P
    for l in range(DEPTH):
        for L in P.Ls:
            phase_filter(P, l, L)
        if stop == f"filter{l}":
            return P
        phase_A(P, l)
        if stop == f"A{l}":
            return P
        phase_H1(P, l)
        if stop == f"H1{l}":
            return P
        for si in range(len(SEQS)):
            phase_H2(P, l, si)
        if stop == f"H2{l}":
            return P
        phase_attn(P, l, "G")
        if stop == f"G{l}":
            return P
        phase_attn(P, l, "D")
        if stop == f"D{l}":
            return P
        phase_M(P, l)
        if stop == f"M{l}":
            return P
    return P


def make_in_maps(inputs, ncores):
    consts = host_consts()
    xp = np.asarray(inputs["x_prompt"], np.float32)
    xs = np.asarray(inputs["x_sample"], np.float32)
    maps = []
    for c in range(ncores):
        m = {}
        m["x"] = np.ascontiguousarray(np.concatenate([xp[c], xs[2 * c], xs[2 * c + 1]], axis=0))
        for k in PARAM_NAMES:
            m[k] = np.ascontiguousarray(np.asarray(inputs[k], np.float32))
        for k, v in consts.items():
            m["c_" + k] = v
        maps.append(m)
    return maps


def kernel(**inputs):
    P = build()
    maps = make_in_maps(inputs, NCORES)
    res = run_bass_kernel_spmd(P.nc, maps, core_ids=list(range(NCORES)))
    Lp, Ls = SEQS[0], SEQS[1]
    yp = np.stack([res.results[c]["y"][:Lp] for c in range(NCORES)], 0)
    ys = np.stack([res.results[c]["y"][Lp + j * Ls: Lp + (j + 1) * Ls] for c in range(NCORES) for j in range(2)], 0)
    return (np.ascontiguousarray(yp, dtype=np.float32), np.ascontiguousarray(ys, dtype=np.float32))
```

```python
import math
from contextlib import ExitStack

import numpy as np
import ml_dtypes

import concourse.bass as bass
import concourse.mybir as mybir
from concourse.bass_utils import run_bass_kernel_spmd

F32 = mybir.dt.float32
BF16 = mybir.dt.bfloat16
AF = mybir.ActivationFunctionType
ALU = mybir.AluOpType
AX = mybir.AxisListType

D = 1024
NCOL = 5376
HW = 512
EPS = 1e-6
DEPTH = 2
NCORES = 8
PI = math.pi

SEQS = [8192, 2048, 2048]
DEBUG_OUT = set()
STOP_AFTER = None

ENGS = ["pe", "act", "dve", "pool", "sp"]
BLK = {"pe": "tensor", "act": "scalar", "dve": "vector", "pool": "gpsimd", "sp": "sync"}


def fft_dims(L):
    n = 2 * L
    n1 = 1
    while n1 * n1 < n:
        n1 *= 2
    assert n1 * n1 == n, "2L must be a power of 4"
    return n1, n1


class DSem:
    def __init__(self, h, glob=False):
        self.h = h
        self.val = 0
        self.glob = glob


class Phase:
    def __init__(self, nc, name):
        self.nc = nc
        self.name = name
        self.es = ExitStack()
        self.q = {e: [] for e in ENGS}
        self.cnt = {e: 0 for e in ENGS}
        self.csem = {e: self.es.enter_context(nc.semaphore(f"{name}_{e}")) for e in ENGS}
        self.seen = {e: {} for e in ENGS}
        self.dsems = []
        self.gused = []
        self.n = 0

    def sb(self, name, shape, dt):
        return self.es.enter_context(self.nc.sbuf_tensor(f"{self.name}_{name}", list(shape), dt))

    def ps(self, name, shape, dt=F32):
        return self.es.enter_context(self.nc.psum_tensor(f"{self.name}_{name}", list(shape), dt))

    def dsem(self, name):
        d = DSem(self.es.enter_context(self.nc.semaphore(f"{self.name}_d_{name}")))
        self.dsems.append(d)
        return d

    def _waits(self, eng, deps):
        w = []
        for t in deps:
            if t is None:
                continue
            kind, key, val = t
            if kind == "c":
                if key == "pe" and eng == "pe":
                    continue
                h = self.csem[key]
                k = ("c", key)
            else:
                h = key.h
                k = ("d", id(key))
            if self.seen[eng].get(k, 0) >= val:
                continue
            self.seen[eng][k] = val
            w.append((h, val))
        return w

    def op(self, eng, fn, deps=()):
        w = self._waits(eng, deps)
        self.cnt[eng] += 1
        self.q[eng].append((w, fn, self.csem[eng], 1))
        self.n += 1
        return ("c", eng, self.cnt[eng])

    def dma(self, eng, out, in_, ds, deps=(), slow=False):
        assert (eng == "pool") == ds.glob, "pool-queue DMAs must use a global DSem (and only they)"
        if ds.glob and ds not in self.gused:
            self.gused.append(ds)
        w = self._waits(eng, deps)
        ds.val += 16
        if slow:
            fn = lambda e: e.dma_start(out=out, in_=in_, allow_slow_non_contiguous=True)
        else:
            fn = lambda e: e.dma_start(out=out, in_=in_)
        self.q[eng].append((w, fn, ds.h, 16))
        self.n += 1
        return ("d", ds, ds.val)

    def run(self):
        fin = []
        for d in self.dsems + self.gused:
            if d.val > 0 and self.seen["sp"].get(("d", id(d)), 0) < d.val:
                fin.append((d.h, d.val))
        q = self.q

        def emit(eng_name):
            def f(e):
                for (w, fn, sem, inc) in q[eng_name]:
                    for (h, v) in w:
                        e.wait_ge(h, v)
                    ins = fn(e)
                    ins.then_inc(sem, inc)
                if eng_name == "sp":
                    for (h, v) in fin:
                        e.wait_ge(h, v)
            return f

        with self.nc.Block() as block:
            for en in ENGS:
                if q[en] or (en == "sp" and fin):
                    getattr(block, BLK[en])(emit(en))
        allsems = [self.csem[e] for e in ENGS] + [d.h for d in self.dsems]

        def clr(e):
            for h in allsems:
                e.sem_clear(h)

        with self.nc.Block() as block:
            block.sync(clr)
        self.es.close()


_CONSTS = {}


def t5_bucket_np(rel):
    nb = 16
    max_exact = 8
    ret = (rel > 0).astype(np.int32) * nb
    n = np.abs(rel)
    nf = np.maximum(n, 1).astype(np.float32)
    large = max_exact + (np.log(nf / np.float32(max_exact)) / np.float32(math.log(128 / max_exact))
                         * np.float32(nb - max_exact)).astype(np.int32)
    large = np.minimum(large, nb - 1)
    return ret + np.where(n < max_exact, n, large)


def host_consts():
    key = tuple(SEQS)
    if key in _CONSTS:
        return _CONSTS[key]
    bf = ml_dtypes.bfloat16
    c = {}
    c["ident"] = np.eye(128, dtype=np.float32).astype(bf)
    ps = np.zeros((128, 128), np.float32)
    for m in range(128):
        if m % 64 < 32:
            ps[m + 32, m] = -1.0
        else:
            ps[m - 32, m] = 1.0
    c["pswap"] = ps.astype(bf)
    ob = np.zeros((128, 128), np.float32)
    ob[:64, :64] = 1.0 / 64
    ob[64:, 64:] = 1.0 / 64
    c["onesblk"] = ob.astype(bf)
    c["ones128"] = np.full((128, 128), 1.0 / 128, np.float32).astype(bf)
    c["ones1"] = np.ones((128, 128), np.float32).astype(bf)
    Lmax = max(SEQS)
    t = np.arange(Lmax)
    row = (t // 64).astype(np.float32)
    col = (t % 64).astype(np.float32)
    n_freq = 16
    inv = (np.float32(10000.0) ** (-np.arange(n_freq, dtype=np.float32) / np.float32(n_freq))).astype(np.float32)
    ang = np.concatenate([row[:, None] * inv, col[:, None] * inv], axis=-1).astype(np.float32)
    cs = np.cos(ang).astype(np.float32).T
    sn = np.sin(ang).astype(np.float32).T
    c["ropec"] = np.ascontiguousarray(np.tile(cs, (4, 1)))
    c["ropes"] = np.ascontiguousarray(np.tile(sn, (4, 1)))
    e = np.arange(1280)
    bk = t5_bucket_np((e - 639).astype(np.int32))
    oh = np.zeros((32, 1280), np.float32)
    oh[bk, e] = 1.0
    c["t5oh"] = oh
    for L in sorted(set(SEQS)):
        t01 = np.linspace(0.0, 1.0, L, dtype=np.float32)[:, None]
        bands = 16
        w = (np.float32(2.0 * math.pi) * np.arange(L, dtype=np.float32)[:, None] / np.float32(L)).astype(np.float32)
        f = np.linspace(1e-4, bands - 1, bands, dtype=np.float32)[None, :]
        z = np.concatenate([t01, np.cos(f * w), -np.sin(f * w)], axis=-1).astype(np.float32)
        c[f"zT{L}"] = np.ascontiguousarray(z.T)
        max_decay = math.log(1e-2) / 0.3
        min_decay = math.log(1e-2) / 1.5
        deltas = np.linspace(min_decay, max_decay, HW, dtype=np.float32)
        win = np.exp(-t01 * np.abs(deltas)[None, :]).astype(np.float32)
        wb = win.copy()
        wb[0, :] = 0.0
        c[f"win{L}"] = np.ascontiguousarray(np.stack([win, wb], 0))
        N1, N2 = fft_dims(L)
        N = N1 * N2
        n1 = np.arange(N1 // 2)[:, None, None]
        n2 = np.arange(N2)[None, :, None]
        k1 = np.arange(N1)[None, None, :]
        ph = (k1 * (N2 * n1 + n2)) % N
        th = 2.0 * np.pi * ph.astype(np.float64) / N
        f1r = np.cos(th).astype(np.float32).astype(bf)
        f1i = (-np.sin(th)).astype(np.float32).astype(bf)
        if N1 // 2 >= 32:
            f1r = np.ascontiguousarray(np.concatenate([f1r[:, 0::2, :], f1r[:, 1::2, :]], axis=0))
            f1i = np.ascontiguousarray(np.concatenate([f1i[:, 0::2, :], f1i[:, 1::2, :]], axis=0))
        c[f"f1r{L}"] = f1r
        c[f"f1i{L}"] = f1i
        a = np.arange(N2)[:, None]
        b = np.arange(N2)[None, :]
        th2 = 2.0 * np.pi * ((a * b) % N2).astype(np.float64) / N2
        c[f"f2{L}"] = np.stack([np.cos(th2), np.sin(th2), -np.sin(th2), -np.cos(th2)], 1).astype(np.float32).astype(bf)
        k1 = np.arange(N1)[:, None, None]
        n2 = np.arange(N2)[None, :, None]
        n1 = np.arange(N1 // 2)[None, None, :]
        ph = (k1 * (N2 * n1 + n2)) % N
        th = 2.0 * np.pi * ph.astype(np.float64) / N
        c[f"i2c{L}"] = (np.cos(th) / N).astype(np.float32).astype(bf)
        c[f"i2s{L}"] = (-np.sin(th) / N).astype(np.float32).astype(bf)
    _CONSTS[key] = c
    return c


PARAM_NAMES = ["rel_bias", "norm_g", "w_in", "hy_conv_w", "hy_conv_b", "hy_f_w1", "hy_f_b1", "hy_f_w2",
               "hy_f_b2", "hy_f_wout", "hy_f_freq", "hy_bias", "q_norm_g", "k_norm_g", "lam_q1", "lam_k1",
               "lam_q2", "lam_k2", "diff_subln_g", "w_branch_hy", "w_branch_gqa", "w_branch_diff",
               "w_merge", "b_merge", "w_out", "final_g"]

PARAM_SHAPES = {
    "rel_bias": (32, 4), "norm_g": (DEPTH, D), "w_in": (DEPTH, D, NCOL), "hy_conv_w": (DEPTH, 3, 1536),
    "hy_conv_b": (DEPTH, 1536), "hy_f_w1": (DEPTH, 33, 64), "hy_f_b1": (DEPTH, 64),
    "hy_f_w2": (DEPTH, 2, 64, 64), "hy_f_b2": (DEPTH, 2, 64), "hy_f_wout": (DEPTH, 64, 1024),
    "hy_f_freq": (DEPTH, 64), "hy_bias": (DEPTH, HW), "q_norm_g": (DEPTH, 64), "k_norm_g": (DEPTH, 64),
    "lam_q1": (DEPTH, 64), "lam_k1": (DEPTH, 64), "lam_q2": (DEPTH, 64), "lam_k2": (DEPTH, 64),
    "diff_subln_g": (DEPTH, 128), "w_branch_hy": (DEPTH, HW, D), "w_branch_gqa": (DEPTH, HW, D),
    "w_branch_diff": (DEPTH, HW, D), "w_merge": (DEPTH, D, 3 * D), "b_merge": (DEPTH, 3 * D),
    "w_out": (DEPTH, D, D), "final_g": (D,),
}


def bcast_rows(ap_1d_or_row, nparts):
    a = ap_1d_or_row
    return bass.AP(tensor=a.tensor, offset=a.offset, ap=[[0, nparts]] + [list(x) for x in a.ap[-1:]])


class Prog:
    def __init__(self):
        self.nc = bass.Bass("TRN2", target_bir_lowering=False)
        nc = self.nc
        self.T = sum(SEQS)
        self.offs = [sum(SEQS[:i]) for i in range(len(SEQS))]
        self.Ls = sorted(set(SEQS), reverse=True)
        self.consts = host_consts()
        self.d = {}
        self.d["x"] = nc.dram_tensor("x", [self.T, D], F32, kind="ExternalInput").ap()
        for k in PARAM_NAMES:
            self.d[k] = nc.dram_tensor(k, list(PARAM_SHAPES[k]), F32, kind="ExternalInput").ap()
        for k, v in self.consts.items():
            dt = BF16 if v.dtype == ml_dtypes.bfloat16 else F32
            self.d["c_" + k] = nc.dram_tensor("c_" + k, list(v.shape), dt, kind="ExternalInput").ap()
        self.d["y"] = nc.dram_tensor("y", [self.T, D], F32, kind="ExternalOutput").ap()
        self.ges = ExitStack()
        self.g = {}
        self.gsem = [DSem(self.ges.enter_context(nc.semaphore(f"gpool{i}")), glob=True) for i in range(4)]

    def scr(self, name, shape, dt):
        kind = "ExternalOutput" if name in DEBUG_OUT else "Internal"
        a = self.nc.dram_tensor(name, list(shape), dt, kind=kind).ap()
        self.d[name] = a
        return a

    def gsb(self, name, shape, dt):
        t = self.ges.enter_context(self.nc.sbuf_tensor("g_" + name, list(shape), dt))
        self.g[name] = t
        return t


def phase_prep(P):
    nc, d, g = P.nc, P.d, P.g
    for nm in ["ident", "pswap", "onesblk", "ones128", "ones1"]:
        P.gsb(nm, [128, 128], BF16)
    P.gsb("wstrip", [128, 4, 1152], BF16)
    P.gsb("farb", [128, 4, 2], F32)
    P.gsb("lamt", [128, DEPTH, 2], F32)
    P.gsb("eps", [128, 1], F32)
    P.gsb("onesf", [128, 128], F32)
    bvr = P.scr("BVR", [4, 1280], F32)
    ph = Phase(nc, "prep")
    ld = ph.dsem("ld")
    toks = []
    for nm in ["ident", "pswap", "onesblk", "ones128", "ones1"]:
        toks.append(ph.dma("sp", g[nm][:], d["c_" + nm][:], ld))
    t_eps = ph.op("pool", lambda e: e.memset(g["eps"][:], EPS))
    ph.op("pool", lambda e: e.memset(g["onesf"][:], 1.0))
    rb = ph.sb("rb", [32, 4], F32)
    oh = ph.sb("oh", [32, 1280], F32)
    bv = ph.sb("bv", [4, 1280], F32)
    pb = ph.ps("pb", [4, 3, 512], F32)
    l2 = ph.dsem("l2")
    ph.dma("sp", rb[:], d["rel_bias"][:], l2)
    t_oh = ph.dma("sp", oh[:], d["c_t5oh"][:], l2)
    widths = [512, 512, 256]
    tcp = []
    for i, wd in enumerate(widths):
        tm = ph.op("pe", lambda e, i=i, wd=wd: e.matmul(pb[:, i, 0:wd], lhsT=rb[:], rhs=oh[:, i * 512:i * 512 + wd],
                                                         start=True, stop=True), deps=[t_oh])
        tcp.append(ph.op("dve", lambda e, i=i, wd=wd: e.tensor_copy(out=bv[:, i * 512:i * 512 + wd], in_=pb[:, i, 0:wd]),
                         deps=[tm]))
    st = ph.dsem("st")
    t_st = ph.dma("sp", bvr[:], bv[:], st, deps=tcp)
    l3 = ph.dsem("l3")
    wrev = ph.sb("wrev", [128, 4, 1152], F32)
    for h in range(4):
        src = bass.AP(tensor=bvr.tensor, offset=h * 1280, ap=[[1, 128], [1, 1152]])
        t_wr = ph.dma("sp", wrev[:, h, :], src, l3, deps=[t_st])
        for side, e0 in enumerate([0, 1278]):
            src = bass.AP(tensor=bvr.tensor, offset=h * 1280 + e0, ap=[[0, 128], [1, 1]])
            t_wr = ph.dma("sp", g["farb"][:, h, side:side + 1], src, l3, deps=[t_st])
    for h in range(4):
        a = wrev[:, h, :]
        rev = bass.AP(tensor=a.tensor, offset=a.offset + 1151, ap=[list(a.ap[0]), [-1, 1152]])
        ph.op("dve", lambda e, h=h, rev=rev: e.tensor_scalar(out=g["wstrip"][:, h, :], in0=rev, scalar1=8.0, scalar2=None, op0=ALU.mult),
              deps=[t_wr])
    lv = ph.sb("lv", [128, DEPTH, 4, 64], F32)
    l4 = ph.dsem("l4")
    t_lv = None
    for l in range(DEPTH):
        for j, nm in enumerate(["lam_q1", "lam_k1", "lam_q2", "lam_k2"]):
            t_lv = ph.dma("sp", lv[:, l, j, :], bcast_rows(d[nm][l:l + 1, :], 128), l4)
    junk = ph.sb("junk", [128, 64], F32)
    ss = ph.sb("ss", [128, DEPTH, 2], F32)
    ee = ph.sb("ee", [128, DEPTH, 2], F32)
    df = ph.sb("df", [128, DEPTH], F32)
    for l in range(DEPTH):
        lam_init = 0.8 - 0.6 * math.exp(-0.3 * l)
        ts = []
        prev = None
        for j in range(2):
            tmul = ph.op("dve", lambda e, l=l, j=j: e.tensor_tensor(out=junk[:], in0=lv[:, l, 2 * j, :], in1=lv[:, l, 2 * j + 1, :],
                                                                 op=ALU.mult), deps=[t_lv, prev])
            prev = ph.op("dve", lambda e, l=l, j=j: e.tensor_reduce(out=ss[:, l, j:j + 1], in_=junk[:], axis=AX.X, op=ALU.add),
                         deps=[tmul])
            ts.append(prev)
        te = ph.op("act", lambda e, l=l: e.activation(out=ee[:, l, :], in_=ss[:, l, :], func=AF.Exp), deps=ts)
        t1 = ph.op("dve", lambda e, l=l: e.tensor_tensor(out=df[:, l:l + 1], in0=ee[:, l, 1:2], in1=ee[:, l, 0:1],
                                                         op=ALU.subtract), deps=[te, prev])
        ph.op("dve", lambda e, l=l, li=lam_init: e.tensor_scalar(out=g["lamt"][:, l, 0:1], in0=df[:, l:l + 1],
                                                                  scalar1=-li, scalar2=None, op0=ALU.add), deps=[t1])
    ph.run()


def fft_stage1(ph, P, L, srcs, combos, Ad, tagp):
    nc, d = P.nc, P.d
    N1, N2 = fft_dims(L)
    H1 = N1 // 2
    G = FG
    paired = H1 >= 32
    NP_ = 2 * H1 if paired else H1
    GH = G // 2 if paired else G
    ncol = N2 // 2 if paired else N2
    tr = ph.sb(tagp + "tr", [NP_, ncol, N1], BF16)
    ti = ph.sb(tagp + "ti", [NP_, ncol, N1], BF16)
    ldt = ph.dsem(tagp + "ldt")
    ph.dma("sp", tr[:], d[f"c_f1r{L}"][:], ldt)
    t_tab = ph.dma("sp", ti[:], d[f"c_f1i{L}"][:], ldt)
    ns = len(srcs)
    NB = 2
    xs = [ph.sb(f"{tagp}x{b}", [NP_, ns, GH, HW], BF16) for b in range(NB)]
    xsem = [ph.dsem(f"{tagp}xs{b}") for b in range(NB)]
    no = len(combos)
    stg = [ph.sb(f"{tagp}st{b}", [N1, no, 2, G, HW], BF16) for b in range(NB)]
    ssem = [ph.dsem(f"{tagp}ss{b}") for b in range(NB)]
    NSL = 4
    pp = ph.ps(tagp + "pp", [N1, NSL, HW], F32)
    ngrp = N2 // G
    x_free = [None] * NB
    st_free = [None] * NB
    pp_free = [None] * NSL
    x_ld = [None] * NB

    def load(gi):
        b = gi % NB
        t = None
        for si, s in enumerate(srcs):
            v = s.rearrange("(a n) c -> a n c", n=N2)
            if paired:
                for hf in range(2):
                    src = v[0:H1, gi * G + hf:(gi + 1) * G:2, :]
                    t = ph.dma("sp", xs[b][hf * H1:(hf + 1) * H1, si, :, :], src, xsem[b], deps=[x_free[b]])
            else:
                t = ph.dma("sp", xs[b][:, si, :, :], v[0:H1, gi * G:(gi + 1) * G, :], xsem[b], deps=[x_free[b]])
        x_ld[b] = t

    load(0)
    cnt = 0
    for gi in range(ngrp):
        b = gi % NB
        if gi + 1 < ngrp:
            load(gi + 1)
        evs = []
        last_mm = None
        for ml in range(GH):
            for o, combo in enumerate(combos):
                for ri, tab in enumerate([tr, ti]):
                    halves = [0, 1] if paired else [0]
                    slots = []
                    for hf in halves:
                        slots.append(cnt % NSL)
                        cnt += 1
                    col = (gi * G) // (2 if paired else 1) + ml
                    for ci, si in enumerate(combo):
                        for hf in halves:
                            slot = slots[hf]
                            p0, p1 = hf * H1, (hf + 1) * H1
                            kw = dict(tile_position=(p0, 0)) if paired else {}
                            last_mm = ph.op("pe", lambda e, slot=slot, tab=tab, col=col, b=b, si=si, ml=ml, ci=ci, nci=len(combo), p0=p0, p1=p1, kw=kw:
                                            e.matmul(pp[:, slot, :], lhsT=tab[p0:p1, col, :], rhs=xs[b][p0:p1, si, ml, :],
                                                     start=(ci == 0), stop=(ci == nci - 1), **kw),
                                            deps=[t_tab, x_ld[b], pp_free[slot]])
                    for hf in halves:
                        slot = slots[hf]
                        n2l = (2 * ml + hf) if paired else ml
                        eng = "act" if (cnt + hf) % 2 else "dve"
                        if eng == "act":
                            ev = ph.op("act", lambda e, slot=slot, b=b, o=o, ri=ri, n2l=n2l:
                                       e.activation(out=stg[b][:, o, ri, n2l, :], in_=pp[:, slot, :], func=AF.Copy),
                                       deps=[last_mm, st_free[b]])
                        else:
                            ev = ph.op("dve", lambda e, slot=slot, b=b, o=o, ri=ri, n2l=n2l:
                                       e.tensor_copy(out=stg[b][:, o, ri, n2l, :], in_=pp[:, slot, :]),
                                       deps=[last_mm, st_free[b]])
                        pp_free[slot] = ev
                        evs.append(ev)
        x_free[b] = last_mm
        t = None
        for o in range(no):
            for ri in range(2):
                dst = Ad[o][ri][gi * G:(gi + 1) * G, :, :].rearrange("n k c -> k n c")
                t = ph.dma("sp", dst, stg[b][:, o, ri, :, :], ssem[b], deps=evs)
        st_free[b] = t


def fft_stage2_tables(ph, P, L, tagp):
    d = P.d
    N1, N2 = fft_dims(L)
    f2 = ph.sb(tagp + "f2", [N2, 4, N2], BF16)
    ldt = ph.dsem(tagp + "ldf2")
    t = ph.dma("sp", f2[:], d[f"c_f2{L}"][:], ldt)
    return f2, t


FG = 4


def load_ktiles(ph, srcs, k0, G, dst, sem, deps, N2):
    t = None
    for si, s in enumerate(srcs):
        t = ph.dma("sp", dst[:, si, :, :], s[:, k0:k0 + G, :], sem, deps=deps)
    return t


def phase_filter(P, l, L):
    nc, d, g = P.nc, P.d, P.g
    N1, N2 = fft_dims(L)
    key = f"{L}"
    if ("FIL" + key) not in d:
        P.scr("FIL" + key, [3, L, HW], BF16)
        for o in range(2):
            for ri in range(2):
                P.scr(f"AF{key}_{o}{ri}", [N2, N1, HW], BF16)
        for ri in range(2):
            P.scr(f"H{key}_{ri}", [N2, N1, HW], BF16)
    FIL = d["FIL" + key]
    AFd = [[d[f"AF{key}_{o}{ri}"] for ri in range(2)] for o in range(2)]
    Hd = [d[f"H{key}_{ri}"] for ri in range(2)]

    ph = Phase(nc, f"fm{l}_{L}")
    w1 = ph.sb("w1", [33, 64], F32)
    w2 = ph.sb("w2", [64, 2, 64], F32)
    wo = ph.sb("wo", [64, 1024], F32)
    fr = ph.sb("fr", [64, 1], F32)
    bb = ph.sb("bb", [64, 3], F32)
    fb = ph.sb("fb", [64, 3], F32)
    zt = ph.sb("zt", [33, L], F32)
    ld = ph.dsem("ld")
    ph.dma("sp", w1[:], d["hy_f_w1"][l], ld)
    for i in range(2):
        ph.dma("sp", w2[:, i, :], d["hy_f_w2"][l, i], ld)
    ph.dma("sp", wo[:], d["hy_f_wout"][l], ld)
    ph.dma("sp", fr[:], d["hy_f_freq"][l:l + 1, :].rearrange("o f -> f o"), ld, slow=True)
    ph.dma("sp", bb[:, 0:1], d["hy_f_b1"][l:l + 1, :].rearrange("o f -> f o"), ld, slow=True)
    for i in range(2):
        ph.dma("sp", bb[:, 1 + i:2 + i], d["hy_f_b2"][l, i:i + 1, :].rearrange("o f -> f o"), ld, slow=True)
    t_ld = ph.dma("sp", zt[:], d[f"c_zT{L}"][:], ld)
    t_fb = ph.op("dve", lambda e: e.tensor_scalar(out=fb[:], in0=bb[:], scalar1=fr[:, 0:1], scalar2=None, op0=ALU.mult),
                 deps=[t_ld])
    pa = ph.ps("pa", [64, 4, 512], F32)
    pf = ph.ps("pf", [128, 4, 512], F32)
    ya = [[ph.sb(f"ya{p}_{i}", [64, 512], F32) for i in range(2)] for p in range(2)]
    aa = [[ph.sb(f"aa{p}_{i}", [64, 512], F32) for i in range(2)] for p in range(2)]
    m1 = [ph.sb(f"m1_{p}", [64, 512], F32) for p in range(2)]
    m2 = [ph.sb(f"m2_{p}", [64, 512], F32) for p in range(2)]
    NBW = 2
    wn = [ph.sb(f"wn{b}", [128, 2, 512], F32) for b in range(NBW)]
    wsem = [ph.dsem(f"ws{b}") for b in range(NBW)]
    fo = [ph.sb(f"fo{b}", [128, 3, 512], BF16) for b in range(NBW)]
    fsem = [ph.dsem(f"fs{b}") for b in range(NBW)]
    wn_free = [None] * NBW
    fo_free = [None] * NBW
    pa_free = [None] * 4
    pf_free = [None] * 4
    a_read = [[None, None], [None, None]]
    ya_read = [[None, None], [None, None]]
    m_read = [None, None]
    ntile = L // 512
    cnt = {"sub": 0, "pfc": 0}

    def tile_stages(j, par):
        cols = slice(j * 512, (j + 1) * 512)
        stt = {"prev_a": None}

        def layer_stage(layer):
            def run():
                pi = par * 2 + layer % 2
                if layer == 0:
                    tm = ph.op("pe", lambda e: e.matmul(pa[:, pi, :], lhsT=w1[:], rhs=zt[:, cols], start=True, stop=True),
                               deps=[t_ld, pa_free[pi]])
                else:
                    src = aa[par][(layer - 1) % 2]
                    tm = ph.op("pe", lambda e: e.matmul(pa[:, pi, :], lhsT=w2[:, layer - 1, :], rhs=src[:], start=True, stop=True),
                               deps=[t_ld, pa_free[pi], stt["prev_a"]])
                    a_read[par][(layer - 1) % 2] = tm
                yb = ya[par][layer % 2]
                t1 = ph.op("dve", lambda e: e.tensor_scalar(out=yb[:], in0=pa[:, pi, :], scalar1=fr[:, 0:1],
                                                            scalar2=fb[:, layer:layer + 1], op0=ALU.mult, op1=ALU.add),
                           deps=[tm, t_fb, ya_read[par][layer % 2]])
                pa_free[pi] = t1
                ta = ph.op("dve", lambda e: e.tensor_scalar(out=m1[par][:], in0=yb[:], scalar1=-PI, scalar2=2 * PI, op0=ALU.is_lt, op1=ALU.mult),
                           deps=[t1, m_read[par]])
                tb = ph.op("dve", lambda e: e.tensor_scalar(out=m2[par][:], in0=yb[:], scalar1=PI, scalar2=-2 * PI, op0=ALU.is_gt, op1=ALU.mult),
                           deps=[t1])
                tc = ph.op("dve", lambda e: e.tensor_tensor(out=yb[:], in0=yb[:], in1=m1[par][:], op=ALU.add), deps=[ta, tb])
                t2 = ph.op("dve", lambda e: e.tensor_tensor(out=yb[:], in0=yb[:], in1=m2[par][:], op=ALU.add), deps=[tc])
                m_read[par] = t2
                ab = aa[par][layer % 2]
                stt["prev_a"] = ph.op("act", lambda e: e.activation(out=ab[:], in_=yb[:], func=AF.Sin),
                                      deps=[t2, a_read[par][layer % 2]])
                ya_read[par][layer % 2] = stt["prev_a"]
            return run

        def tm_stage(s):
            def run():
                a3 = aa[par][0]
                b = cnt["sub"] % NBW
                cnt["sub"] += 1
                t0 = j * 512 + s * 128
                t_w = ph.dma("sp", wn[b][:], d[f"c_win{L}"][0:2, t0:t0 + 128, :].rearrange("w t c -> t w c"), wsem[b], deps=[wn_free[b]])
                outs = []
                for half in range(2):
                    slot = cnt["pfc"] % 4
                    cnt["pfc"] += 1
                    tm = ph.op("pe", lambda e, slot=slot, half=half: e.matmul(pf[:, slot, :], lhsT=a3[:, s * 128:(s + 1) * 128],
                                                                              rhs=wo[:, half * 512:(half + 1) * 512], start=True, stop=True),
                               deps=[stt["prev_a"], pf_free[slot]])
                    a_read[par][0] = tm
                    te = ph.op("dve", lambda e, slot=slot, half=half: e.tensor_tensor(out=fo[b][:, half, :], in0=pf[:, slot, :],
                                                                                      in1=wn[b][:, half, :], op=ALU.mult),
                               deps=[tm, t_w, fo_free[b]])
                    pf_free[slot] = te
                    outs.append(te)
                wn_free[b] = outs[-1]
                tn = ph.op("act", lambda e: e.activation(out=fo[b][:, 2, :], in_=fo[b][:, 1, :], func=AF.Copy, scale=-1.0),
                           deps=[outs[-1], fo_free[b]])
                fo_free[b] = ph.dma("sp", FIL[:, t0:t0 + 128, :].rearrange("w t c -> t w c"), fo[b][:], fsem[b], deps=[outs[0], tn])
            return run

        return [layer_stage(0), layer_stage(1), layer_stage(2)] + [tm_stage(s) for s in range(4)]

    for j0 in range(0, ntile, 2):
        lists = [tile_stages(j, j - j0) for j in range(j0, min(j0 + 2, ntile))]
        for k in range(7):
            for lst in lists:
                lst[k]()
    ph.run()

    ph = Phase(nc, f"ff1{l}_{L}")
    fft_stage1(ph, P, L, [FIL[0], FIL[1], FIL[2]], [[0, 1], [0, 2]], AFd, "a")
    ph.run()

    ph = Phase(nc, f"ff2{l}_{L}")
    f2, t_f2 = fft_stage2_tables(ph, P, L, "b")
    hb = ph.sb("hb", [128, HW], F32)
    lb = ph.dsem("lb")
    t_hb = ph.dma("sp", hb[:], bcast_rows(d["hy_bias"][l:l + 1, :], 128), lb)
    G = FG
    NB = 2
    xin = [ph.sb(f"xin{b}", [N2, 4, G, HW], BF16) for b in range(NB)]
    xsem = [ph.dsem(f"xs{b}") for b in range(NB)]
    hst = [ph.sb(f"hst{b}", [N2, 2, G, HW], BF16) for b in range(NB)]
    hsem = [ph.dsem(f"hs{b}") for b in range(NB)]
    pp = ph.ps("pp", [N2, 4, HW], F32)
    x_free = [None] * NB
    h_free = [None] * NB
    pp_free = [None] * 4
    x_ld = [None] * NB
    srcs = [AFd[0][0], AFd[0][1], AFd[1][0], AFd[1][1]]
    ngrp = N1 // G
    x_ld[0] = load_ktiles(ph, srcs, 0, G, xin[0], xsem[0], [], N2)
    cnt = 0
    for gi in range(ngrp):
        b = gi % NB
        if gi + 1 < ngrp:
            nb_ = (gi + 1) % NB
            x_ld[nb_] = load_ktiles(ph, srcs, (gi + 1) * G, G, xin[nb_], xsem[nb_], [x_free[nb_]], N2)
        evs = []
        last = None
        for kl in range(G):
            for ri, (ia, ib, tb) in enumerate([(0, 1, 1), (3, 2, 2)]):
                slot = cnt % 4
                cnt += 1
                ph.op("pe", lambda e, slot=slot, b=b, ia=ia, kl=kl: e.matmul(pp[:, slot, :], lhsT=f2[:, 0, :], rhs=xin[b][:, ia, kl, :],
                                                                           start=True, stop=False),
                      deps=[t_f2, x_ld[b], pp_free[slot]])
                last = ph.op("pe", lambda e, slot=slot, b=b, ib=ib, kl=kl, tb=tb: e.matmul(pp[:, slot, :], lhsT=f2[:, tb, :], rhs=xin[b][:, ib, kl, :],
                                                                                         start=False, stop=True))
                if ri == 0:
                    ev = ph.op("dve", lambda e, slot=slot, b=b, kl=kl: e.tensor_tensor(out=hst[b][:, 0, kl, :], in0=pp[:, slot, :], in1=hb[0:N2, :], op=ALU.add),
                               deps=[last, t_hb, h_free[b]])
                else:
                    ev = ph.op("act", lambda e, slot=slot, b=b, kl=kl: e.activation(out=hst[b][:, 1, kl, :], in_=pp[:, slot, :], func=AF.Copy),
                               deps=[last, h_free[b]])
                pp_free[slot] = ev
                evs.append(ev)
        x_free[b] = last
        t = None
        for ri in range(2):
            t = ph.dma("sp", Hd[ri][:, gi * G:(gi + 1) * G, :], hst[b][:, ri, :, :], hsem[b], deps=evs)
        h_free[b] = t
    ph.run()


def ensure_act_scratch(P):
    d, T = P.d, P.T
    if "UH" in d:
        return
    P.scr("UH", [2048, T], BF16)
    P.scr("GQ", [512, T], BF16)
    P.scr("GK2", [2, 128, T], BF16)
    P.scr("GV", [2, T, 64], BF16)
    P.scr("GS", [512, T], BF16)
    P.scr("DQ", [512, T], BF16)
    P.scr("DK", [512, T], BF16)
    P.scr("DV", [T, 512], BF16)
    P.scr("DS", [512, T], BF16)
    P.scr("HT", [1024, T], BF16)
    P.scr("XR", [T, D], F32)
    P.scr("HV", [T, HW], BF16)
    P.scr("GG", [512, T], BF16)
    P.scr("YC", [512, T], BF16)
    P.scr("YG", [512, T], BF16)
    P.scr("YD", [512, T], BF16)


def seq_of(P, t0):
    for i, o in enumerate(P.offs):
        if o <= t0 < o + SEQS[i]:
            return i, t0 - o
    raise ValueError


def phase_A(P, l):
    nc, d, g = P.nc, P.d, P.g
    T = P.T
    ensure_act_scratch(P)
    xsrc = d["x"] if l == 0 else d["XR"]
    ph = Phase(nc, f"A{l}")
    Wb = ph.sb("Wb", [128, 8, NCOL], BF16)
    gam = ph.sb("gam", [128, D], F32)
    qg = ph.sb("qg", [128, 2], F32)
    wl = P.gsem[0]
    t_w = None
    for k in range(8):
        t_w = ph.dma("pool", Wb[:, k, :], d["w_in"][l, k * 128:(k + 1) * 128, :], wl)
    cl = ph.dsem("cl")
    ph.dma("sp", gam[:], bcast_rows(d["norm_g"][l:l + 1, :], 128), cl)
    for hh in range(2):
        ph.dma("sp", qg[hh * 64:(hh + 1) * 64, 0:1], d["q_norm_g"][l:l + 1, :].rearrange("o f -> f o"), cl, slow=True)
        t_c = ph.dma("sp", qg[hh * 64:(hh + 1) * 64, 1:2], d["k_norm_g"][l:l + 1, :].rearrange("o f -> f o"), cl, slow=True)

    NXB = 3
    xb = [ph.sb(f"xb{i}", [128, D], F32) for i in range(NXB)]
    xsem = [ph.dsem(f"xs{i}") for i in range(NXB)]
    xb_free = [None] * NXB
    hb = [ph.sb(f"hb{i}", [128, D], BF16) for i in range(4)]
    hb_free = [None] * 4
    hb_rdy = [None] * 4
    junk = ph.sb("junk", [128, D], F32)
    ssq = ph.sb("ssq", [128, 4], F32)
    hT = [ph.sb(f"hT{i}", [128, 8, 512], BF16) for i in range(2)]
    hT_free = [None] * 2
    hT_st = [None] * 2
    hT_rdy = [None] * 2
    htsem = [ph.dsem(f"hts{i}") for i in range(2)]
    cs = [ph.sb(f"cs{i}", [128, 2, 512], F32) for i in range(2)]
    cssem = [ph.dsem(f"css{i}") for i in range(2)]
    cs_free = [None] * 2
    cs_ld = [None] * 2
    tp = ph.ps("tp", [128, 2, 8, 128], BF16)
    tp_free = [None] * 2
    NS = 6
    pg = ph.ps("pg", [128, NS, 512], F32)
    slot_free = [None] * NS
    st = {"slot": 0, "sub": 0, "og": 0, "tpi": 0, "qs": 0}
    NOG = 5
    og = [ph.sb(f"og{i}", [128, 4, 512], BF16) for i in range(NOG)]
    ogsem = [ph.dsem(f"ogs{i}") for i in range(NOG)]
    og_free = [None] * NOG
    vs = [ph.sb(f"vs{i}", [128, 640], BF16) for i in range(2)]
    vssem = [ph.dsem(f"vss{i}") for i in range(2)]
    vs_free = [None] * 2
    sqb = [ph.sb(f"sqb{i}", [128, 512], BF16) for i in range(2)]
    rt = [ph.sb(f"rt{i}", [128, 512], F32) for i in range(2)]
    qn = [ph.sb(f"qn{i}", [128, 512], BF16) for i in range(2)]
    t1b = [ph.sb(f"t1b{i}", [128, 512], F32) for i in range(2)]
    t2b = [ph.sb(f"t2b{i}", [128, 512], F32) for i in range(2)]
    lastuse = [dict() for _ in range(2)]
    ssm = ph.sb("ssm", [128, 4], F32)
    rsd = ph.sb("rsd", [128, 4], F32)
    ss_free = [None] * 4

    def getslot():
        s = st["slot"] % NS
        st["slot"] += 1
        return s

    ntile = T // 512

    def norm_part1(tt):
        b = tt % 2
        t0 = tt * 512
        si, pos = seq_of(P, t0)
        cs_ld[b] = None
        ph.dma("sp", cs[b][:, 0, :], d["c_ropec"][:, pos:pos + 512], cssem[b], deps=[cs_free[b]])
        cs_ld[b] = ph.dma("sp", cs[b][:, 1, :], d["c_ropes"][:, pos:pos + 512], cssem[b], deps=[cs_free[b]])
        for s in range(4):
            i = st["sub"] % NXB
            st["sub"] += 1
            j = s
            q4 = s
            r0 = t0 + s * 128
            t_x = ph.dma("sp", xb[i][:], xsrc[r0:r0 + 128, :], xsem[i], deps=[xb_free[i]])
            t_ss = ph.op("act", lambda e, i=i, q4=q4: e.activation(out=junk[:], in_=xb[i][:], func=AF.Square,
                                                                  accum_out=ssq[:, q4:q4 + 1]), deps=[t_x, ss_free[q4]])
            t_sd = ph.op("act", lambda e, q4=q4: e.activation(out=ssm[:, q4:q4 + 1], in_=ssq[:, q4:q4 + 1], func=AF.Sqrt,
                                                             bias=g["eps"][:], scale=1.0 / D), deps=[t_ss])
            t_r = ph.op("dve", lambda e, q4=q4: e.reciprocal(out=rsd[:, q4:q4 + 1], in_=ssm[:, q4:q4 + 1]), deps=[t_sd])
            t_h = ph.op("dve", lambda e, i=i, j=j, q4=q4: e.scalar_tensor_tensor(
                out=hb[j][:], in0=xb[i][:], scalar=rsd[:, q4:q4 + 1], in1=gam[:], op0=ALU.mult, op1=ALU.mult),
                deps=[t_r, t_c, hb_free[j]])
            ss_free[q4] = t_h
            xb_free[i] = t_h
            hb_rdy[j] = t_h

    def norm_part2(tt):
        b = tt % 2
        t0 = tt * 512
        evs = []
        for s in range(4):
            j = s
            tpi = st["tpi"] % 2
            st["tpi"] += 1
            last = None
            for k in range(8):
                last = ph.op("pe", lambda e, tpi=tpi, k=k, j=j: e.transpose(out=tp[:, tpi, k, :], in_=hb[j][:, k * 128:(k + 1) * 128],
                                                                           identity=g["ident"][:]),
                             deps=[hb_rdy[j], tp_free[tpi]])
            hb_free[j] = last
            ev = ph.op("act", lambda e, tpi=tpi, b=b, s=s: e.activation(out=hT[b][:, :, s * 128:(s + 1) * 128], in_=tp[:, tpi, :, :], func=AF.Copy),
                       deps=[last, hT_free[b], hT_st[b]])
            tp_free[tpi] = ev
            evs.append(ev)
        hT_rdy[b] = evs[-1]
        hT_st[b] = ph.dma("sp", d["HT"][:, t0:t0 + 512].rearrange("(k p) t -> p k t", p=128), hT[b][:], htsem[b], deps=evs)

    def grp4(c0, kind, name):
        return [(c0 + j, kind, name, j) for j in range(4)]
    plan = []
    plan += [(16, "qk", "GQ", 0)] + grp4(0, "copy", "UH0") + [(20, "qk", "GK", 0)] + grp4(4, "copy", "UH1")
    plan += [(17, "qk", "GQ", 1)] + grp4(8, "copy", "UH2") + grp4(26, "copy", "DQ")
    plan += [(18, "qk", "GQ", 2)] + grp4(30, "copy", "DK") + grp4(12, "silu", "UH3")
    plan += [(19, "qk", "GQ", 3)] + grp4(22, "silu", "GS") + grp4(38, "silu", "DS")
    dests = {"UH0": (d["UH"], 0), "UH1": (d["UH"], 512), "UH2": (d["UH"], 1024), "UH3": (d["UH"], 1536),
             "DQ": (d["DQ"], 0), "DK": (d["DK"], 0), "GS": (d["GS"], 0), "DS": (d["DS"], 0), "GQ": (d["GQ"], 0)}

    norm_part1(0)
    norm_part2(0)
    for tt in range(ntile):
        b = tt % 2
        t0 = tt * 512
        cur = {}
        gevs = {}
        last_pe = None
        deferred = []
        for ci, (c, kind, grp, j) in enumerate(plan):
            if ci == 5 and tt + 1 < ntile:
                norm_part1(tt + 1)
            if ci == 24 and tt + 1 < ntile:
                norm_part2(tt + 1)
            while deferred and deferred[0][0] <= ci:
                deferred.pop(0)[1]()
            if grp not in cur:
                if grp == "GQ":
                    cur[grp] = 3
                elif grp == "GK":
                    cur[grp] = 4
                else:
                    cur[grp] = st["og"] % 3
                    st["og"] += 1
                gevs[grp] = []
            cur_og = cur[grp]
            grp_evs = gevs[grp]
            su = getslot()
            for k in range(8):
                last_pe = ph.op("pe", lambda e, su=su, k=k, c=c, b=b: e.matmul(pg[:, su, :], lhsT=Wb[:, k, c * 128:(c + 1) * 128], rhs=hT[b][:, k, :],
                                                                             start=(k == 0), stop=(k == 7)),
                                deps=[t_w, hT_rdy[b], slot_free[su]])
            mm = last_pe
            o = cur_og
            if kind == "copy":
                ev = ph.op("act", lambda e, su=su, o=o, j=j: e.activation(out=og[o][:, j, :], in_=pg[:, su, :], func=AF.Copy),
                           deps=[mm, og_free[o]])
                slot_free[su] = ev
            elif kind == "silu":
                ev = ph.op("act", lambda e, su=su, o=o, j=j: e.activation(out=og[o][:, j, :], in_=pg[:, su, :], func=AF.Silu),
                           deps=[mm, og_free[o]])
                slot_free[su] = ev
            else:
                q = st["qs"] % 2
                st["qs"] += 1
                lu = lastuse[q]
                gcol = 0 if grp == "GQ" else 1
                t_sq = ph.op("act", lambda e, su=su, q=q: e.activation(out=sqb[q][:], in_=pg[:, su, :], func=AF.Square),
                             deps=[mm, lu.get("sqb")])
                box = {}

                def stepA(su=su, q=q, lu=lu, gcol=gcol, t_sq=t_sq, box=box):
                    sm = getslot()
                    t_ms = ph.op("pe", lambda e: e.matmul(pg[:, sm, :], lhsT=g["onesblk"][:], rhs=sqb[q][:], start=True, stop=True),
                                 deps=[t_sq, slot_free[sm]])
                    lu["sqb"] = t_ms
                    t_sd = ph.op("act", lambda e: e.activation(out=rt[q][:], in_=pg[:, sm, :], func=AF.Sqrt, bias=g["eps"][:], scale=1.0),
                                 deps=[t_ms, lu.get("rt")])
                    slot_free[sm] = t_sd
                    t_rs = ph.op("dve", lambda e: e.reciprocal(out=rt[q][:], in_=rt[q][:]), deps=[t_sd])
                    t_qn = ph.op("dve", lambda e: e.scalar_tensor_tensor(
                        out=qn[q][:], in0=pg[:, su, :], scalar=qg[:, gcol:gcol + 1], in1=rt[q][:], op0=ALU.mult, op1=ALU.mult),
                        deps=[t_rs, t_c, lu.get("qn")])
                    slot_free[su] = t_qn
                    lu["rt"] = t_qn
                    box["t_qn"] = t_qn

                def stepB(q=q, lu=lu, b=b, o=o, j=j, grp=grp, box=box, grp_evs=grp_evs, t0=t0):
                    t_qn = box["t_qn"]
                    sw = getslot()
                    t_sw = ph.op("pe", lambda e: e.matmul(pg[:, sw, :], lhsT=g["pswap"][:], rhs=qn[q][:], start=True, stop=True),
                                 deps=[t_qn, slot_free[sw]])
                    t_1 = ph.op("dve", lambda e: e.tensor_tensor(out=t1b[q][:], in0=qn[q][:], in1=cs[b][:, 0, :], op=ALU.mult),
                                deps=[t_qn, cs_ld[b], lu.get("t1b")])
                    t_2 = ph.op("dve", lambda e: e.tensor_tensor(out=t2b[q][:], in0=pg[:, sw, :], in1=cs[b][:, 1, :], op=ALU.mult),
                                deps=[t_sw, cs_ld[b], lu.get("t2b")])
                    slot_free[sw] = t_2
                    lu["qn"] = t_2
                    cs_free[b] = t_2
                    ev = ph.op("pool", lambda e: e.tensor_tensor(out=og[o][:, j, :], in0=t1b[q][:], in1=t2b[q][:], op=ALU.add),
                               deps=[t_1, t_2, og_free[o]])
                    lu["t1b"] = ev
                    lu["t2b"] = ev
                    grp_evs.append(ev)
                    if grp == "GK":
                        tk = None
                        for kv in range(2):
                            for dup in range(2):
                                tk = ph.dma("sp", d["GK2"][kv, dup * 64:(dup + 1) * 64, t0:t0 + 512], og[o][kv * 64:(kv + 1) * 64, 0, :], ogsem[o], deps=grp_evs)
                        og_free[o] = tk
                    elif j == 3:
                        dst, r0 = dests[grp]
                        og_free[o] = ph.dma("sp", dst[r0:r0 + 512, t0:t0 + 512].rearrange("(j p) t -> p j t", p=128), og[o][:], ogsem[o], deps=grp_evs)

                deferred.append((ci + 2, stepA))
                deferred.append((ci + 6, stepB))
                deferred.sort(key=lambda x: x[0])
                continue
            grp_evs.append(ev)
            if grp == "GK":
                tk = None
                for kv in range(2):
                    for dup in range(2):
                        tk = ph.dma("sp", d["GK2"][kv, dup * 64:(dup + 1) * 64, t0:t0 + 512], og[o][kv * 64:(kv + 1) * 64, 0, :], ogsem[o], deps=grp_evs)
                og_free[o] = tk
            elif j == 3:
                dst, r0 = dests[grp]
                og_free[o] = ph.dma("sp", dst[r0:r0 + 512, t0:t0 + 512].rearrange("(j p) t -> p j t", p=128), og[o][:], ogsem[o], deps=grp_evs)
        while deferred and deferred[0][0] <= len(plan) + 1:
            deferred.pop(0)[1]()
        for s in range(4):
            vb = (tt * 4 + s) % 2
            s1 = getslot()
            for k in range(8):
                last_pe = ph.op("pe", lambda e, s1=s1, k=k, s=s, b=b: e.matmul(pg[:, s1, 0:128], lhsT=hT[b][:, k, s * 128:(s + 1) * 128], rhs=Wb[:, k, 2688:2816],
                                                                             start=(k == 0), stop=(k == 7)),
                                deps=[t_w, hT_rdy[b], slot_free[s1]])
            e1 = ph.op("act", lambda e, s1=s1, vb=vb: e.activation(out=vs[vb][:, 0:128], in_=pg[:, s1, 0:128], func=AF.Copy),
                       deps=[last_pe, vs_free[vb]])
            slot_free[s1] = e1
            s2 = getslot()
            for k in range(8):
                last_pe = ph.op("pe", lambda e, s2=s2, k=k, s=s, b=b: e.matmul(pg[:, s2, :], lhsT=hT[b][:, k, s * 128:(s + 1) * 128], rhs=Wb[:, k, 4352:4864],
                                                                             start=(k == 0), stop=(k == 7)),
                                deps=[slot_free[s2]])
            e2 = ph.op("dve", lambda e, s2=s2, vb=vb: e.tensor_copy(out=vs[vb][:, 128:640], in_=pg[:, s2, :]),
                       deps=[last_pe, vs_free[vb]])
            slot_free[s2] = e2
            r0 = t0 + s * 128
            for kv in range(2):
                ph.dma("sp", d["GV"][kv, r0:r0 + 128, :], vs[vb][:, kv * 64:(kv + 1) * 64], vssem[vb], deps=[e1])
            vs_free[vb] = ph.dma("sp", d["DV"][r0:r0 + 128, :], vs[vb][:, 128:640], vssem[vb], deps=[e2])
        hT_free[b] = last_pe
        while deferred:
            deferred.pop(0)[1]()
    ph.run()


def phase_attn(P, l, mode):
    nc, d, g = P.nc, P.d, P.g
    isD = mode == "D"
    ph = Phase(nc, f"{mode}{l}")
    Lmax = max(SEQS)
    VW = 128 if isD else 64
    NKB = 2
    Kb = [ph.sb(f"K{i}", [128, Lmax], BF16) for i in range(NKB)]
    Vb = [ph.sb(f"V{i}", [128, Lmax // 128, VW], BF16) for i in range(NKB)]
    ksem = [ph.dsem(f"ks{i}") for i in range(NKB)]
    k_free = [None] * NKB
    k_ld = [None] * NKB
    NQB = 3
    Qb = [ph.sb(f"Q{i}", [128, 512], BF16) for i in range(NQB)]
    Gb = [ph.sb(f"Gt{i}", [128, 512], BF16) for i in range(NQB)]
    qsem = [ph.dsem(f"qs{i}") for i in range(NQB)]
    q_free = [None] * NQB
    q_ld = [None] * NQB
    NP = 4
    p_s = [ph.sb(f"p{i}", [128, 2, 512], BF16) for i in range(NP)]
    p_free = [None] * NP
    ps_s = ph.ps("s", [128, 2, 2, 512], F32)
    s_free = [None] * 2
    if isD:
        acc = ph.ps("acc", [128, 4, 512], F32)
        NACC = 1
        gsub = ph.sb("gsub", [128, 1], F32)
        gs0 = ph.sb("gs0", [128, 1], F32)
        cl = ph.dsem("cl")
        t_g0 = ph.dma("sp", gs0[:], d["diff_subln_g"][l:l + 1, :].rearrange("o f -> f o"), cl, slow=True)
        lam_init = 0.8 - 0.6 * math.exp(-0.3 * l)
        t_gs = ph.op("dve", lambda e: e.tensor_scalar(out=gsub[:], in0=gs0[:], scalar1=1.0 - lam_init, scalar2=None, op0=ALU.mult),
                     deps=[t_g0])
        sqd = [ph.sb(f"sqd{i}", [128, 512], BF16) for i in range(2)]
        dcp = [ph.sb(f"dcp{i}", [64, 512], F32) for i in range(2)]
        obA = [ph.sb(f"obA{i}", [128, 512], F32) for i in range(2)]
        obB = [ph.sb(f"obB{i}", [128, 512], F32) for i in range(2)]
        rbD = [ph.sb(f"rbD{i}", [128, 512], F32) for i in range(2)]
    else:
        acc = ph.ps("acc", [128, 2, 2, 512], F32)
        NACC = 2
    acc_free = [None] * NACC
    den_free = [None]
    rb1 = ph.sb("rb1", [128, 512], F32)
    ob1 = ph.sb("ob1", [128, 512], F32)
    NOS = 2
    ost = [ph.sb(f"ost{i}", [128, 512], BF16) for i in range(NOS)]
    osem = [ph.dsem(f"os{i}") for i in range(NOS)]
    o_free = [None] * NOS
    ones = g["ones1"]

    groups = []
    kvsets = []
    for si, L in enumerate(SEQS):
        nsets = 4 if isD else 2
        for a in range(nsets):
            kvsets.append((si, a))
            subs = [a] if isD else [2 * a, 2 * a + 1]
            for hp in subs:
                for qj in range(L // 512):
                    groups.append(dict(si=si, a=a, hp=hp, qj=qj, L=L, off=P.offs[si], ks=len(kvsets) - 1))
    Ksrc = d["DK"] if isD else None
    Qsrc = d["DQ"] if isD else d["GQ"]
    Ssrc = d["DS"] if isD else d["GS"]
    Ydst = d["YD"] if isD else d["YG"]

    def load_kv(ksi):
        si, a = kvsets[ksi]
        L, off = SEQS[si], P.offs[si]
        b = ksi % NKB
        if isD:
            ph.dma("sp", Kb[b][:, 0:L], d["DK"][a * 128:(a + 1) * 128, off:off + L], ksem[b], deps=[k_free[b]])
            src = d["DV"][off:off + L, a * 128:(a + 1) * 128].rearrange("(c p) e -> p c e", p=128)
        else:
            ph.dma("sp", Kb[b][:, 0:L], d["GK2"][a, :, off:off + L], ksem[b], deps=[k_free[b]])
            src = d["GV"][a, off:off + L, :].rearrange("(c p) e -> p c e", p=128)
        k_ld[b] = ph.dma("sp", Vb[b][:, 0:L // 128, :], src, ksem[b], deps=[k_free[b]])

    def load_q(gi):
        grp = groups[gi]
        b = gi % NQB
        r0 = grp["hp"] * 128
        c0 = grp["off"] + grp["qj"] * 512
        ph.dma("sp", Qb[b][:], Qsrc[r0:r0 + 128, c0:c0 + 512], qsem[b], deps=[q_free[b]])
        q_ld[b] = ph.dma("sp", Gb[b][:], Ssrc[r0:r0 + 128, c0:c0 + 512], qsem[b], deps=[q_free[b]])

    tiles = []
    for gi, grp in enumerate(groups):
        nkc = grp["L"] // 128
        for kc in range(nkc):
            tiles.append((gi, kc, nkc))
    nt = len(tiles)
    qk_tok = [None] * nt
    exp_tok = [None] * nt
    state = {"last_av": None}

    def emit_qk(i):
        gi, kc, nkc = tiles[i]
        grp = groups[gi]
        if kc == 0:
            flush_group(gi - NQB)
        kb = grp["ks"] % NKB
        qb = gi % NQB
        sb_ = i % 2
        near = False
        if isD:
            o = 128 * kc - 512 * grp["qj"]
            near = -256 < o < 640
        ph.op("pe", lambda e: e.matmul(ps_s[:, sb_, 0, :], lhsT=Kb[kb][0:64, kc * 128:(kc + 1) * 128], rhs=Qb[qb][0:64, :],
                                       start=True, stop=not near, tile_position=(0, 0)),
              deps=[k_ld[kb], q_ld[qb], s_free[sb_]])
        qk_tok[i] = ph.op("pe", lambda e: e.matmul(ps_s[:, sb_, 1, :], lhsT=Kb[kb][64:128, kc * 128:(kc + 1) * 128], rhs=Qb[qb][64:128, :],
                                                   start=True, stop=not near, tile_position=(64, 0)))
        if near:
            brhs = g["wstrip"][:, grp["a"], 512 - o:1024 - o]
            ph.op("pe", lambda e: e.matmul(ps_s[:, sb_, 0, :], lhsT=g["ident"][:], rhs=brhs, start=False, stop=True))
            qk_tok[i] = ph.op("pe", lambda e: e.matmul(ps_s[:, sb_, 1, :], lhsT=g["ident"][:], rhs=brhs, start=False, stop=True))

    def emit_exp(i):
        gi, kc, nkc = tiles[i]
        grp = groups[gi]
        sb_ = i % 2
        pb = i % NP
        if isD:
            h = grp["a"]
            o = 128 * kc - 512 * grp["qj"]
            if o <= -256 or o >= 640:
                side = 0 if o <= -256 else 1
                exp_tok[i] = ph.op("act", lambda e: e.activation(out=p_s[pb][:], in_=ps_s[:, sb_, :, :], func=AF.Exp,
                                                                bias=g["farb"][:, h, side:side + 1], scale=0.125),
                                   deps=[qk_tok[i], p_free[pb]])
                s_free[sb_] = exp_tok[i]
            else:
                exp_tok[i] = ph.op("act", lambda e: e.activation(out=p_s[pb][:], in_=ps_s[:, sb_, :, :], func=AF.Exp, scale=0.125),
                                   deps=[qk_tok[i], p_free[pb]])
                s_free[sb_] = exp_tok[i]
        else:
            exp_tok[i] = ph.op("act", lambda e: e.activation(out=p_s[pb][:], in_=ps_s[:, sb_, :, :], func=AF.Exp, scale=0.125),
                               deps=[qk_tok[i], p_free[pb]])
            s_free[sb_] = exp_tok[i]

    def emit_den(i):
        gi, kc, nkc = tiles[i]
        pb = i % NP
        first, last = kc == 0, kc == nkc - 1
        ph.op("pe", lambda e: e.matmul(acc[0:32, 2, :], lhsT=ones[:, 0:32], rhs=p_s[pb][:, 0, :], start=first, stop=last,
                                       tile_position=(0, 0)), deps=[exp_tok[i], den_free[0] if first else None])
        t = ph.op("pe", lambda e: e.matmul(acc[32:64, 2, :], lhsT=ones[:, 0:32], rhs=p_s[pb][:, 1, :], start=first, stop=last,
                                           tile_position=(0, 32)))
        p_free[pb] = t
        return t

    def emit_av(i):
        gi, kc, nkc = tiles[i]
        grp = groups[gi]
        kb = grp["ks"] % NKB
        pb = i % NP
        a_ = gi % NACC
        first, last = kc == 0, kc == nkc - 1
        deps = [exp_tok[i], acc_free[a_] if first else None]
        if isD:
            if i > 0 and not first:
                emit_den(i - 1)
            ph.op("pe", lambda e: e.matmul(acc[:, 0, :], lhsT=Vb[kb][:, kc, :], rhs=p_s[pb][:, 0, :], start=first, stop=last), deps=deps)
            t = ph.op("pe", lambda e: e.matmul(acc[:, 1, :], lhsT=Vb[kb][:, kc, :], rhs=p_s[pb][:, 1, :], start=first, stop=last))
            if last:
                t = emit_den(i)
        else:
            ph.op("pe", lambda e: e.matmul(acc[0:64, a_, 0, :], lhsT=Vb[kb][:, kc, :], rhs=p_s[pb][:, 0, :], start=first, stop=last,
                                           tile_position=(0, 0)), deps=deps)
            ph.op("pe", lambda e: e.matmul(acc[64:128, a_, 0, :], lhsT=Vb[kb][:, kc, :], rhs=p_s[pb][:, 1, :], start=first, stop=last,
                                           tile_position=(0, 64)))
            ph.op("pe", lambda e: e.matmul(acc[0:64, a_, 1, :], lhsT=ones[:, 0:64], rhs=p_s[pb][:, 0, :], start=first, stop=last,
                                           tile_position=(0, 0)))
            t = ph.op("pe", lambda e: e.matmul(acc[64:128, a_, 1, :], lhsT=ones[:, 0:64], rhs=p_s[pb][:, 1, :], start=first, stop=last,
                                               tile_position=(0, 64)))
        if not isD:
            p_free[pb] = t
        state["last_av"] = t
        return t

    pending = {}
    es_last = [None, None]
    sp_state = {"free": None}

    def flush_group(gq):
        for (_due, fn) in pending.pop(gq, []):
            fn()

    def run_due(i):
        for gq in sorted(pending.keys()):
            lst = pending[gq]
            while lst and lst[0][0] <= i:
                lst.pop(0)[1]()
            if not lst:
                pending.pop(gq)

    def finish_group(gi, t11):
        grp = groups[gi]
        qb = gi % NQB
        osl = gi % NOS
        r0 = grp["hp"] * 128
        c0 = grp["off"] + grp["qj"] * 512
        q_free[qb] = t11
        o_free[osl] = ph.dma("sp", Ydst[r0:r0 + 128, c0:c0 + 512], ost[osl][:], osem[osl], deps=[t11])
        if gi + NQB < len(groups):
            load_q(gi + NQB)

    def epilogue(gi, t_last, i_tile):
        grp = groups[gi]
        a_ = gi % NACC
        qb = gi % NQB
        osl = gi % NOS
        if isD:
            es_ = gi % 2
            flush_group(gi - 2)
            oA, oB, dc, sq_, rD = obA[es_], obB[es_], dcp[es_], sqd[es_], rbD[es_]
            c1 = ph.op("dve", lambda e: e.tensor_copy(out=oA[:], in_=acc[:, 0, :]), deps=[t_last, es_last[es_]])
            c2 = ph.op("dve", lambda e: e.tensor_copy(out=oB[:], in_=acc[:, 1, :]), deps=[t_last])
            c3 = ph.op("dve", lambda e: e.tensor_copy(out=dc[:], in_=acc[0:64, 2, :]), deps=[t_last])
            acc_free[a_] = c3
            den_free[0] = c3
            tr = ph.op("dve", lambda e: e.reciprocal(out=dc[:], in_=dc[:]), deps=[c3])
            stt = {}

            def step1():
                b1 = ph.op("pe", lambda e: e.matmul(acc[:, 3, :], lhsT=g["onesf"][0:1, :], rhs=dc[0:1, :], start=True, stop=True),
                           deps=[tr, sp_state["free"]])
                stt["o1"] = ph.op("dve", lambda e: e.tensor_tensor(out=oA[:], in0=oA[:], in1=acc[:, 3, :], op=ALU.mult), deps=[b1, c1])
                sp_state["free"] = stt["o1"]

            def step2():
                b2 = ph.op("pe", lambda e: e.matmul(acc[:, 3, :], lhsT=g["onesf"][32:33, :], rhs=dc[32:33, :], start=True, stop=True),
                           deps=[tr, sp_state["free"]])
                o2 = ph.op("dve", lambda e: e.tensor_tensor(out=oB[:], in0=oB[:], in1=acc[:, 3, :], op=ALU.mult), deps=[b2, c2])
                sp_state["free"] = o2
                t5 = ph.op("dve", lambda e: e.scalar_tensor_tensor(out=oA[:], in0=oB[:], scalar=g["lamt"][:, l, 0:1], in1=oA[:],
                                                                  op0=ALU.mult, op1=ALU.add), deps=[o2, stt["o1"]])
                stt["t5"] = t5
                stt["t6"] = ph.op("act", lambda e: e.activation(out=sq_[:], in_=oA[:], func=AF.Square), deps=[t5])

            def step3():
                t7 = ph.op("pe", lambda e: e.matmul(acc[:, 3, :], lhsT=g["ones128"][:], rhs=sq_[:], start=True, stop=True),
                           deps=[stt["t6"], sp_state["free"]])
                t8 = ph.op("act", lambda e: e.activation(out=rD[:], in_=acc[:, 3, :], func=AF.Sqrt, bias=g["eps"][:], scale=1.0), deps=[t7])
                sp_state["free"] = t8
                t9 = ph.op("dve", lambda e: e.reciprocal(out=rD[:], in_=rD[:]), deps=[t8])
                t10 = ph.op("dve", lambda e: e.scalar_tensor_tensor(out=oA[:], in0=oA[:], scalar=gsub[:, 0:1], in1=rD[:],
                                                                   op0=ALU.mult, op1=ALU.mult), deps=[t9, t_gs, stt["t5"]])
                t11 = ph.op("dve", lambda e: e.tensor_tensor(out=ost[osl][:], in0=oA[:], in1=Gb[qb][:], op=ALU.mult),
                            deps=[t10, q_ld[qb], o_free[osl]])
                es_last[es_] = t11
                finish_group(gi, t11)

            pending[gi] = [(i_tile + 5, step1), (i_tile + 7, step2), (i_tile + 10, step3)]
        else:
            t1 = ph.op("dve", lambda e: e.reciprocal(out=rb1[:], in_=acc[:, a_, 1, :]), deps=[t_last])
            t3 = ph.op("dve", lambda e: e.tensor_tensor(out=ob1[:], in0=acc[:, a_, 0, :], in1=rb1[:], op=ALU.mult), deps=[t1])
            acc_free[a_] = t3
            t11 = ph.op("dve", lambda e: e.tensor_tensor(out=ost[osl][:], in0=ob1[:], in1=Gb[qb][:], op=ALU.mult),
                        deps=[t3, q_ld[qb], o_free[osl]])
            finish_group(gi, t11)

    load_kv(0)
    for gq in range(min(NQB, len(groups))):
        load_q(gq)
    emit_qk(0)
    for i in range(nt):
        gi, kc, nkc = tiles[i]
        grp = groups[gi]
        if kc == 0:
            if (gi == 0 or groups[gi - 1]["ks"] != grp["ks"]) and grp["ks"] + 1 < len(kvsets):
                load_kv(grp["ks"] + 1)
        emit_exp(i)
        if i + 1 < nt:
            emit_qk(i + 1)
        t = emit_av(i)
        run_due(i)
        if kc == nkc - 1:
            if gi + 1 >= len(groups) or groups[gi + 1]["ks"] != grp["ks"]:
                k_free[grp["ks"] % NKB] = t
            epilogue(gi, t, i)
    for gq in sorted(pending.keys()):
        flush_group(gq)
    ph.run()


def phase_H1(P, l):
    nc, d, g = P.nc, P.d, P.g
    ph = Phase(nc, f"H1{l}")
    cw = ph.sb("cw", [128, 12, 4], F32)
    cl = ph.dsem("cl")
    for j in range(3):
        ph.dma("sp", cw[:, :, j:j + 1], d["hy_conv_w"][l, j:j + 1, :].rearrange("o (c p) -> p c o", p=128), cl, slow=True)
    t_cw = ph.dma("sp", cw[:, :, 3:4], d["hy_conv_b"][l:l + 1, :].rearrange("o (c p) -> p c o", p=128), cl, slow=True)
    BWmax = min(2048, max(SEQS))
    NB = 2
    U = [[ph.sb(f"U{b}_{j}", [128, BWmax + 2], BF16) for j in range(3)] for b in range(NB)]
    SG = [ph.sb(f"SG{b}", [128, BWmax], BF16) for b in range(NB)]
    usem = [ph.dsem(f"us{b}") for b in range(NB)]
    u_free = [None] * NB
    x1c = [ph.sb(f"x1c{i}", [128, 512], F32) for i in range(2)]
    x1c_free = [None] * 2
    hvb = ph.sb("hvb", [128, BWmax], BF16)
    gb = ph.sb("gb", [128, BWmax], BF16)
    gsem = ph.dsem("gs")
    hvT = ph.sb("hvT", [128, BWmax // 128, 128], BF16)
    hsem = ph.dsem("hs")
    tpp = ph.ps("tpp", [128, BWmax // 128, 128], BF16)
    pc = ph.ps("pc", [128, 2, 3, 512], F32)
    pc_free = [[None] * 3 for _ in range(2)]
    dg = ph.sb("dg", [128, 12, 3, 128], BF16)
    t_dg = None
    for ch in range(12):
        for j in range(3):
            t_dg = ph.op("dve", lambda e, ch=ch, j=j: e.tensor_scalar(out=dg[:, ch, j, :], in0=g["ident"][:], scalar1=cw[:, ch, j:j + 1],
                                                                     scalar2=None, op0=ALU.mult), deps=[t_cw])
    blocks = []
    for si, L in enumerate(SEQS):
        BW = min(2048, L)
        for cc in range(4):
            for c0 in range(0, L, BW):
                blocks.append((si, L, P.offs[si], cc, c0, BW))
    ld_tok = [None] * NB
    ms_tok = [None] * NB

    def load(bi):
        si, L, off, cc, c0, BW = blocks[bi]
        b = bi % NB
        lo = 1 if c0 == 0 else 0
        hi = BW + 1 if c0 + BW == L else BW + 2
        mt = None
        for j in range(3):
            row0 = (j * 4 + cc) * 128
            if lo == 1:
                mt = ph.op("pool", lambda e, b=b, j=j: e.memset(U[b][j][:, 0:1], 0.0), deps=[u_free[b]])
            if hi == BW + 1:
                mt = ph.op("pool", lambda e, b=b, j=j, BW=BW: e.memset(U[b][j][:, BW + 1:BW + 2], 0.0), deps=[u_free[b]])
            ph.dma("sp", U[b][j][:, lo:hi], d["UH"][row0:row0 + 128, off + c0 - 1 + lo:off + c0 - 1 + hi], usem[b], deps=[u_free[b]])
        row0 = (12 + cc) * 128
        ld_tok[b] = ph.dma("sp", SG[b][:, 0:BW], d["UH"][row0:row0 + 128, off + c0:off + c0 + BW], usem[b], deps=[u_free[b]])
        ms_tok[b] = mt

    g_st = None
    h_st = None
    tp_free = None
    load(0)
    for bi, (si, L, off, cc, c0, BW) in enumerate(blocks):
        b = bi % NB
        if bi + 1 < len(blocks):
            load(bi + 1)
        t_hv = None
        t_g = None
        for ct in range(BW // 512):
            pb_ = (bi * 4 + ct) % 2
            c0c = ct * 512
            mm = []
            for j in range(3):
                ch = j * 4 + cc
                for tap in range(3):
                    t = ph.op("pe", lambda e, pb_=pb_, j=j, ch=ch, tap=tap, b=b, c0c=c0c: e.matmul(
                        pc[:, pb_, j, :], lhsT=dg[:, ch, tap, :], rhs=U[b][j][:, c0c + tap:c0c + tap + 512], start=(tap == 0), stop=(tap == 2)),
                        deps=[t_dg, ld_tok[b], ms_tok[b], pc_free[pb_][j]])
                mm.append(t)
            xi = (bi * 4 + ct) % 2
            t_x1 = ph.op("act", lambda e, pb_=pb_, xi=xi, cc=cc: e.activation(out=x1c[xi][:], in_=pc[:, pb_, 1, :], func=AF.Identity,
                                                                             bias=cw[:, 4 + cc, 3:4], scale=1.0),
                         deps=[mm[1], x1c_free[xi]])
            pc_free[pb_][1] = t_x1
            t_hv = ph.op("dve", lambda e, pb_=pb_, xi=xi, cc=cc, c0c=c0c: e.scalar_tensor_tensor(
                out=hvb[:, c0c:c0c + 512], in0=pc[:, pb_, 2, :], scalar=cw[:, 8 + cc, 3:4], in1=x1c[xi][:], op0=ALU.add, op1=ALU.mult),
                deps=[mm[2], t_x1, tp_free])
            pc_free[pb_][2] = t_hv
            x1c_free[xi] = t_hv
            t_g = ph.op("dve", lambda e, pb_=pb_, cc=cc, b=b, c0c=c0c: e.scalar_tensor_tensor(
                out=gb[:, c0c:c0c + 512], in0=pc[:, pb_, 0, :], scalar=cw[:, cc, 3:4], in1=SG[b][:, c0c:c0c + 512], op0=ALU.add, op1=ALU.mult),
                deps=[mm[0], g_st])
            pc_free[pb_][0] = t_g
        u_free[b] = t_g
        g_st = ph.dma("sp", d["GG"][cc * 128:(cc + 1) * 128, off + c0:off + c0 + BW], gb[:, 0:BW], gsem, deps=[t_g])
        ns = BW // 128
        last = None
        for s in range(ns):
            last = ph.op("pe", lambda e, s=s: e.transpose(out=tpp[:, s, :], in_=hvb[:, s * 128:(s + 1) * 128], identity=g["ident"][:]),
                         deps=[t_hv, h_ev if s == 0 and bi > 0 else None])
        tp_free = last
        h_ev = ph.op("act", lambda e, ns=ns: e.activation(out=hvT[:, 0:ns, :], in_=tpp[:, 0:ns, :], func=AF.Copy), deps=[last, h_st])
        h_st = ph.dma("sp", d["HV"][off + c0:off + c0 + BW, cc * 128:(cc + 1) * 128].rearrange("(s p) c -> p s c", p=128),
                      hvT[:, 0:ns, :], hsem, deps=[h_ev])
    ph.run()


def phase_H2(P, l, si):
    nc, d, g = P.nc, P.d, P.g
    L, off = SEQS[si], P.offs[si]
    N1, N2 = fft_dims(L)
    H1 = N1 // 2
    key = f"{L}"
    if f"AD{key}_0" not in d:
        for ri in range(2):
            P.scr(f"AD{key}_{ri}", [N2, N1, HW], BF16)
            P.scr(f"DD{key}_{ri}", [N1, N2, HW], BF16)
    AD = [d[f"AD{key}_{ri}"] for ri in range(2)]
    DD = [d[f"DD{key}_{ri}"] for ri in range(2)]
    Hd = [d[f"H{key}_{ri}"] for ri in range(2)]

    ph = Phase(nc, f"h2a{l}_{si}")
    fft_stage1(ph, P, L, [d["HV"][off:off + L, :]], [[0]], [AD], "a")
    ph.run()

    ph = Phase(nc, f"h2b{l}_{si}")
    f2, t_f2 = fft_stage2_tables(ph, P, L, "b")
    G = FG
    NB = 2
    xin = [ph.sb(f"xin{b}", [N2, 2, G, HW], BF16) for b in range(NB)]
    hin = [ph.sb(f"hin{b}", [N2, 2, G, HW], BF16) for b in range(NB)]
    xsem = [ph.dsem(f"xs{b}") for b in range(NB)]
    dst = [ph.sb(f"dst{b}", [N2, 2, G, HW], BF16) for b in range(NB)]
    dsem_ = [ph.dsem(f"ds{b}") for b in range(NB)]
    NT = 2
    tt = [[ph.sb(f"t{q}_{j}", [N2, HW], F32) for j in range(4)] for q in range(NT)]
    yy = [ph.sb(f"y{q}", [N2, 2, HW], BF16) for q in range(NT)]
    t_free = [[None] * 4 for _ in range(NT)]
    y_free = [None] * NT
    NS = 8
    pp = ph.ps("pp", [N2, NS, HW], F32)
    pp_free = [None] * NS
    x_free = [None] * NB
    d_free = [None] * NB
    x_ld = [None] * NB
    st = {"slot": 0, "q": 0}

    def getslot():
        s = st["slot"] % NS
        st["slot"] += 1
        return s

    def load(gi):
        b = gi % NB
        k0 = gi * G
        for ri in range(2):
            ph.dma("sp", xin[b][:, ri, :, :], AD[ri][:, k0:k0 + G, :], xsem[b], deps=[x_free[b]])
        for ri in range(2):
            x_ld[b] = ph.dma("sp", hin[b][:, ri, :, :], Hd[ri][:, k0:k0 + G, :], xsem[b], deps=[x_free[b]])

    ngrp = N1 // G
    items = [(gi, kl) for gi in range(ngrp) for kl in range(G)]
    f2tok = {}

    def emit_f2(i):
        gi, kl = items[i]
        b = gi % NB
        sr, si_ = getslot(), getslot()
        ph.op("pe", lambda e: e.matmul(pp[:, sr, :], lhsT=f2[:, 0, :], rhs=xin[b][:, 0, kl, :], start=True, stop=False),
              deps=[t_f2, x_ld[b], pp_free[sr]])
        t_br = ph.op("pe", lambda e: e.matmul(pp[:, sr, :], lhsT=f2[:, 1, :], rhs=xin[b][:, 1, kl, :], start=False, stop=True))
        ph.op("pe", lambda e: e.matmul(pp[:, si_, :], lhsT=f2[:, 0, :], rhs=xin[b][:, 1, kl, :], start=True, stop=False),
              deps=[pp_free[si_]])
        t_bi = ph.op("pe", lambda e: e.matmul(pp[:, si_, :], lhsT=f2[:, 2, :], rhs=xin[b][:, 0, kl, :], start=False, stop=True))
        f2tok[i] = (sr, si_, t_br, t_bi)

    load(0)
    emit_f2(0)
    evs = []
    for i, (gi, kl) in enumerate(items):
        b = gi % NB
        if kl == 0:
            evs = []
            if gi + 1 < ngrp:
                load(gi + 1)
        if i + 1 < len(items):
            emit_f2(i + 1)
        sr, si_, t_br, t_bi = f2tok.pop(i)
        q = i % NT
        m1 = ph.op("dve", lambda e, q=q, sr=sr, b=b, kl=kl: e.tensor_tensor(out=tt[q][0][:], in0=pp[:, sr, :], in1=hin[b][:, 0, kl, :], op=ALU.mult),
                   deps=[t_br, x_ld[b], t_free[q][0]])
        m2 = ph.op("dve", lambda e, q=q, si_=si_, b=b, kl=kl: e.tensor_tensor(out=tt[q][1][:], in0=pp[:, si_, :], in1=hin[b][:, 1, kl, :], op=ALU.mult),
                   deps=[t_bi, t_free[q][1]])
        m3 = ph.op("dve", lambda e, q=q, sr=sr, b=b, kl=kl: e.tensor_tensor(out=tt[q][2][:], in0=pp[:, sr, :], in1=hin[b][:, 1, kl, :], op=ALU.mult),
                   deps=[t_free[q][2]])
        m4 = ph.op("dve", lambda e, q=q, si_=si_, b=b, kl=kl: e.tensor_tensor(out=tt[q][3][:], in0=pp[:, si_, :], in1=hin[b][:, 0, kl, :], op=ALU.mult),
                   deps=[t_free[q][3]])
        pp_free[sr] = m3
        pp_free[si_] = m4
        a1 = ph.op("pool", lambda e, q=q: e.tensor_tensor(out=yy[q][:, 0, :], in0=tt[q][0][:], in1=tt[q][1][:], op=ALU.subtract),
                   deps=[m1, m2, y_free[q]])
        a2 = ph.op("pool", lambda e, q=q: e.tensor_tensor(out=yy[q][:, 1, :], in0=tt[q][2][:], in1=tt[q][3][:], op=ALU.add),
                   deps=[m3, m4])
        t_free[q][0] = a1
        t_free[q][1] = a1
        t_free[q][2] = a2
        t_free[q][3] = a2
        dr, di = getslot(), getslot()
        ph.op("pe", lambda e, dr=dr, q=q: e.matmul(pp[:, dr, :], lhsT=f2[:, 0, :], rhs=yy[q][:, 0, :], start=True, stop=False),
              deps=[a1, a2, pp_free[dr]])
        t_dr = ph.op("pe", lambda e, dr=dr, q=q: e.matmul(pp[:, dr, :], lhsT=f2[:, 2, :], rhs=yy[q][:, 1, :], start=False, stop=True))
        ph.op("pe", lambda e, di=di, q=q: e.matmul(pp[:, di, :], lhsT=f2[:, 0, :], rhs=yy[q][:, 1, :], start=True, stop=False),
              deps=[pp_free[di]])
        t_di = ph.op("pe", lambda e, di=di, q=q: e.matmul(pp[:, di, :], lhsT=f2[:, 1, :], rhs=yy[q][:, 0, :], start=False, stop=True))
        y_free[q] = t_di
        e1 = ph.op("act", lambda e, dr=dr, b=b, kl=kl: e.activation(out=dst[b][:, 0, kl, :], in_=pp[:, dr, :], func=AF.Copy),
                   deps=[t_dr, d_free[b]])
        e2 = ph.op("act", lambda e, di=di, b=b, kl=kl: e.activation(out=dst[b][:, 1, kl, :], in_=pp[:, di, :], func=AF.Copy),
                   deps=[t_di, d_free[b]])
        pp_free[dr] = e1
        pp_free[di] = e2
        evs += [e1, e2]
        if kl == G - 1:
            x_free[b] = m4
            t = None
            for ri in range(2):
                t = ph.dma("sp", DD[ri][gi * G:(gi + 1) * G, :, :].rearrange("k n c -> n k c"), dst[b][:, ri, :, :], dsem_[b], deps=evs)
            d_free[b] = t
    ph.run()

    ph = Phase(nc, f"h2c{l}_{si}")
    ic = ph.sb("ic", [N1, N2, H1], BF16)
    isn = ph.sb("isn", [N1, N2, H1], BF16)
    ldt = ph.dsem("ldt")
    ph.dma("sp", ic[:], d[f"c_i2c{L}"][:], ldt)
    t_tab = ph.dma("sp", isn[:], d[f"c_i2s{L}"][:], ldt)
    yc = ph.sb("yc", [128, 4, L], BF16)
    dd = [ph.sb(f"dd{b}", [N1, 2, G, HW], BF16) for b in range(NB)]
    ddsem = [ph.dsem(f"dds{b}") for b in range(NB)]
    dd_free = [None] * NB
    dd_ld = [None] * NB
    NZ = 4
    pz = ph.ps("pz", [128, NZ, 512], F32)
    pz_free = [None] * NZ
    zc = 0

    def load2(gi):
        b = gi % NB
        for ri in range(2):
            dd_ld[b] = ph.dma("sp", dd[b][:, ri, :, :], DD[ri][:, gi * G:(gi + 1) * G, :], ddsem[b], deps=[dd_free[b]])

    ngrp = N2 // G
    load2(0)
    evs = []
    for gi in range(ngrp):
        b = gi % NB
        if gi + 1 < ngrp:
            load2(gi + 1)
        last = None
        for cc in range(4):
            z = zc % NZ
            zc += 1
            for n2l in range(G):
                n2 = gi * G + n2l
                ph.op("pe", lambda e, z=z, n2l=n2l, n2=n2, b=b, cc=cc: e.matmul(pz[:, z, n2l * H1:(n2l + 1) * H1], lhsT=dd[b][:, 0, n2l, cc * 128:(cc + 1) * 128],
                                                                              rhs=ic[:, n2, :], start=True, stop=False),
                      deps=[t_tab, dd_ld[b], pz_free[z]])
                last = ph.op("pe", lambda e, z=z, n2l=n2l, n2=n2, b=b, cc=cc: e.matmul(pz[:, z, n2l * H1:(n2l + 1) * H1], lhsT=dd[b][:, 1, n2l, cc * 128:(cc + 1) * 128],
                                                                                     rhs=isn[:, n2, :], start=False, stop=True))
            dstv = yc[:, cc, :].rearrange("p (a n) -> p n a", n=N2)[:, gi * G:(gi + 1) * G, :]
            if cc % 2 == 0:
                ev = ph.op("act", lambda e, z=z, dstv=dstv: e.activation(out=dstv, in_=pz[:, z, 0:G * H1].rearrange("p (g a) -> p g a", a=H1), func=AF.Copy), deps=[last])
            else:
                ev = ph.op("dve", lambda e, z=z, dstv=dstv: e.tensor_copy(out=dstv, in_=pz[:, z, 0:G * H1].rearrange("p (g a) -> p g a", a=H1)), deps=[last])
            pz_free[z] = ev
            evs.append(ev)
        dd_free[b] = last
    ysem = ph.dsem("ys")
    for cc in range(4):
        ph.dma("sp", d["YC"][cc * 128:(cc + 1) * 128, off:off + L], yc[:, cc, :], ysem, deps=evs[-8:])
    ph.run()


def phase_M(P, l):
    nc, d, g = P.nc, P.d, P.g
    T = P.T
    last_layer = l == DEPTH - 1
    xsrc = d["x"] if l == 0 else d["XR"]
    xdst = d["y"] if last_layer else d["XR"]
    ph = Phase(nc, f"M{l}")
    Wm = ph.sb("Wm", [128, 8, 3 * D], BF16)
    Wbr = ph.sb("Wbr", [128, 3, 4, D], BF16)
    Wo = ph.sb("Wo", [128, 8, D], BF16)
    bm = ph.sb("bm", [128, 24], F32)
    fg = ph.sb("fg", [128, D], F32)
    wl = P.gsem[0]
    t_w = None
    for k in range(8):
        t_w = ph.dma("pool", Wm[:, k, :], d["w_merge"][l, k * 128:(k + 1) * 128, :], wl)
    for j, nm in enumerate(["w_branch_hy", "w_branch_gqa", "w_branch_diff"]):
        for k in range(4):
            t_w = ph.dma("pool", Wbr[:, j, k, :], d[nm][l, k * 128:(k + 1) * 128, :], wl)
    for k in range(8):
        t_w = ph.dma("pool", Wo[:, k, :], d["w_out"][l, k * 128:(k + 1) * 128, :], wl)
    cl = ph.dsem("cl")
    ph.dma("sp", bm[:], d["b_merge"][l:l + 1, :].rearrange("o (j p) -> p (o j)", p=128), cl, slow=True)
    t_c = ph.dma("sp", fg[:], bcast_rows(d["final_g"].rearrange("(o f) -> o f", o=1), 128), cl)

    NB = 2
    hT = [ph.sb(f"hT{b}", [128, 8, 512], BF16) for b in range(NB)]
    Y = [ph.sb(f"Y{b}", [128, 4, 4, 512], BF16) for b in range(NB)]
    isem = [ph.dsem(f"is{b}") for b in range(NB)]
    in_free = [None] * NB
    in_ld = [None] * NB
    xt = [ph.sb(f"xt{s}", [128, D], F32) for s in range(4)]
    xsem = [ph.dsem(f"xs{s}") for s in range(4)]
    x_free = [None] * 4
    x_ld = [None] * 4
    gt = [ph.sb(f"gt{j}", [128, 512], F32) for j in range(3)]
    gt_free = [None] * 3
    acc = ph.sb("acc", [128, 512], F32)
    tm1 = ph.sb("tm1", [128, 512], F32)
    tm2 = ph.sb("tm2", [128, 512], F32)
    mg = ph.sb("mg", [128, 8, 512], BF16)
    mg_free = None
    xo = [ph.sb(f"xo{i}", [128, D], F32) for i in range(2)]
    xosem = [ph.dsem(f"xos{i}") for i in range(2)]
    xo_free = [None] * 2
    junk = ph.sb("junk", [128, D], BF16)
    ssq = ph.sb("ssq", [128, 2], F32)
    NS = 7
    pg = ph.ps("pg", [128, NS, 512], F32)
    slot_free = [None] * NS
    st = {"slot": 0, "xo": 0}

    def getslot():
        s = st["slot"] % NS
        st["slot"] += 1
        return s

    ntile = T // 512

    def load_in(tt):
        b = tt % NB
        t0 = tt * 512
        ph.dma("sp", hT[b][:], d["HT"][:, t0:t0 + 512].rearrange("(k p) t -> p k t", p=128), isem[b], deps=[in_free[b]])
        for j, nm in enumerate(["YC", "GG", "YG", "YD"]):
            in_ld[b] = ph.dma("sp", Y[b][:, j, :, :], d[nm][:, t0:t0 + 512].rearrange("(k p) t -> p k t", p=128), isem[b], deps=[in_free[b]])

    load_in(0)
    a_rd = None
    for tt in range(ntile):
        b = tt % NB
        t0 = tt * 512
        if tt + 1 < ntile:
            load_in(tt + 1)
        for s in range(4):
            x_ld[s] = ph.dma("sp", xt[s][:], xsrc[t0 + s * 128:t0 + (s + 1) * 128, :], xsem[s], deps=[x_free[s]])
        t_yh = ph.op("dve", lambda e, b=b: e.tensor_tensor(out=Y[b][:, 0, :, :], in0=Y[b][:, 0, :, :], in1=Y[b][:, 1, :, :], op=ALU.mult),
                     deps=[in_ld[b]])
        ysel = [0, 2, 3]
        last_pe = None
        for m in range(8):
            tg = []
            for j in range(3):
                su = getslot()
                for k in range(8):
                    last_pe = ph.op("pe", lambda e, su=su, k=k, j=j, m=m, b=b: e.matmul(pg[:, su, :], lhsT=Wm[:, k, j * D + m * 128:j * D + (m + 1) * 128],
                                                                                      rhs=hT[b][:, k, :], start=(k == 0), stop=(k == 7)),
                                    deps=[t_w, in_ld[b], slot_free[su]])
                t = ph.op("act", lambda e, su=su, j=j, m=m: e.activation(out=gt[j][:], in_=pg[:, su, :], func=AF.Sigmoid,
                                                                       bias=bm[:, j * 8 + m:j * 8 + m + 1], scale=1.0),
                          deps=[last_pe, gt_free[j], t_c])
                slot_free[su] = t
                tg.append(t)
            tb = []
            sb_ = []
            for j in range(3):
                su = getslot()
                sb_.append(su)
                for k in range(4):
                    last_pe = ph.op("pe", lambda e, su=su, k=k, j=j, m=m, b=b: e.matmul(pg[:, su, :], lhsT=Wbr[:, j, k, m * 128:(m + 1) * 128],
                                                                                      rhs=Y[b][:, ysel[j], k, :], start=(k == 0), stop=(k == 3)),
                                    deps=[t_yh, slot_free[su]])
                tb.append(last_pe)
            d0 = ph.op("dve", lambda e, su=sb_[0]: e.tensor_tensor(out=acc[:], in0=pg[:, su, :], in1=gt[0][:], op=ALU.mult),
                       deps=[tb[0], tg[0], a_rd])
            d1 = ph.op("dve", lambda e, su=sb_[1]: e.tensor_tensor(out=tm1[:], in0=pg[:, su, :], in1=gt[1][:], op=ALU.mult),
                       deps=[tb[1], tg[1]])
            d2 = ph.op("dve", lambda e, su=sb_[2]: e.tensor_tensor(out=tm2[:], in0=pg[:, su, :], in1=gt[2][:], op=ALU.mult),
                       deps=[tb[2], tg[2]])
            slot_free[sb_[0]] = d0
            slot_free[sb_[1]] = d1
            slot_free[sb_[2]] = d2
            gt_free[0] = d0
            gt_free[1] = d1
            gt_free[2] = d2
            p1 = ph.op("pool", lambda e: e.tensor_tensor(out=acc[:], in0=acc[:], in1=tm1[:], op=ALU.add), deps=[d0, d1])
            p2 = ph.op("pool", lambda e, m=m: e.tensor_tensor(out=mg[:, m, :], in0=acc[:], in1=tm2[:], op=ALU.add), deps=[p1, d2, mg_free])
            a_rd = p2
        in_free[b] = last_pe
        mg_rdy = a_rd
        for s in range(4):
            xi = st["xo"] % 2
            st["xo"] += 1
            tr = []
            for hh in range(2):
                su = getslot()
                for k in range(8):
                    last_pe = ph.op("pe", lambda e, su=su, k=k, s=s, hh=hh: e.matmul(pg[:, su, :], lhsT=mg[:, k, s * 128:(s + 1) * 128],
                                                                                   rhs=Wo[:, k, hh * 512:(hh + 1) * 512], start=(k == 0), stop=(k == 7)),
                                    deps=[mg_rdy, slot_free[su]])
                t = ph.op("dve", lambda e, su=su, s=s, hh=hh, xi=xi: e.tensor_tensor(out=xo[xi][:, hh * 512:(hh + 1) * 512], in0=pg[:, su, :],
                                                                                   in1=xt[s][:, hh * 512:(hh + 1) * 512], op=ALU.add),
                          deps=[last_pe, x_ld[s], xo_free[xi]])
                slot_free[su] = t
                tr.append(t)
            x_free[s] = tr[-1]
            r0 = t0 + s * 128
            if last_layer:
                c = xi
                t_ss = ph.op("act", lambda e, xi=xi, c=c: e.activation(out=junk[:], in_=xo[xi][:], func=AF.Square, accum_out=ssq[:, c:c + 1]),
                             deps=tr)
                t_sd = ph.op("act", lambda e, c=c: e.activation(out=ssq[:, c:c + 1], in_=ssq[:, c:c + 1], func=AF.Sqrt, bias=g["eps"][:], scale=1.0 / D),
                             deps=[t_ss])
                t_r = ph.op("dve", lambda e, c=c: e.reciprocal(out=ssq[:, c:c + 1], in_=ssq[:, c:c + 1]), deps=[t_sd])
                t_y = ph.op("dve", lambda e, xi=xi, c=c: e.scalar_tensor_tensor(out=xo[xi][:], in0=xo[xi][:], scalar=ssq[:, c:c + 1], in1=fg[:],
                                                                             op0=ALU.mult, op1=ALU.mult), deps=[t_r, t_c])
                xo_free[xi] = ph.dma("sp", xdst[r0:r0 + 128, :], xo[xi][:], xosem[xi], deps=[t_y])
            else:
                xo_free[xi] = ph.dma("sp", xdst[r0:r0 + 128, :], xo[xi][:], xosem[xi], deps=tr)
        mg_free = last_pe
    ph.run()


def build(stop=None):
    P = Prog()
    stop = stop or STOP_AFTER
    phase_prep(P)
    if stop == "prep":
        return P
    for l in range(DEPTH):
        for L in P.Ls:
            phase_filter(P, l, L)
        if stop == f"filter{l}":
            return P
        phase_A(P, l)
        if stop == f"A{l}":
            return P
        phase_H1(P, l)
        if stop == f"H1{l}":
            return P
        for si in range(len(SEQS)):
            phase_H2(P, l, si)
        if stop == f"H2{l}":
            return P
        phase_attn(P, l, "G")
        if stop == f"G{l}":
            return P
        phase_attn(P, l, "D")
        if stop == f"D{l}":
            return P
        phase_M(P, l)
        if stop == f"M{l}":
            return P
    return P


def make_in_maps(inputs, ncores):
    consts = host_consts()
    xp = np.asarray(inputs["x_prompt"], np.float32)
    xs = np.asarray(inputs["x_sample"], np.float32)
    maps = []
    for c in range(ncores):
        m = {}
        m["x"] = np.ascontiguousarray(np.concatenate([xp[c], xs[2 * c], xs[2 * c + 1]], axis=0))
        for k in PARAM_NAMES:
            m[k] = np.ascontiguousarray(np.asarray(inputs[k], np.float32))
        for k, v in consts.items():
            m["c_" + k] = v
        maps.append(m)
    return maps


def kernel(**inputs):
    P = build()
    maps = make_in_maps(inputs, NCORES)
    res = run_bass_kernel_spmd(P.nc, maps, core_ids=list(range(NCORES)))
    Lp, Ls = SEQS[0], SEQS[1]
    yp = np.stack([res.results[c]["y"][:Lp] for c in range(NCORES)], 0)
    ys = np.stack([res.results[c]["y"][Lp + j * Ls: Lp + (j + 1) * Ls] for c in range(NCORES) for j in range(2)], 0)
    return (np.ascontiguousarray(yp, dtype=np.float32), np.ascontiguousarray(ys, dtype=np.float32))
```

```python
import math
from contextlib import ExitStack

import numpy as np
import ml_dtypes

import concourse.bass as bass
import concourse.mybir as mybir
from concourse.bass_utils import run_bass_kernel_spmd

F32 = mybir.dt.float32
BF16 = mybir.dt.bfloat16
AF = mybir.ActivationFunctionType
ALU = mybir.AluOpType
AX = mybir.AxisListType

D = 1024
NCOL = 5376
HW = 512
EPS = 1e-6
DEPTH = 2
NCORES = 8
PI = math.pi

SEQS = [8192, 2048, 2048]
DEBUG_OUT = set()
STOP_AFTER = None

ENGS = ["pe", "act", "dve", "pool", "sp"]
BLK = {"pe": "tensor", "act": "scalar", "dve": "vector", "pool": "gpsimd", "sp": "sync"}


def fft_dims(L):
    n = 2 * L
    n1 = 1
    while n1 * n1 < n:
        n1 *= 2
    assert n1 * n1 == n, "2L must be a power of 4"
    return n1, n1


class DSem:
    def __init__(self, h, glob=False):
        self.h = h
        self.val = 0
        self.glob = glob


class Phase:
    def __init__(self, nc, name):
        self.nc = nc
        self.name = name
        self.es = ExitStack()
        self.q = {e: [] for e in ENGS}
        self.cnt = {e: 0 for e in ENGS}
        self.csem = {e: self.es.enter_context(nc.semaphore(f"{name}_{e}")) for e in ENGS}
        self.seen = {e: {} for e in ENGS}
        self.dsems = []
        self.gused = []
        self.n = 0

    def sb(self, name, shape, dt):
        return self.es.enter_context(self.nc.sbuf_tensor(f"{self.name}_{name}", list(shape), dt))

    def ps(self, name, shape, dt=F32):
        return self.es.enter_context(self.nc.psum_tensor(f"{self.name}_{name}", list(shape), dt))

    def dsem(self, name):
        d = DSem(self.es.enter_context(self.nc.semaphore(f"{self.name}_d_{name}")))
        self.dsems.append(d)
        return d

    def _waits(self, eng, deps):
        w = []
        for t in deps:
            if t is None:
                continue
            kind, key, val = t
            if kind == "c":
                if key == "pe" and eng == "pe":
                    continue
                h = self.csem[key]
                k = ("c", key)
            else:
                h = key.h
                k = ("d", id(key))
            if self.seen[eng].get(k, 0) >= val:
                continue
            self.seen[eng][k] = val
            w.append((h, val))
        return w

    def op(self, eng, fn, deps=()):
        w = self._waits(eng, deps)
        self.cnt[eng] += 1
        self.q[eng].append((w, fn, self.csem[eng], 1))
        self.n += 1
        return ("c", eng, self.cnt[eng])

    def dma(self, eng, out, in_, ds, deps=(), slow=False):
        assert (eng == "pool") == ds.glob, "pool-queue DMAs must use a global DSem (and only they)"
        if ds.glob and ds not in self.gused:
            self.gused.append(ds)
        w = self._waits(eng, deps)
        ds.val += 16
        if slow:
            fn = lambda e: e.dma_start(out=out, in_=in_, allow_slow_non_contiguous=True)
        else:
            fn = lambda e: e.dma_start(out=out, in_=in_)
        self.q[eng].append((w, fn, ds.h, 16))
        self.n += 1
        return ("d", ds, ds.val)

    def run(self):
        fin = []
        for d in self.dsems + self.gused:
            if d.val > 0 and self.seen["sp"].get(("d", id(d)), 0) < d.val:
                fin.append((d.h, d.val))
        q = self.q

        def emit(eng_name):
            def f(e):
                for (w, fn, sem, inc) in q[eng_name]:
                    for (h, v) in w:
                        e.wait_ge(h, v)
                    ins = fn(e)
                    ins.then_inc(sem, inc)
                if eng_name == "sp":
                    for (h, v) in fin:
                        e.wait_ge(h, v)
            return f

        with self.nc.Block() as block:
            for en in ENGS:
                if q[en] or (en == "sp" and fin):
                    getattr(block, BLK[en])(emit(en))
        allsems = [self.csem[e] for e in ENGS] + [d.h for d in self.dsems]

        def clr(e):
            for h in allsems:
                e.sem_clear(h)

        with self.nc.Block() as block:
            block.sync(clr)
        self.es.close()


_CONSTS = {}


def t5_bucket_np(rel):
    nb = 16
    max_exact = 8
    ret = (rel > 0).astype(np.int32) * nb
    n = np.abs(rel)
    nf = np.maximum(n, 1).astype(np.float32)
    large = max_exact + (np.log(nf / np.float32(max_exact)) / np.float32(math.log(128 / max_exact))
                         * np.float32(nb - max_exact)).astype(np.int32)
    large = np.minimum(large, nb - 1)
    return ret + np.where(n < max_exact, n, large)


def host_consts():
    key = tuple(SEQS)
    if key in _CONSTS:
        return _CONSTS[key]
    bf = ml_dtypes.bfloat16
    c = {}
    c["ident"] = np.eye(128, dtype=np.float32).astype(bf)
    ps = np.zeros((128, 128), np.float32)
    for m in range(128):
        if m % 64 < 32:
            ps[m + 32, m] = -1.0
        else:
            ps[m - 32, m] = 1.0
    c["pswap"] = ps.astype(bf)
    ob = np.zeros((128, 128), np.float32)
    ob[:64, :64] = 1.0 / 64
    ob[64:, 64:] = 1.0 / 64
    c["onesblk"] = ob.astype(bf)
    c["ones128"] = np.full((128, 128), 1.0 / 128, np.float32).astype(bf)
    c["ones1"] = np.ones((128, 128), np.float32).astype(bf)
    Lmax = max(SEQS)
    t = np.arange(Lmax)
    row = (t // 64).astype(np.float32)
    col = (t % 64).astype(np.float32)
    n_freq = 16
    inv = (np.float32(10000.0) ** (-np.arange(n_freq, dtype=np.float32) / np.float32(n_freq))).astype(np.float32)
    ang = np.concatenate([row[:, None] * inv, col[:, None] * inv], axis=-1).astype(np.float32)
    cs = np.cos(ang).astype(np.float32).T
    sn = np.sin(ang).astype(np.float32).T
    c["ropec"] = np.ascontiguousarray(np.tile(cs, (4, 1)))
    c["ropes"] = np.ascontiguousarray(np.tile(sn, (4, 1)))
    e = np.arange(1280)
    bk = t5_bucket_np((e - 639).astype(np.int32))
    oh = np.zeros((32, 1280), np.float32)
    oh[bk, e] = 1.0
    c["t5oh"] = oh
    for L in sorted(set(SEQS)):
        t01 = np.linspace(0.0, 1.0, L, dtype=np.float32)[:, None]
        bands = 16
        w = (np.float32(2.0 * math.pi) * np.arange(L, dtype=np.float32)[:, None] / np.float32(L)).astype(np.float32)
        f = np.linspace(1e-4, bands - 1, bands, dtype=np.float32)[None, :]
        z = np.concatenate([t01, np.cos(f * w), -np.sin(f * w)], axis=-1).astype(np.float32)
        c[f"zT{L}"] = np.ascontiguousarray(z.T)
        max_decay = math.log(1e-2) / 0.3
        min_decay = math.log(1e-2) / 1.5
        deltas = np.linspace(min_decay, max_decay, HW, dtype=np.float32)
        win = np.exp(-t01 * np.abs(deltas)[None, :]).astype(np.float32)
        wb = win.copy()
        wb[0, :] = 0.0
        c[f"win{L}"] = np.ascontiguousarray(np.stack([win, wb], 0))
        N1, N2 = fft_dims(L)
        N = N1 * N2
        n1 = np.arange(N1 // 2)[:, None, None]
        n2 = np.arange(N2)[None, :, None]
        k1 = np.arange(N1)[None, None, :]
        ph = (k1 * (N2 * n1 + n2)) % N
        th = 2.0 * np.pi * ph.astype(np.float64) / N
        f1r = np.cos(th).astype(np.float32).astype(bf)
        f1i = (-np.sin(th)).astype(np.float32).astype(bf)
        if N1 // 2 >= 32:
            f1r = np.ascontiguousarray(np.concatenate([f1r[:, 0::2, :], f1r[:, 1::2, :]], axis=0))
            f1i = np.ascontiguousarray(np.concatenate([f1i[:, 0::2, :], f1i[:, 1::2, :]], axis=0))
        c[f"f1r{L}"] = f1r
        c[f"f1i{L}"] = f1i
        a = np.arange(N2)[:, None]
        b = np.arange(N2)[None, :]
        th2 = 2.0 * np.pi * ((a * b) % N2).astype(np.float64) / N2
        c[f"f2{L}"] = np.stack([np.cos(th2), np.sin(th2), -np.sin(th2), -np.cos(th2)], 1).astype(np.float32).astype(bf)
        k1 = np.arange(N1)[:, None, None]
        n2 = np.arange(N2)[None, :, None]
        n1 = np.arange(N1 // 2)[None, None, :]
        ph = (k1 * (N2 * n1 + n2)) % N
        th = 2.0 * np.pi * ph.astype(np.float64) / N
        c[f"i2c{L}"] = (np.cos(th) / N).astype(np.float32).astype(bf)
        c[f"i2s{L}"] = (-np.sin(th) / N).astype(np.float32).astype(bf)
    _CONSTS[key] = c
    return c


PARAM_NAMES = ["rel_bias", "norm_g", "w_in", "hy_conv_w", "hy_conv_b", "hy_f_w1", "hy_f_b1", "hy_f_w2",
               "hy_f_b2", "hy_f_wout", "hy_f_freq", "hy_bias", "q_norm_g", "k_norm_g", "lam_q1", "lam_k1",
               "lam_q2", "lam_k2", "diff_subln_g", "w_branch_hy", "w_branch_gqa", "w_branch_diff",
               "w_merge", "b_merge", "w_out", "final_g"]

PARAM_SHAPES = {
    "rel_bias": (32, 4), "norm_g": (DEPTH, D), "w_in": (DEPTH, D, NCOL), "hy_conv_w": (DEPTH, 3, 1536),
    "hy_conv_b": (DEPTH, 1536), "hy_f_w1": (DEPTH, 33, 64), "hy_f_b1": (DEPTH, 64),
    "hy_f_w2": (DEPTH, 2, 64, 64), "hy_f_b2": (DEPTH, 2, 64), "hy_f_wout": (DEPTH, 64, 1024),
    "hy_f_freq": (DEPTH, 64), "hy_bias": (DEPTH, HW), "q_norm_g": (DEPTH, 64), "k_norm_g": (DEPTH, 64),
    "lam_q1": (DEPTH, 64), "lam_k1": (DEPTH, 64), "lam_q2": (DEPTH, 64), "lam_k2": (DEPTH, 64),
    "diff_subln_g": (DEPTH, 128), "w_branch_hy": (DEPTH, HW, D), "w_branch_gqa": (DEPTH, HW, D),
    "w_branch_diff": (DEPTH, HW, D), "w_merge": (DEPTH, D, 3 * D), "b_merge": (DEPTH, 3 * D),
    "w_out": (DEPTH, D, D), "final_g": (D,),
}


def bcast_rows(ap_1d_or_row, nparts):
    a = ap_1d_or_row
    return bass.AP(tensor=a.tensor, offset=a.offset, ap=[[0, nparts]] + [list(x) for x in a.ap[-1:]])


class Prog:
    def __init__(self):
        self.nc = bass.Bass("TRN2", target_bir_lowering=False)
        nc = self.nc
        self.T = sum(SEQS)
        self.offs = [sum(SEQS[:i]) for i in range(len(SEQS))]
        self.Ls = sorted(set(SEQS), reverse=True)
        self.consts = host_consts()
        self.d = {}
        self.d["x"] = nc.dram_tensor("x", [self.T, D], F32, kind="ExternalInput").ap()
        for k in PARAM_NAMES:
            self.d[k] = nc.dram_tensor(k, list(PARAM_SHAPES[k]), F32, kind="ExternalInput").ap()
        for k, v in self.consts.items():
            dt = BF16 if v.dtype == ml_dtypes.bfloat16 else F32
            self.d["c_" + k] = nc.dram_tensor("c_" + k, list(v.shape), dt, kind="ExternalInput").ap()
        self.d["y"] = nc.dram_tensor("y", [self.T, D], F32, kind="ExternalOutput").ap()
        self.ges = ExitStack()
        self.g = {}
        self.gsem = [DSem(self.ges.enter_context(nc.semaphore(f"gpool{i}")), glob=True) for i in range(4)]

    def scr(self, name, shape, dt):
        kind = "ExternalOutput" if name in DEBUG_OUT else "Internal"
        a = self.nc.dram_tensor(name, list(shape), dt, kind=kind).ap()
        self.d[name] = a
        return a

    def gsb(self, name, shape, dt):
        t = self.ges.enter_context(self.nc.sbuf_tensor("g_" + name, list(shape), dt))
        self.g[name] = t
        return t


def phase_prep(P):
    nc, d, g = P.nc, P.d, P.g
    for nm in ["ident", "pswap", "onesblk", "ones128", "ones1"]:
        P.gsb(nm, [128, 128], BF16)
    P.gsb("wstrip", [128, 4, 1152], BF16)
    P.gsb("farb", [128, 4, 2], F32)
    P.gsb("lamt", [128, DEPTH, 2], F32)
    P.gsb("eps", [128, 1], F32)
    P.gsb("onesf", [128, 128], F32)
    bvr = P.scr("BVR", [4, 1280], F32)
    ph = Phase(nc, "prep")
    ld = ph.dsem("ld")
    toks = []
    for nm in ["ident", "pswap", "onesblk", "ones128", "ones1"]:
        toks.append(ph.dma("sp", g[nm][:], d["c_" + nm][:], ld))
    t_eps = ph.op("pool", lambda e: e.memset(g["eps"][:], EPS))
    ph.op("pool", lambda e: e.memset(g["onesf"][:], 1.0))
    rb = ph.sb("rb", [32, 4], F32)
    oh = ph.sb("oh", [32, 1280], F32)
    bv = ph.sb("bv", [4, 1280], F32)
    pb = ph.ps("pb", [4, 3, 512], F32)
    l2 = ph.dsem("l2")
    ph.dma("sp", rb[:], d["rel_bias"][:], l2)
    t_oh = ph.dma("sp", oh[:], d["c_t5oh"][:], l2)
    widths = [512, 512, 256]
    tcp = []
    for i, wd in enumerate(widths):
        tm = ph.op("pe", lambda e, i=i, wd=wd: e.matmul(pb[:, i, 0:wd], lhsT=rb[:], rhs=oh[:, i * 512:i * 512 + wd],
                                                         start=True, stop=True), deps=[t_oh])
        tcp.append(ph.op("dve", lambda e, i=i, wd=wd: e.tensor_copy(out=bv[:, i * 512:i * 512 + wd], in_=pb[:, i, 0:wd]),
                         deps=[tm]))
    st = ph.dsem("st")
    t_st = ph.dma("sp", bvr[:], bv[:], st, deps=tcp)
    l3 = ph.dsem("l3")
    wrev = ph.sb("wrev", [128, 4, 1152], F32)
    for h in range(4):
        src = bass.AP(tensor=bvr.tensor, offset=h * 1280, ap=[[1, 128], [1, 1152]])
        t_wr = ph.dma("sp", wrev[:, h, :], src, l3, deps=[t_st])
        for side, e0 in enumerate([0, 1278]):
            src = bass.AP(tensor=bvr.tensor, offset=h * 1280 + e0, ap=[[0, 128], [1, 1]])
            t_wr = ph.dma("sp", g["farb"][:, h, side:side + 1], src, l3, deps=[t_st])
    for h in range(4):
        a = wrev[:, h, :]
        rev = bass.AP(tensor=a.tensor, offset=a.offset + 1151, ap=[list(a.ap[0]), [-1, 1152]])
        ph.op("dve", lambda e, h=h, rev=rev: e.tensor_scalar(out=g["wstrip"][:, h, :], in0=rev, scalar1=8.0, scalar2=None, op0=ALU.mult),
              deps=[t_wr])
    lv = ph.sb("lv", [128, DEPTH, 4, 64], F32)
    l4 = ph.dsem("l4")
    t_lv = None
    for l in range(DEPTH):
        for j, nm in enumerate(["lam_q1", "lam_k1", "lam_q2", "lam_k2"]):
            t_lv = ph.dma("sp", lv[:, l, j, :], bcast_rows(d[nm][l:l + 1, :], 128), l4)
    junk = ph.sb("junk", [128, 64], F32)
    ss = ph.sb("ss", [128, DEPTH, 2], F32)
    ee = ph.sb("ee", [128, DEPTH, 2], F32)
    df = ph.sb("df", [128, DEPTH], F32)
    for l in range(DEPTH):
        lam_init = 0.8 - 0.6 * math.exp(-0.3 * l)
        ts = []
        prev = None
        for j in range(2):
            tmul = ph.op("dve", lambda e, l=l, j=j: e.tensor_tensor(out=junk[:], in0=lv[:, l, 2 * j, :], in1=lv[:, l, 2 * j + 1, :],
                                                                 op=ALU.mult), deps=[t_lv, prev])
            prev = ph.op("dve", lambda e, l=l, j=j: e.tensor_reduce(out=ss[:, l, j:j + 1], in_=junk[:], axis=AX.X, op=ALU.add),
                         deps=[tmul])
            ts.append(prev)
        te = ph.op("act", lambda e, l=l: e.activation(out=ee[:, l, :], in_=ss[:, l, :], func=AF.Exp), deps=ts)
        t1 = ph.op("dve", lambda e, l=l: e.tensor_tensor(out=df[:, l:l + 1], in0=ee[:, l, 1:2], in1=ee[:, l, 0:1],
                                                         op=ALU.subtract), deps=[te, prev])
        ph.op("dve", lambda e, l=l, li=lam_init: e.tensor_scalar(out=g["lamt"][:, l, 0:1], in0=df[:, l:l + 1],
                                                                  scalar1=-li, scalar2=None, op0=ALU.add), deps=[t1])
    ph.run()


def fft_stage1(ph, P, L, srcs, combos, Ad, tagp):
    nc, d = P.nc, P.d
    N1, N2 = fft_dims(L)
    H1 = N1 // 2
    G = FG
    paired = H1 >= 32
    NP_ = 2 * H1 if paired else H1
    GH = G // 2 if paired else G
    ncol = N2 // 2 if paired else N2
    tr = ph.sb(tagp + "tr", [NP_, ncol, N1], BF16)
    ti = ph.sb(tagp + "ti", [NP_, ncol, N1], BF16)
    ldt = ph.dsem(tagp + "ldt")
    ph.dma("sp", tr[:], d[f"c_f1r{L}"][:], ldt)
    t_tab = ph.dma("sp", ti[:], d[f"c_f1i{L}"][:], ldt)
    ns = len(srcs)
    NB = 2
    xs = [ph.sb(f"{tagp}x{b}", [NP_, ns, GH, HW], BF16) for b in range(NB)]
    xsem = [ph.dsem(f"{tagp}xs{b}") for b in range(NB)]
    no = len(combos)
    stg = [ph.sb(f"{tagp}st{b}", [N1, no, 2, G, HW], BF16) for b in range(NB)]
    ssem = [ph.dsem(f"{tagp}ss{b}") for b in range(NB)]
    NSL = 4
    pp = ph.ps(tagp + "pp", [N1, NSL, HW], F32)
    ngrp = N2 // G
    x_free = [None] * NB
    st_free = [None] * NB
    pp_free = [None] * NSL
    x_ld = [None] * NB

    def load(gi):
        b = gi % NB
        t = None
        for si, s in enumerate(srcs):
            v = s.rearrange("(a n) c -> a n c", n=N2)
            if paired:
                for hf in range(2):
                    src = v[0:H1, gi * G + hf:(gi + 1) * G:2, :]
                    t = ph.dma("sp", xs[b][hf * H1:(hf + 1) * H1, si, :, :], src, xsem[b], deps=[x_free[b]])
            else:
                t = ph.dma("sp", xs[b][:, si, :, :], v[0:H1, gi * G:(gi + 1) * G, :], xsem[b], deps=[x_free[b]])
        x_ld[b] = t

    load(0)
    cnt = 0
    for gi in range(ngrp):
        b = gi % NB
        if gi + 1 < ngrp:
            load(gi + 1)
        evs = []
        last_mm = None
        for ml in range(GH):
            for o, combo in enumerate(combos):
                for ri, tab in enumerate([tr, ti]):
                    halves = [0, 1] if paired else [0]
                    slots = []
                    for hf in halves:
                        slots.append(cnt % NSL)
                        cnt += 1
                    col = (gi * G) // (2 if paired else 1) + ml
                    for ci, si in enumerate(combo):
                        for hf in halves:
                            slot = slots[hf]
                            p0, p1 = hf * H1, (hf + 1) * H1
                            kw = dict(tile_position=(p0, 0)) if paired else {}
                            last_mm = ph.op("pe", lambda e, slot=slot, tab=tab, col=col, b=b, si=si, ml=ml, ci=ci, nci=len(combo), p0=p0, p1=p1, kw=kw:
                                            e.matmul(pp[:, slot, :], lhsT=tab[p0:p1, col, :], rhs=xs[b][p0:p1, si, ml, :],
                                                     start=(ci == 0), stop=(ci == nci - 1), **kw),
                                            deps=[t_tab, x_ld[b], pp_free[slot]])
                    for hf in halves:
                        slot = slots[hf]
                        n2l = (2 * ml + hf) if paired else ml
                        eng = "act" if (cnt + hf) % 2 else "dve"
                        if eng == "act":
                            ev = ph.op("act", lambda e, slot=slot, b=b, o=o, ri=ri, n2l=n2l:
                                       e.activation(out=stg[b][:, o, ri, n2l, :], in_=pp[:, slot, :], func=AF.Copy),
                                       deps=[last_mm, st_free[b]])
                        else:
                            ev = ph.op("dve", lambda e, slot=slot, b=b, o=o, ri=ri, n2l=n2l:
                                       e.tensor_copy(out=stg[b][:, o, ri, n2l, :], in_=pp[:, slot, :]),
                                       deps=[last_mm, st_free[b]])
                        pp_free[slot] = ev
                        evs.append(ev)
        x_free[b] = last_mm
        t = None
        for o in range(no):
            for ri in range(2):
                dst = Ad[o][ri][gi * G:(gi + 1) * G, :, :].rearrange("n k c -> k n c")
                t = ph.dma("sp", dst, stg[b][:, o, ri, :, :], ssem[b], deps=evs)
        st_free[b] = t


def fft_stage2_tables(ph, P, L, tagp):
    d = P.d
    N1, N2 = fft_dims(L)
    f2 = ph.sb(tagp + "f2", [N2, 4, N2], BF16)
    ldt = ph.dsem(tagp + "ldf2")
    t = ph.dma("sp", f2[:], d[f"c_f2{L}"][:], ldt)
    return f2, t


FG = 4


def load_ktiles(ph, srcs, k0, G, dst, sem, deps, N2):
    t = None
    for si, s in enumerate(srcs):
        t = ph.dma("sp", dst[:, si, :, :], s[:, k0:k0 + G, :], sem, deps=deps)
    return t


def phase_filter(P, l, L):
    nc, d, g = P.nc, P.d, P.g
    N1, N2 = fft_dims(L)
    key = f"{L}"
    if ("FIL" + key) not in d:
        P.scr("FIL" + key, [3, L, HW], BF16)
        for o in range(2):
            for ri in range(2):
                P.scr(f"AF{key}_{o}{ri}", [N2, N1, HW], BF16)
        for ri in range(2):
            P.scr(f"H{key}_{ri}", [N2, N1, HW], BF16)
    FIL = d["FIL" + key]
    AFd = [[d[f"AF{key}_{o}{ri}"] for ri in range(2)] for o in range(2)]
    Hd = [d[f"H{key}_{ri}"] for ri in range(2)]

    ph = Phase(nc, f"fm{l}_{L}")
    w1 = ph.sb("w1", [33, 64], F32)
    w2 = ph.sb("w2", [64, 2, 64], F32)
    wo = ph.sb("wo", [64, 1024], F32)
    fr = ph.sb("fr", [64, 1], F32)
    bb = ph.sb("bb", [64, 3], F32)
    fb = ph.sb("fb", [64, 3], F32)
    zt = ph.sb("zt", [33, L], F32)
    ld = ph.dsem("ld")
    ph.dma("sp", w1[:], d["hy_f_w1"][l], ld)
    for i in range(2):
        ph.dma("sp", w2[:, i, :], d["hy_f_w2"][l, i], ld)
    ph.dma("sp", wo[:], d["hy_f_wout"][l], ld)
    ph.dma("sp", fr[:], d["hy_f_freq"][l:l + 1, :].rearrange("o f -> f o"), ld, slow=True)
    ph.dma("sp", bb[:, 0:1], d["hy_f_b1"][l:l + 1, :].rearrange("o f -> f o"), ld, slow=True)
    for i in range(2):
        ph.dma("sp", bb[:, 1 + i:2 + i], d["hy_f_b2"][l, i:i + 1, :].rearrange("o f -> f o"), ld, slow=True)
    t_ld = ph.dma("sp", zt[:], d[f"c_zT{L}"][:], ld)
    t_fb = ph.op("dve", lambda e: e.tensor_scalar(out=fb[:], in0=bb[:], scalar1=fr[:, 0:1], scalar2=None, op0=ALU.mult),
                 deps=[t_ld])
    pa = ph.ps("pa", [64, 4, 512], F32)
    pf = ph.ps("pf", [128, 4, 512], F32)
    ya = [[ph.sb(f"ya{p}_{i}", [64, 512], F32) for i in range(2)] for p in range(2)]
    aa = [[ph.sb(f"aa{p}_{i}", [64, 512], F32) for i in range(2)] for p in range(2)]
    m1 = [ph.sb(f"m1_{p}", [64, 512], F32) for p in range(2)]
    m2 = [ph.sb(f"m2_{p}", [64, 512], F32) for p in range(2)]
    NBW = 2
    wn = [ph.sb(f"wn{b}", [128, 2, 512], F32) for b in range(NBW)]
    wsem = [ph.dsem(f"ws{b}") for b in range(NBW)]
    fo = [ph.sb(f"fo{b}", [128, 3, 512], BF16) for b in range(NBW)]
    fsem = [ph.dsem(f"fs{b}") for b in range(NBW)]
    wn_free = [None] * NBW
    fo_free = [None] * NBW
    pa_free = [None] * 4
    pf_free = [None] * 4
    a_read = [[None, None], [None, None]]
    ya_read = [[None, None], [None, None]]
    m_read = [None, None]
    ntile = L // 512
    cnt = {"sub": 0, "pfc": 0}

    def tile_stages(j, par):
        cols = slice(j * 512, (j + 1) * 512)
        stt = {"prev_a": None}

        def layer_stage(layer):
            def run():
                pi = par * 2 + layer % 2
                if layer == 0:
                    tm = ph.op("pe", lambda e: e.matmul(pa[:, pi, :], lhsT=w1[:], rhs=zt[:, cols], start=True, stop=True),
                               deps=[t_ld, pa_free[pi]])
                else:
                    src = aa[par][(layer - 1) % 2]
                    tm = ph.op("pe", lambda e: e.matmul(pa[:, pi, :], lhsT=w2[:, layer - 1, :], rhs=src[:], start=True, stop=True),
                               deps=[t_ld, pa_free[pi], stt["prev_a"]])
                    a_read[par][(layer - 1) % 2] = tm
                yb = ya[par][layer % 2]
                t1 = ph.op("dve", lambda e: e.tensor_scalar(out=yb[:], in0=pa[:, pi, :], scalar1=fr[:, 0:1],
                                                            scalar2=fb[:, layer:layer + 1], op0=ALU.mult, op1=ALU.add),
                           deps=[tm, t_fb, ya_read[par][layer % 2]])
                pa_free[pi] = t1
                ta = ph.op("dve", lambda e: e.tensor_scalar(out=m1[par][:], in0=yb[:], scalar1=-PI, scalar2=2 * PI, op0=ALU.is_lt, op1=ALU.mult),
                           deps=[t1, m_read[par]])
                tb = ph.op("dve", lambda e: e.tensor_scalar(out=m2[par][:], in0=yb[:], scalar1=PI, scalar2=-2 * PI, op0=ALU.is_gt, op1=ALU.mult),
                           deps=[t1])
                tc = ph.op("dve", lambda e: e.tensor_tensor(out=yb[:], in0=yb[:], in1=m1[par][:], op=ALU.add), deps=[ta, tb])
                t2 = ph.op("dve", lambda e: e.tensor_tensor(out=yb[:], in0=yb[:], in1=m2[par][:], op=ALU.add), deps=[tc])
                m_read[par] = t2
                ab = aa[par][layer % 2]
                stt["prev_a"] = ph.op("act", lambda e: e.activation(out=ab[:], in_=yb[:], func=AF.Sin),
                                      deps=[t2, a_read[par][layer % 2]])
                ya_read[par][layer % 2] = stt["prev_a"]
            return run

        def tm_stage(s):
            def run():
                a3 = aa[par][0]
                b = cnt["sub"] % NBW
                cnt["sub"] += 1
                t0 = j * 512 + s * 128
                t_w = ph.dma("sp", wn[b][:], d[f"c_win{L}"][0:2, t0:t0 + 128, :].rearrange("w t c -> t w c"), wsem[b], deps=[wn_free[b]])
                outs = []
                for half in range(2):
                    slot = cnt["pfc"] % 4
                    cnt["pfc"] += 1
                    tm = ph.op("pe", lambda e, slot=slot, half=half: e.matmul(pf[:, slot, :], lhsT=a3[:, s * 128:(s + 1) * 128],
                                                                              rhs=wo[:, half * 512:(half + 1) * 512], start=True, stop=True),
                               deps=[stt["prev_a"], pf_free[slot]])
                    a_read[par][0] = tm
                    te = ph.op("dve", lambda e, slot=slot, half=half: e.tensor_tensor(out=fo[b][:, half, :], in0=pf[:, slot, :],
                                                                                      in1=wn[b][:, half, :], op=ALU.mult),
                               deps=[tm, t_w, fo_free[b]])
                    pf_free[slot] = te
                    outs.append(te)
                wn_free[b] = outs[-1]
                tn = ph.op("act", lambda e: e.activation(out=fo[b][:, 2, :], in_=fo[b][:, 1, :], func=AF.Copy, scale=-1.0),
                           deps=[outs[-1], fo_free[b]])
                fo_free[b] = ph.dma("sp", FIL[:, t0:t0 + 128, :].rearrange("w t c -> t w c"), fo[b][:], fsem[b], deps=[outs[0], tn])
            return run

        return [layer_stage(0), layer_stage(1), layer_stage(2)] + [tm_stage(s) for s in range(4)]

    for j0 in range(0, ntile, 2):
        lists = [tile_stages(j, j - j0) for j in range(j0, min(j0 + 2, ntile))]
        for k in range(7):
            for lst in lists:
                lst[k]()
    ph.run()

    ph = Phase(nc, f"ff1{l}_{L}")
    fft_stage1(ph, P, L, [FIL[0], FIL[1], FIL[2]], [[0, 1], [0, 2]], AFd, "a")
    ph.run()

    ph = Phase(nc, f"ff2{l}_{L}")
    f2, t_f2 = fft_stage2_tables(ph, P, L, "b")
    hb = ph.sb("hb", [128, HW], F32)
    lb = ph.dsem("lb")
    t_hb = ph.dma("sp", hb[:], bcast_rows(d["hy_bias"][l:l + 1, :], 128), lb)
    G = FG
    NB = 2
    xin = [ph.sb(f"xin{b}", [N2, 4, G, HW], BF16) for b in range(NB)]
    xsem = [ph.dsem(f"xs{b}") for b in range(NB)]
    hst = [ph.sb(f"hst{b}", [N2, 2, G, HW], BF16) for b in range(NB)]
    hsem = [ph.dsem(f"hs{b}") for b in range(NB)]
    pp = ph.ps("pp", [N2, 4, HW], F32)
    x_free = [None] * NB
    h_free = [None] * NB
    pp_free = [None] * 4
    x_ld = [None] * NB
    srcs = [AFd[0][0], AFd[0][1], AFd[1][0], AFd[1][1]]
    ngrp = N1 // G
    x_ld[0] = load_ktiles(ph, srcs, 0, G, xin[0], xsem[0], [], N2)
    cnt = 0
    for gi in range(ngrp):
        b = gi % NB
        if gi + 1 < ngrp:
            nb_ = (gi + 1) % NB
            x_ld[nb_] = load_ktiles(ph, srcs, (gi + 1) * G, G, xin[nb_], xsem[nb_], [x_free[nb_]], N2)
        evs = []
        last = None
        for kl in range(G):
            for ri, (ia, ib, tb) in enumerate([(0, 1, 1), (3, 2, 2)]):
                slot = cnt % 4
                cnt += 1
                ph.op("pe", lambda e, slot=slot, b=b, ia=ia, kl=kl: e.matmul(pp[:, slot, :], lhsT=f2[:, 0, :], rhs=xin[b][:, ia, kl, :],
                                                                           start=True, stop=False),
                      deps=[t_f2, x_ld[b], pp_free[slot]])
                last = ph.op("pe", lambda e, slot=slot, b=b, ib=ib, kl=kl, tb=tb: e.matmul(pp[:, slot, :], lhsT=f2[:, tb, :], rhs=xin[b][:, ib, kl, :],
                                                                                         start=False, stop=True))
                if ri == 0:
                    ev = ph.op("dve", lambda e, slot=slot, b=b, kl=kl: e.tensor_tensor(out=hst[b][:, 0, kl, :], in0=pp[:, slot, :], in1=hb[0:N2, :], op=ALU.add),
                               deps=[last, t_hb, h_free[b]])
                else:
                    ev = ph.op("act", lambda e, slot=slot, b=b, kl=kl: e.activation(out=hst[b][:, 1, kl, :], in_=pp[:, slot, :], func=AF.Copy),
                               deps=[last, h_free[b]])
                pp_free[slot] = ev
                evs.append(ev)
        x_free[b] = last
        t = None
        for ri in range(2):
            t = ph.dma("sp", Hd[ri][:, gi * G:(gi + 1) * G, :], hst[b][:, ri, :, :], hsem[b], deps=evs)
        h_free[b] = t
    ph.run()


def ensure_act_scratch(P):
    d, T = P.d, P.T
    if "UH" in d:
        return
    P.scr("UH", [2048, T], BF16)
    P.scr("GQ", [512, T], BF16)
    P.scr("GK2", [2, 128, T], BF16)
    P.scr("GV", [2, T, 64], BF16)
    P.scr("GS", [512, T], BF16)
    P.scr("DQ", [512, T], BF16)
    P.scr("DK", [512, T], BF16)
    P.scr("DV", [T, 512], BF16)
    P.scr("DS", [512, T], BF16)
    P.scr("HT", [1024, T], BF16)
    P.scr("XR", [T, D], F32)
    P.scr("HV", [T, HW], BF16)
    P.scr("GG", [512, T], BF16)
    P.scr("YC", [512, T], BF16)
    P.scr("YG", [512, T], BF16)
    P.scr("YD", [512, T], BF16)


def seq_of(P, t0):
    for i, o in enumerate(P.offs):
        if o <= t0 < o + SEQS[i]:
            return i, t0 - o
    raise ValueError


def phase_A(P, l):
    nc, d, g = P.nc, P.d, P.g
    T = P.T
    ensure_act_scratch(P)
    xsrc = d["x"] if l == 0 else d["XR"]
    ph = Phase(nc, f"A{l}")
    Wb = ph.sb("Wb", [128, 8, NCOL], BF16)
    gam = ph.sb("gam", [128, D], F32)
    qg = ph.sb("qg", [128, 2], F32)
    wl = P.gsem[0]
    t_w = None
    for k in range(8):
        t_w = ph.dma("pool", Wb[:, k, :], d["w_in"][l, k * 128:(k + 1) * 128, :], wl)
    cl = ph.dsem("cl")
    ph.dma("sp", gam[:], bcast_rows(d["norm_g"][l:l + 1, :], 128), cl)
    for hh in range(2):
        ph.dma("sp", qg[hh * 64:(hh + 1) * 64, 0:1], d["q_norm_g"][l:l + 1, :].rearrange("o f -> f o"), cl, slow=True)
        t_c = ph.dma("sp", qg[hh * 64:(hh + 1) * 64, 1:2], d["k_norm_g"][l:l + 1, :].rearrange("o f -> f o"), cl, slow=True)

    NXB = 3
    xb = [ph.sb(f"xb{i}", [128, D], F32) for i in range(NXB)]
    xsem = [ph.dsem(f"xs{i}") for i in range(NXB)]
    xb_free = [None] * NXB
    hb = [ph.sb(f"hb{i}", [128, D], BF16) for i in range(4)]
    hb_free = [None] * 4
    hb_rdy = [None] * 4
    junk = ph.sb("junk", [128, D], F32)
    ssq = ph.sb("ssq", [128, 4], F32)
    hT = [ph.sb(f"hT{i}", [128, 8, 512], BF16) for i in range(2)]
    hT_free = [None] * 2
    hT_st = [None] * 2
    hT_rdy = [None] * 2
    htsem = [ph.dsem(f"hts{i}") for i in range(2)]
    cs = [ph.sb(f"cs{i}", [128, 2, 512], F32) for i in range(2)]
    cssem = [ph.dsem(f"css{i}") for i in range(2)]
    cs_free = [None] * 2
    cs_ld = [None] * 2
    tp = ph.ps("tp", [128, 2, 8, 128], BF16)
    tp_free = [None] * 2
    NS = 6
    pg = ph.ps("pg", [128, NS, 512], F32)
    slot_free = [None] * NS
    st = {"slot": 0, "sub": 0, "og": 0, "tpi": 0, "qs": 0}
    NOG = 5
    og = [ph.sb(f"og{i}", [128, 4, 512], BF16) for i in range(NOG)]
    ogsem = [ph.dsem(f"ogs{i}") for i in range(NOG)]
    og_free = [None] * NOG
    vs = [ph.sb(f"vs{i}", [128, 640], BF16) for i in range(2)]
    vssem = [ph.dsem(f"vss{i}") for i in range(2)]
    vs_free = [None] * 2
    sqb = [ph.sb(f"sqb{i}", [128, 512], BF16) for i in range(2)]
    rt = [ph.sb(f"rt{i}", [128, 512], F32) for i in range(2)]
    qn = [ph.sb(f"qn{i}", [128, 512], BF16) for i in range(2)]
    t1b = [ph.sb(f"t1b{i}", [128, 512], F32) for i in range(2)]
    t2b = [ph.sb(f"t2b{i}", [128, 512], F32) for i in range(2)]
    lastuse = [dict() for _ in range(2)]
    ssm = ph.sb("ssm", [128, 4], F32)
    rsd = ph.sb("rsd", [128, 4], F32)
    ss_free = [None] * 4

    def getslot():
        s = st["slot"] % NS
        st["slot"] += 1
        return s

    ntile = T // 512

    def norm_part1(tt):
        b = tt % 2
        t0 = tt * 512
        si, pos = seq_of(P, t0)
        cs_ld[b] = None
        ph.dma("sp", cs[b][:, 0, :], d["c_ropec"][:, pos:pos + 512], cssem[b], deps=[cs_free[b]])
        cs_ld[b] = ph.dma("sp", cs[b][:, 1, :], d["c_ropes"][:, pos:pos + 512], cssem[b], deps=[cs_free[b]])
        for s in range(4):
            i = st["sub"] % NXB
            st["sub"] += 1
            j = s
            q4 = s
            r0 = t0 + s * 128
            t_x = ph.dma("sp", xb[i][:], xsrc[r0:r0 + 128, :], xsem[i], deps=[xb_free[i]])
            t_ss = ph.op("act", lambda e, i=i, q4=q4: e.activation(out=junk[:], in_=xb[i][:], func=AF.Square,
                                                                  accum_out=ssq[:, q4:q4 + 1]), deps=[t_x, ss_free[q4]])
            t_sd = ph.op("act", lambda e, q4=q4: e.activation(out=ssm[:, q4:q4 + 1], in_=ssq[:, q4:q4 + 1], func=AF.Sqrt,
                                                             bias=g["eps"][:], scale=1.0 / D), deps=[t_ss])
            t_r = ph.op("dve", lambda e, q4=q4: e.reciprocal(out=rsd[:, q4:q4 + 1], in_=ssm[:, q4:q4 + 1]), deps=[t_sd])
            t_h = ph.op("dve", lambda e, i=i, j=j, q4=q4: e.scalar_tensor_tensor(
                out=hb[j][:], in0=xb[i][:], scalar=rsd[:, q4:q4 + 1], in1=gam[:], op0=ALU.mult, op1=ALU.mult),
                deps=[t_r, t_c, hb_free[j]])
            ss_free[q4] = t_h
            xb_free[i] = t_h
            hb_rdy[j] = t_h

    def norm_part2(tt):
        b = tt % 2
        t0 = tt * 512
        evs = []
        for s in range(4):
            j = s
            tpi = st["tpi"] % 2
            st["tpi"] += 1
            last = None
            for k in range(8):
                last = ph.op("pe", lambda e, tpi=tpi, k=k, j=j: e.transpose(out=tp[:, tpi, k, :], in_=hb[j][:, k * 128:(k + 1) * 128],
                                                                           identity=g["ident"][:]),
                             deps=[hb_rdy[j], tp_free[tpi]])
            hb_free[j] = last
            ev = ph.op("act", lambda e, tpi=tpi, b=b, s=s: e.activation(out=hT[b][:, :, s * 128:(s + 1) * 128], in_=tp[:, tpi, :, :], func=AF.Copy),
                       deps=[last, hT_free[b], hT_st[b]])
            tp_free[tpi] = ev
            evs.append(ev)
        hT_rdy[b] = evs[-1]
        hT_st[b] = ph.dma("sp", d["HT"][:, t0:t0 + 512].rearrange("(k p) t -> p k t", p=128), hT[b][:], htsem[b], deps=evs)

    def grp4(c0, kind, name):
        return [(c0 + j, kind, name, j) for j in range(4)]
    plan = []
    plan += [(16, "qk", "GQ", 0)] + grp4(0, "copy", "UH0") + [(20, "qk", "GK", 0)] + grp4(4, "copy", "UH1")
    plan += [(17, "qk", "GQ", 1)] + grp4(8, "copy", "UH2") + grp4(26, "copy", "DQ")
    plan += [(18, "qk", "GQ", 2)] + grp4(30, "copy", "DK") + grp4(12, "silu", "UH3")
    plan += [(19, "qk", "GQ", 3)] + grp4(22, "silu", "GS") + grp4(38, "silu", "DS")
    dests = {"UH0": (d["UH"], 0), "UH1": (d["UH"], 512), "UH2": (d["UH"], 1024), "UH3": (d["UH"], 1536),
             "DQ": (d["DQ"], 0), "DK": (d["DK"], 0), "GS": (d["GS"], 0), "DS": (d["DS"], 0), "GQ": (d["GQ"], 0)}

    norm_part1(0)
    norm_part2(0)
    for tt in range(ntile):
        b = tt % 2
        t0 = tt * 512
        cur = {}
        gevs = {}
        last_pe = None
        deferred = []
        for ci, (c, kind, grp, j) in enumerate(plan):
            if ci == 5 and tt + 1 < ntile:
                norm_part1(tt + 1)
            if ci == 24 and tt + 1 < ntile:
                norm_part2(tt + 1)
            while deferred and deferred[0][0] <= ci:
                deferred.pop(0)[1]()
            if grp not in cur:
                if grp == "GQ":
                    cur[grp] = 3
                elif grp == "GK":
                    cur[grp] = 4
                else:
                    cur[grp] = st["og"] % 3
                    st["og"] += 1
                gevs[grp] = []
            cur_og = cur[grp]
            grp_evs = gevs[grp]
            su = getslot()
            for k in range(8):
                last_pe = ph.op("pe", lambda e, su=su, k=k, c=c, b=b: e.matmul(pg[:, su, :], lhsT=Wb[:, k, c * 128:(c + 1) * 128], rhs=hT[b][:, k, :],
                                                                             start=(k == 0), stop=(k == 7)),
                                deps=[t_w, hT_rdy[b], slot_free[su]])
            mm = last_pe
            o = cur_og
            if kind == "copy":
                ev = ph.op("act", lambda e, su=su, o=o, j=j: e.activation(out=og[o][:, j, :], in_=pg[:, su, :], func=AF.Copy),
                           deps=[mm, og_free[o]])
                slot_free[su] = ev
            elif kind == "silu":
                ev = ph.op("act", lambda e, su=su, o=o, j=j: e.activation(out=og[o][:, j, :], in_=pg[:, su, :], func=AF.Silu),
                           deps=[mm, og_free[o]])
                slot_free[su] = ev
            else:
                q = st["qs"] % 2
                st["qs"] += 1
                lu = lastuse[q]
                gcol = 0 if grp == "GQ" else 1
                t_sq = ph.op("act", lambda e, su=su, q=q: e.activation(out=sqb[q][:], in_=pg[:, su, :], func=AF.Square),
                             deps=[mm, lu.get("sqb")])
                box = {}

                def stepA(su=su, q=q, lu=lu, gcol=gcol, t_sq=t_sq, box=box):
                    sm = getslot()
                    t_ms = ph.op("pe", lambda e: e.matmul(pg[:, sm, :], lhsT=g["onesblk"][:], rhs=sqb[q][:], start=True, stop=True),
                                 deps=[t_sq, slot_free[sm]])
                    lu["sqb"] = t_ms
                    t_sd = ph.op("act", lambda e: e.activation(out=rt[q][:], in_=pg[:, sm, :], func=AF.Sqrt, bias=g["eps"][:], scale=1.0),
                                 deps=[t_ms, lu.get("rt")])
                    slot_free[sm] = t_sd
                    t_rs = ph.op("dve", lambda e: e.reciprocal(out=rt[q][:], in_=rt[q][:]), deps=[t_sd])
                    t_qn = ph.op("dve", lambda e: e.scalar_tensor_tensor(
                        out=qn[q][:], in0=pg[:, su, :], scalar=qg[:, gcol:gcol + 1], in1=rt[q][:], op0=ALU.mult, op1=ALU.mult),
                        deps=[t_rs, t_c, lu.get("qn")])
                    slot_free[su] = t_qn
                    lu["rt"] = t_qn
                    box["t_qn"] = t_qn

                def stepB(q=q, lu=lu, b=b, o=o, j=j, grp=grp, box=box, grp_evs=grp_evs, t0=t0):
                    t_qn = box["t_qn"]
                    sw = getslot()
                    t_sw = ph.op("pe", lambda e: e.matmul(pg[:, sw, :], lhsT=g["pswap"][:], rhs=qn[q][:], start=True, stop=True),
                                 deps=[t_qn, slot_free[sw]])
                    t_1 = ph.op("dve", lambda e: e.tensor_tensor(out=t1b[q][:], in0=qn[q][:], in1=cs[b][:, 0, :], op=ALU.mult),
                                deps=[t_qn, cs_ld[b], lu.get("t1b")])
                    t_2 = ph.op("dve", lambda e: e.tensor_tensor(out=t2b[q][:], in0=pg[:, sw, :], in1=cs[b][:, 1, :], op=ALU.mult),
                                deps=[t_sw, cs_ld[b], lu.get("t2b")])
                    slot_free[sw] = t_2
                    lu["qn"] = t_2
                    cs_free[b] = t_2
                    ev = ph.op("pool", lambda e: e.tensor_tensor(out=og[o][:, j, :], in0=t1b[q][:], in1=t2b[q][:], op=ALU.add),
                               deps=[t_1, t_2, og_free[o]])
                    lu["t1b"] = ev
                    lu["t2b"] = ev
                    grp_evs.append(ev)
                    if grp == "GK":
                        tk = None
                        for kv in range(2):
                            for dup in range(2):
                                tk = ph.dma("sp", d["GK2"][kv, dup * 64:(dup + 1) * 64, t0:t0 + 512], og[o][kv * 64:(kv + 1) * 64, 0, :], ogsem[o], deps=grp_evs)
                        og_free[o] = tk
                    elif j == 3:
                        dst, r0 = dests[grp]
                        og_free[o] = ph.dma("sp", dst[r0:r0 + 512, t0:t0 + 512].rearrange("(j p) t -> p j t", p=128), og[o][:], ogsem[o], deps=grp_evs)

                deferred.append((ci + 2, stepA))
                deferred.append((ci + 6, stepB))
                deferred.sort(key=lambda x: x[0])
                continue
            grp_evs.append(ev)
            if grp == "GK":
                tk = None
                for kv in range(2):
                    for dup in range(2):
                        tk = ph.dma("sp", d["GK2"][kv, dup * 64:(dup + 1) * 64, t0:t0 + 512], og[o][kv * 64:(kv + 1) * 64, 0, :], ogsem[o], deps=grp_evs)
                og_free[o] = tk
            elif j == 3:
                dst, r0 = dests[grp]
                og_free[o] = ph.dma("sp", dst[r0:r0 + 512, t0:t0 + 512].rearrange("(j p) t -> p j t", p=128), og[o][:], ogsem[o], deps=grp_evs)
        while deferred and deferred[0][0] <= len(plan) + 1:
            deferred.pop(0)[1]()
        for s in range(4):
            vb = (tt * 4 + s) % 2
            s1 = getslot()
            for k in range(8):
                last_pe = ph.op("pe", lambda e, s1=s1, k=k, s=s, b=b: e.matmul(pg[:, s1, 0:128], lhsT=hT[b][:, k, s * 128:(s + 1) * 128], rhs=Wb[:, k, 2688:2816],
                                                                             start=(k == 0), stop=(k == 7)),
                                deps=[t_w, hT_rdy[b], slot_free[s1]])
            e1 = ph.op("act", lambda e, s1=s1, vb=vb: e.activation(out=vs[vb][:, 0:128], in_=pg[:, s1, 0:128], func=AF.Copy),
                       deps=[last_pe, vs_free[vb]])
            slot_free[s1] = e1
            s2 = getslot()
            for k in range(8):
                last_pe = ph.op("pe", lambda e, s2=s2, k=k, s=s, b=b: e.matmul(pg[:, s2, :], lhsT=hT[b][:, k, s * 128:(s + 1) * 128], rhs=Wb[:, k, 4352:4864],
                                                                             start=(k == 0), stop=(k == 7)),
                                deps=[slot_free[s2]])
            e2 = ph.op("dve", lambda e, s2=s2, vb=vb: e.tensor_copy(out=vs[vb][:, 128:640], in_=pg[:, s2, :]),
                       deps=[last_pe, vs_free[vb]])
            slot_free[s2] = e2
            r0 = t0 + s * 128
            for kv in range(2):
                ph.dma("sp", d["GV"][kv, r0:r0 + 128, :], vs[vb][:, kv * 64:(kv + 1) * 64], vssem[vb], deps=[e1])
            vs_free[vb] = ph.dma("sp", d["DV"][r0:r0 + 128, :], vs[vb][:, 128:640], vssem[vb], deps=[e2])
        hT_free[b] = last_pe
        while deferred:
            deferred.pop(0)[1]()
    ph.run()


def phase_attn(P, l, mode):
    nc, d, g = P.nc, P.d, P.g
    isD = mode == "D"
    ph = Phase(nc, f"{mode}{l}")
    Lmax = max(SEQS)
    VW = 128 if isD else 64
    NKB = 2
    Kb = [ph.sb(f"K{i}", [128, Lmax], BF16) for i in range(NKB)]
    Vb = [ph.sb(f"V{i}", [128, Lmax // 128, VW], BF16) for i in range(NKB)]
    ksem = [ph.dsem(f"ks{i}") for i in range(NKB)]
    k_free = [None] * NKB
    k_ld = [None] * NKB
    NQB = 3
    Qb = [ph.sb(f"Q{i}", [128, 512], BF16) for i in range(NQB)]
    Gb = [ph.sb(f"Gt{i}", [128, 512], BF16) for i in range(NQB)]
    qsem = [ph.dsem(f"qs{i}") for i in range(NQB)]
    q_free = [None] * NQB
    q_ld = [None] * NQB
    NP = 4
    p_s = [ph.sb(f"p{i}", [128, 2, 512], BF16) for i in range(NP)]
    p_free = [None] * NP
    ps_s = ph.ps("s", [128, 2, 2, 512], F32)
    s_free = [None] * 2
    if isD:
        acc = ph.ps("acc", [128, 4, 512], F32)
        NACC = 1
        gsub = ph.sb("gsub", [128, 1], F32)
        gs0 = ph.sb("gs0", [128, 1], F32)
        cl = ph.dsem("cl")
        t_g0 = ph.dma("sp", gs0[:], d["diff_subln_g"][l:l + 1, :].rearrange("o f -> f o"), cl, slow=True)
        lam_init = 0.8 - 0.6 * math.exp(-0.3 * l)
        t_gs = ph.op("dve", lambda e: e.tensor_scalar(out=gsub[:], in0=gs0[:], scalar1=1.0 - lam_init, scalar2=None, op0=ALU.mult),
                     deps=[t_g0])
        sqd = [ph.sb(f"sqd{i}", [128, 512], BF16) for i in range(2)]
        dcp = [ph.sb(f"dcp{i}", [64, 512], F32) for i in range(2)]
        obA = [ph.sb(f"obA{i}", [128, 512], F32) for i in range(2)]
        obB = [ph.sb(f"obB{i}", [128, 512], F32) for i in range(2)]
        rbD = [ph.sb(f"rbD{i}", [128, 512], F32) for i in range(2)]
    else:
        acc = ph.ps("acc", [128, 2, 2, 512], F32)
        NACC = 2
    acc_free = [None] * NACC
    den_free = [None]
    rb1 = ph.sb("rb1", [128, 512], F32)
    ob1 = ph.sb("ob1", [128, 512], F32)
    NOS = 2
    ost = [ph.sb(f"ost{i}", [128, 512], BF16) for i in range(NOS)]
    osem = [ph.dsem(f"os{i}") for i in range(NOS)]
    o_free = [None] * NOS
    ones = g["ones1"]

    groups = []
    kvsets = []
    for si, L in enumerate(SEQS):
        nsets = 4 if isD else 2
        for a in range(nsets):
            kvsets.append((si, a))
            subs = [a] if isD else [2 * a, 2 * a + 1]
            for hp in subs:
                for qj in range(L // 512):
                    groups.append(dict(si=si, a=a, hp=hp, qj=qj, L=L, off=P.offs[si], ks=len(kvsets) - 1))
    Ksrc = d["DK"] if isD else None
    Qsrc = d["DQ"] if isD else d["GQ"]
    Ssrc = d["DS"] if isD else d["GS"]
    Ydst = d["YD"] if isD else d["YG"]

    def load_kv(ksi):
        si, a = kvsets[ksi]
        L, off = SEQS[si], P.offs[si]
        b = ksi % NKB
        if isD:
            ph.dma("sp", Kb[b][:, 0:L], d["DK"][a * 128:(a + 1) * 128, off:off + L], ksem[b], deps=[k_free[b]])
            src = d["DV"][off:off + L, a * 128:(a + 1) * 128].rearrange("(c p) e -> p c e", p=128)
        else:
            ph.dma("sp", Kb[b][:, 0:L], d["GK2"][a, :, off:off + L], ksem[b], deps=[k_free[b]])
            src = d["GV"][a, off:off + L, :].rearrange("(c p) e -> p c e", p=128)
        k_ld[b] = ph.dma("sp", Vb[b][:, 0:L // 128, :], src, ksem[b], deps=[k_free[b]])

    def load_q(gi):
        grp = groups[gi]
        b = gi % NQB
        r0 = grp["hp"] * 128
        c0 = grp["off"] + grp["qj"] * 512
        ph.dma("sp", Qb[b][:], Qsrc[r0:r0 + 128, c0:c0 + 512], qsem[b], deps=[q_free[b]])
        q_ld[b] = ph.dma("sp", Gb[b][:], Ssrc[r0:r0 + 128, c0:c0 + 512], qsem[b], deps=[q_free[b]])

    tiles = []
    for gi, grp in enumerate(groups):
        nkc = grp["L"] // 128
        for kc in range(nkc):
            tiles.append((gi, kc, nkc))
    nt = len(tiles)
    qk_tok = [None] * nt
    exp_tok = [None] * nt
    state = {"last_av": None}

    def emit_qk(i):
        gi, kc, nkc = tiles[i]
        grp = groups[gi]
        if kc == 0:
            flush_group(gi - NQB)
        kb = grp["ks"] % NKB
        qb = gi % NQB
        sb_ = i % 2
        near = False
        if isD:
            o = 128 * kc - 512 * grp["qj"]
            near = -256 < o < 640
        ph.op("pe", lambda e: e.matmul(ps_s[:, sb_, 0, :], lhsT=Kb[kb][0:64, kc * 128:(kc + 1) * 128], rhs=Qb[qb][0:64, :],
                                       start=True, stop=not near, tile_position=(0, 0)),
              deps=[k_ld[kb], q_ld[qb], s_free[sb_]])
        qk_tok[i] = ph.op("pe", lambda e: e.matmul(ps_s[:, sb_, 1, :], lhsT=Kb[kb][64:128, kc * 128:(kc + 1) * 128], rhs=Qb[qb][64:128, :],
                                                   start=True, stop=not near, tile_position=(64, 0)))
        if near:
            brhs = g["wstrip"][:, grp["a"], 512 - o:1024 - o]
            ph.op("pe", lambda e: e.matmul(ps_s[:, sb_, 0, :], lhsT=g["ident"][:], rhs=brhs, start=False, stop=True))
            qk_tok[i] = ph.op("pe", lambda e: e.matmul(ps_s[:, sb_, 1, :], lhsT=g["ident"][:], rhs=brhs, start=False, stop=True))

    def emit_exp(i):
        gi, kc, nkc = tiles[i]
        grp = groups[gi]
        sb_ = i % 2
        pb = i % NP
        if isD:
            h = grp["a"]
            o = 128 * kc - 512 * grp["qj"]
            if o <= -256 or o >= 640:
                side = 0 if o <= -256 else 1
                exp_tok[i] = ph.op("act", lambda e: e.activation(out=p_s[pb][:], in_=ps_s[:, sb_, :, :], func=AF.Exp,
                                                                bias=g["farb"][:, h, side:side + 1], scale=0.125),
                                   deps=[qk_tok[i], p_free[pb]])
                s_free[sb_] = exp_tok[i]
            else:
                exp_tok[i] = ph.op("act", lambda e: e.activation(out=p_s[pb][:], in_=ps_s[:, sb_, :, :], func=AF.Exp, scale=0.125),
                                   deps=[qk_tok[i], p_free[pb]])
                s_free[sb_] = exp_tok[i]
        else:
            exp_tok[i] = ph.op("act", lambda e: e.activation(out=p_s[pb][:], in_=ps_s[:, sb_, :, :], func=AF.Exp, scale=0.125),
                               deps=[qk_tok[i], p_free[pb]])
            s_free[sb_] = exp_tok[i]

    def emit_den(i):
        gi, kc, nkc = tiles[i]
        pb = i % NP
        first, last = kc == 0, kc == nkc - 1
        ph.op("pe", lambda e: e.matmul(acc[0:32, 2, :], lhsT=ones[:, 0:32], rhs=p_s[pb][:, 0, :], start=first, stop=last,
                                       tile_position=(0, 0)), deps=[exp_tok[i], den_free[0] if first else None])
        t = ph.op("pe", lambda e: e.matmul(acc[32:64, 2, :], lhsT=ones[:, 0:32], rhs=p_s[pb][:, 1, :], start=first, stop=last,
                                           tile_position=(0, 32)))
        p_free[pb] = t
        return t

    def emit_av(i):
        gi, kc, nkc = tiles[i]
        grp = groups[gi]
        kb = grp["ks"] % NKB
        pb = i % NP
        a_ = gi % NACC
        first, last = kc == 0, kc == nkc - 1
        deps = [exp_tok[i], acc_free[a_] if first else None]
        if isD:
            if i > 0 and not first:
                emit_den(i - 1)
            ph.op("pe", lambda e: e.matmul(acc[:, 0, :], lhsT=Vb[kb][:, kc, :], rhs=p_s[pb][:, 0, :], start=first, stop=last), deps=deps)
            t = ph.op("pe", lambda e: e.matmul(acc[:, 1, :], lhsT=Vb[kb][:, kc, :], rhs=p_s[pb][:, 1, :], start=first, stop=last))
            if last:
                t = emit_den(i)
        else:
            ph.op("pe", lambda e: e.matmul(acc[0:64, a_, 0, :], lhsT=Vb[kb][:, kc, :], rhs=p_s[pb][:, 0, :], start=first, stop=last,
                                           tile_position=(0, 0)), deps=deps)
            ph.op("pe", lambda e: e.matmul(acc[64:128, a_, 0, :], lhsT=Vb[kb][:, kc, :], rhs=p_s[pb][:, 1, :], start=first, stop=last,
                                           tile_position=(0, 64)))
            ph.op("pe", lambda e: e.matmul(acc[0:64, a_, 1, :], lhsT=ones[:, 0:64], rhs=p_s[pb][:, 0, :], start=first, stop=last,
                                           tile_position=(0, 0)))
            t = ph.op("pe", lambda e: e.matmul(acc[64:128, a_, 1, :], lhsT=ones[:, 0:64], rhs=p_s[pb][:, 1, :], start=first, stop=last,
                                               tile_position=(0, 64)))
        if not isD:
            p_free[pb] = t
        state["last_av"] = t
        return t

    pending = {}
    es_last = [None, None]
    sp_state = {"free": None}

    def flush_group(gq):
        for (_due, fn) in pending.pop(gq, []):
            fn()

    def run_due(i):
        for gq in sorted(pending.keys()):
            lst = pending[gq]
            while lst and lst[0][0] <= i:
                lst.pop(0)[1]()
            if not lst:
                pending.pop(gq)

    def finish_group(gi, t11):
        grp = groups[gi]
        qb = gi % NQB
        osl = gi % NOS
        r0 = grp["hp"] * 128
        c0 = grp["off"] + grp["qj"] * 512
        q_free[qb] = t11
        o_free[osl] = ph.dma("sp", Ydst[r0:r0 + 128, c0:c0 + 512], ost[osl][:], osem[osl], deps=[t11])
        if gi + NQB < len(groups):
            load_q(gi + NQB)

    def epilogue(gi, t_last, i_tile):
        grp = groups[gi]
        a_ = gi % NACC
        qb = gi % NQB
        osl = gi % NOS
        if isD:
            es_ = gi % 2
            flush_group(gi - 2)
            oA, oB, dc, sq_, rD = obA[es_], obB[es_], dcp[es_], sqd[es_], rbD[es_]
            c1 = ph.op("dve", lambda e: e.tensor_copy(out=oA[:], in_=acc[:, 0, :]), deps=[t_last, es_last[es_]])
            c2 = ph.op("dve", lambda e: e.tensor_copy(out=oB[:], in_=acc[:, 1, :]), deps=[t_last])
            c3 = ph.op("dve", lambda e: e.tensor_copy(out=dc[:], in_=acc[0:64, 2, :]), deps=[t_last])
            acc_free[a_] = c3
            den_free[0] = c3
            tr = ph.op("dve", lambda e: e.reciprocal(out=dc[:], in_=dc[:]), deps=[c3])
            stt = {}

            def step1():
                b1 = ph.op("pe", lambda e: e.matmul(acc[:, 3, :], lhsT=g["onesf"][0:1, :], rhs=dc[0:1, :], start=True, stop=True),
                           deps=[tr, sp_state["free"]])
                stt["o1"] = ph.op("dve", lambda e: e.tensor_tensor(out=oA[:], in0=oA[:], in1=acc[:, 3, :], op=ALU.mult), deps=[b1, c1])
                sp_state["free"] = stt["o1"]

            def step2():
                b2 = ph.op("pe", lambda e: e.matmul(acc[:, 3, :], lhsT=g["onesf"][32:33, :], rhs=dc[32:33, :], start=True, stop=True),
                           deps=[tr, sp_state["free"]])
                o2 = ph.op("dve", lambda e: e.tensor_tensor(out=oB[:], in0=oB[:], in1=acc[:, 3, :], op=ALU.mult), deps=[b2, c2])
                sp_state["free"] = o2
                t5 = ph.op("dve", lambda e: e.scalar_tensor_tensor(out=oA[:], in0=oB[:], scalar=g["lamt"][:, l, 0:1], in1=oA[:],
                                                                  op0=ALU.mult, op1=ALU.add), deps=[o2, stt["o1"]])
                stt["t5"] = t5
                stt["t6"] = ph.op("act", lambda e: e.activation(out=sq_[:], in_=oA[:], func=AF.Square), deps=[t5])

            def step3():
                t7 = ph.op("pe", lambda e: e.matmul(acc[:, 3, :], lhsT=g["ones128"][:], rhs=sq_[:], start=True, stop=True),
                           deps=[stt["t6"], sp_state["free"]])
                t8 = ph.op("act", lambda e: e.activation(out=rD[:], in_=acc[:, 3, :], func=AF.Sqrt, bias=g["eps"][:], scale=1.0), deps=[t7])
                sp_state["free"] = t8
                t9 = ph.op("dve", lambda e: e.reciprocal(out=rD[:], in_=rD[:]), deps=[t8])
                t10 = ph.op("dve", lambda e: e.scalar_tensor_tensor(out=oA[:], in0=oA[:], scalar=gsub[:, 0:1], in1=rD[:],
                                                                   op0=ALU.mult, op1=ALU.mult), deps=[t9, t_gs, stt["t5"]])
                t11 = ph.op("dve", lambda e: e.tensor_tensor(out=ost[osl][:], in0=oA[:], in1=Gb[qb][:], op=ALU.mult),
                            deps=[t10, q_ld[qb], o_free[osl]])
                es_last[es_] = t11
                finish_group(gi, t11)

            pending[gi] = [(i_tile + 5, step1), (i_tile + 7, step2), (i_tile + 10, step3)]
        else:
            t1 = ph.op("dve", lambda e: e.reciprocal(out=rb1[:], in_=acc[:, a_, 1, :]), deps=[t_last])
            t3 = ph.op("dve", lambda e: e.tensor_tensor(out=ob1[:], in0=acc[:, a_, 0, :], in1=rb1[:], op=ALU.mult), deps=[t1])
            acc_free[a_] = t3
            t11 = ph.op("dve", lambda e: e.tensor_tensor(out=ost[osl][:], in0=ob1[:], in1=Gb[qb][:], op=ALU.mult),
                        deps=[t3, q_ld[qb], o_free[osl]])
            finish_group(gi, t11)

    load_kv(0)
    for gq in range(min(NQB, len(groups))):
        load_q(gq)
    LA = 1 if isD else 2
    for i0 in range(min(LA, nt)):
        emit_qk(i0)
    for i in range(nt):
        gi, kc, nkc = tiles[i]
        grp = groups[gi]
        if kc == 0:
            if (gi == 0 or groups[gi - 1]["ks"] != grp["ks"]) and grp["ks"] + 1 < len(kvsets):
                load_kv(grp["ks"] + 1)
        emit_exp(i)
        if i + LA < nt:
            emit_qk(i + LA)
        t = emit_av(i)
        run_due(i)
        if kc == nkc - 1:
            if gi + 1 >= len(groups) or groups[gi + 1]["ks"] != grp["ks"]:
                k_free[grp["ks"] % NKB] = t
            epilogue(gi, t, i)
    for gq in sorted(pending.keys()):
        flush_group(gq)
    ph.run()


def phase_H1(P, l):
    nc, d, g = P.nc, P.d, P.g
    ph = Phase(nc, f"H1{l}")
    cw = ph.sb("cw", [128, 12, 4], F32)
    cl = ph.dsem("cl")
    for j in range(3):
        ph.dma("sp", cw[:, :, j:j + 1], d["hy_conv_w"][l, j:j + 1, :].rearrange("o (c p) -> p c o", p=128), cl, slow=True)
    t_cw = ph.dma("sp", cw[:, :, 3:4], d["hy_conv_b"][l:l + 1, :].rearrange("o (c p) -> p c o", p=128), cl, slow=True)
    BWmax = min(2048, max(SEQS))
    NB = 2
    U = [[ph.sb(f"U{b}_{j}", [128, BWmax + 2], BF16) for j in range(3)] for b in range(NB)]
    SG = [ph.sb(f"SG{b}", [128, BWmax], BF16) for b in range(NB)]
    usem = [ph.dsem(f"us{b}") for b in range(NB)]
    u_free = [None] * NB
    x1c = [ph.sb(f"x1c{i}", [128, 512], F32) for i in range(2)]
    x1c_free = [None] * 2
    hvb = ph.sb("hvb", [128, BWmax], BF16)
    gb = ph.sb("gb", [128, BWmax], BF16)
    gsem = ph.dsem("gs")
    hvT = ph.sb("hvT", [128, BWmax // 128, 128], BF16)
    hsem = ph.dsem("hs")
    tpp = ph.ps("tpp", [128, BWmax // 128, 128], BF16)
    pc = ph.ps("pc", [128, 2, 3, 512], F32)
    pc_free = [[None] * 3 for _ in range(2)]
    dg = ph.sb("dg", [128, 12, 3, 128], BF16)
    t_dg = None
    for ch in range(12):
        for j in range(3):
            t_dg = ph.op("dve", lambda e, ch=ch, j=j: e.tensor_scalar(out=dg[:, ch, j, :], in0=g["ident"][:], scalar1=cw[:, ch, j:j + 1],
                                                                     scalar2=None, op0=ALU.mult), deps=[t_cw])
    blocks = []
    for si, L in enumerate(SEQS):
        BW = min(2048, L)
        for cc in range(4):
            for c0 in range(0, L, BW):
                blocks.append((si, L, P.offs[si], cc, c0, BW))
    ld_tok = [None] * NB
    ms_tok = [None] * NB

    def load(bi):
        si, L, off, cc, c0, BW = blocks[bi]
        b = bi % NB
        lo = 1 if c0 == 0 else 0
        hi = BW + 1 if c0 + BW == L else BW + 2
        mt = None
        for j in range(3):
            row0 = (j * 4 + cc) * 128
            if lo == 1:
                mt = ph.op("pool", lambda e, b=b, j=j: e.memset(U[b][j][:, 0:1], 0.0), deps=[u_free[b]])
            if hi == BW + 1:
                mt = ph.op("pool", lambda e, b=b, j=j, BW=BW: e.memset(U[b][j][:, BW + 1:BW + 2], 0.0), deps=[u_free[b]])
            ph.dma("sp", U[b][j][:, lo:hi], d["UH"][row0:row0 + 128, off + c0 - 1 + lo:off + c0 - 1 + hi], usem[b], deps=[u_free[b]])
        row0 = (12 + cc) * 128
        ld_tok[b] = ph.dma("sp", SG[b][:, 0:BW], d["UH"][row0:row0 + 128, off + c0:off + c0 + BW], usem[b], deps=[u_free[b]])
        ms_tok[b] = mt

    g_st = None
    h_st = None
    tp_free = None
    load(0)
    for bi, (si, L, off, cc, c0, BW) in enumerate(blocks):
        b = bi % NB
        if bi + 1 < len(blocks):
            load(bi + 1)
        t_hv = None
        t_g = None
        for ct in range(BW // 512):
            pb_ = (bi * 4 + ct) % 2
            c0c = ct * 512
            mm = []
            for j in range(3):
                ch = j * 4 + cc
                for tap in range(3):
                    t = ph.op("pe", lambda e, pb_=pb_, j=j, ch=ch, tap=tap, b=b, c0c=c0c: e.matmul(
                        pc[:, pb_, j, :], lhsT=dg[:, ch, tap, :], rhs=U[b][j][:, c0c + tap:c0c + tap + 512], start=(tap == 0), stop=(tap == 2)),
                        deps=[t_dg, ld_tok[b], ms_tok[b], pc_free[pb_][j]])
                mm.append(t)
            xi = (bi * 4 + ct) % 2
            t_x1 = ph.op("act", lambda e, pb_=pb_, xi=xi, cc=cc: e.activation(out=x1c[xi][:], in_=pc[:, pb_, 1, :], func=AF.Identity,
                                                                             bias=cw[:, 4 + cc, 3:4], scale=1.0),
                         deps=[mm[1], x1c_free[xi]])
            pc_free[pb_][1] = t_x1
            t_hv = ph.op("dve", lambda e, pb_=pb_, xi=xi, cc=cc, c0c=c0c: e.scalar_tensor_tensor(
                out=hvb[:, c0c:c0c + 512], in0=pc[:, pb_, 2, :], scalar=cw[:, 8 + cc, 3:4], in1=x1c[xi][:], op0=ALU.add, op1=ALU.mult),
                deps=[mm[2], t_x1, tp_free])
            pc_free[pb_][2] = t_hv
            x1c_free[xi] = t_hv
            t_g = ph.op("dve", lambda e, pb_=pb_, cc=cc, b=b, c0c=c0c: e.scalar_tensor_tensor(
                out=gb[:, c0c:c0c + 512], in0=pc[:, pb_, 0, :], scalar=cw[:, cc, 3:4], in1=SG[b][:, c0c:c0c + 512], op0=ALU.add, op1=ALU.mult),
                deps=[mm[0], g_st])
            pc_free[pb_][0] = t_g
        u_free[b] = t_g
        g_st = ph.dma("sp", d["GG"][cc * 128:(cc + 1) * 128, off + c0:off + c0 + BW], gb[:, 0:BW], gsem, deps=[t_g])
        ns = BW // 128
        last = None
        for s in range(ns):
            last = ph.op("pe", lambda e, s=s: e.transpose(out=tpp[:, s, :], in_=hvb[:, s * 128:(s + 1) * 128], identity=g["ident"][:]),
                         deps=[t_hv, h_ev if s == 0 and bi > 0 else None])
        tp_free = last
        h_ev = ph.op("act", lambda e, ns=ns: e.activation(out=hvT[:, 0:ns, :], in_=tpp[:, 0:ns, :], func=AF.Copy), deps=[last, h_st])
        h_st = ph.dma("sp", d["HV"][off + c0:off + c0 + BW, cc * 128:(cc + 1) * 128].rearrange("(s p) c -> p s c", p=128),
                      hvT[:, 0:ns, :], hsem, deps=[h_ev])
    ph.run()


def phase_H2(P, l, si):
    nc, d, g = P.nc, P.d, P.g
    L, off = SEQS[si], P.offs[si]
    N1, N2 = fft_dims(L)
    H1 = N1 // 2
    key = f"{L}"
    if f"AD{key}_0" not in d:
        for ri in range(2):
            P.scr(f"AD{key}_{ri}", [N2, N1, HW], BF16)
            P.scr(f"DD{key}_{ri}", [N1, N2, HW], BF16)
    AD = [d[f"AD{key}_{ri}"] for ri in range(2)]
    DD = [d[f"DD{key}_{ri}"] for ri in range(2)]
    Hd = [d[f"H{key}_{ri}"] for ri in range(2)]

    ph = Phase(nc, f"h2a{l}_{si}")
    fft_stage1(ph, P, L, [d["HV"][off:off + L, :]], [[0]], [AD], "a")
    ph.run()

    ph = Phase(nc, f"h2b{l}_{si}")
    f2, t_f2 = fft_stage2_tables(ph, P, L, "b")
    G = FG
    NB = 2
    xin = [ph.sb(f"xin{b}", [N2, 2, G, HW], BF16) for b in range(NB)]
    hin = [ph.sb(f"hin{b}", [N2, 2, G, HW], BF16) for b in range(NB)]
    xsem = [ph.dsem(f"xs{b}") for b in range(NB)]
    dst = [ph.sb(f"dst{b}", [N2, 2, G, HW], BF16) for b in range(NB)]
    dsem_ = [ph.dsem(f"ds{b}") for b in range(NB)]
    NT = 3
    tq = [ph.sb(f"tq{q}", [N2, 4, HW], BF16) for q in range(NT)]
    y_free = [None] * NT
    NS = 8
    pp = ph.ps("pp", [N2, NS, HW], F32)
    pp_free = [None] * NS
    x_free = [None] * NB
    d_free = [None] * NB
    x_ld = [None] * NB
    st = {"slot": 0, "q": 0}

    def getslot():
        s = st["slot"] % NS
        st["slot"] += 1
        return s

    def load(gi):
        b = gi % NB
        k0 = gi * G
        for ri in range(2):
            ph.dma("sp", xin[b][:, ri, :, :], AD[ri][:, k0:k0 + G, :], xsem[b], deps=[x_free[b]])
        for ri in range(2):
            x_ld[b] = ph.dma("sp", hin[b][:, ri, :, :], Hd[ri][:, k0:k0 + G, :], xsem[b], deps=[x_free[b]])

    ngrp = N1 // G
    items = [(gi, kl) for gi in range(ngrp) for kl in range(G)]
    f2tok = {}

    def emit_f2(i):
        gi, kl = items[i]
        b = gi % NB
        sr, si_ = getslot(), getslot()
        ph.op("pe", lambda e: e.matmul(pp[:, sr, :], lhsT=f2[:, 0, :], rhs=xin[b][:, 0, kl, :], start=True, stop=False),
              deps=[t_f2, x_ld[b], pp_free[sr]])
        t_br = ph.op("pe", lambda e: e.matmul(pp[:, sr, :], lhsT=f2[:, 1, :], rhs=xin[b][:, 1, kl, :], start=False, stop=True))
        ph.op("pe", lambda e: e.matmul(pp[:, si_, :], lhsT=f2[:, 0, :], rhs=xin[b][:, 1, kl, :], start=True, stop=False),
              deps=[pp_free[si_]])
        t_bi = ph.op("pe", lambda e: e.matmul(pp[:, si_, :], lhsT=f2[:, 2, :], rhs=xin[b][:, 0, kl, :], start=False, stop=True))
        f2tok[i] = (sr, si_, t_br, t_bi)

    load(0)
    emit_f2(0)
    evs = []
    for i, (gi, kl) in enumerate(items):
        b = gi % NB
        if kl == 0:
            evs = []
            if gi + 1 < ngrp:
                load(gi + 1)
        if i + 1 < len(items):
            emit_f2(i + 1)
        sr, si_, t_br, t_bi = f2tok.pop(i)
        q = i % NT
        m1 = ph.op("dve", lambda e, q=q, sr=sr, b=b, kl=kl: e.tensor_tensor(out=tq[q][:, 0, :], in0=pp[:, sr, :], in1=hin[b][:, 0, kl, :], op=ALU.mult),
                   deps=[t_br, x_ld[b], y_free[q]])
        m2 = ph.op("dve", lambda e, q=q, si_=si_, b=b, kl=kl: e.tensor_tensor(out=tq[q][:, 1, :], in0=pp[:, si_, :], in1=hin[b][:, 1, kl, :], op=ALU.mult),
                   deps=[t_bi])
        m3 = ph.op("dve", lambda e, q=q, sr=sr, b=b, kl=kl: e.tensor_tensor(out=tq[q][:, 2, :], in0=pp[:, sr, :], in1=hin[b][:, 1, kl, :], op=ALU.mult))
        m4 = ph.op("dve", lambda e, q=q, si_=si_, b=b, kl=kl: e.tensor_tensor(out=tq[q][:, 3, :], in0=pp[:, si_, :], in1=hin[b][:, 0, kl, :], op=ALU.mult))
        pp_free[sr] = m3
        pp_free[si_] = m4
        dr, di = getslot(), getslot()
        for n_, (tbl, src) in enumerate([(0, 0), (3, 1), (2, 2), (2, 3)]):
            t_dr = ph.op("pe", lambda e, dr=dr, q=q, tbl=tbl, src=src, n_=n_: e.matmul(pp[:, dr, :], lhsT=f2[:, tbl, :], rhs=tq[q][:, src, :],
                                                                                   start=(n_ == 0), stop=(n_ == 3)),
                         deps=[m4, pp_free[dr]] if n_ == 0 else [])
        for n_, (tbl, src) in enumerate([(0, 2), (0, 3), (1, 0), (2, 1)]):
            t_di = ph.op("pe", lambda e, di=di, q=q, tbl=tbl, src=src, n_=n_: e.matmul(pp[:, di, :], lhsT=f2[:, tbl, :], rhs=tq[q][:, src, :],
                                                                                   start=(n_ == 0), stop=(n_ == 3)),
                         deps=[pp_free[di]] if n_ == 0 else [])
        y_free[q] = t_di
        e1 = ph.op("act", lambda e, dr=dr, b=b, kl=kl: e.activation(out=dst[b][:, 0, kl, :], in_=pp[:, dr, :], func=AF.Copy),
                   deps=[t_dr, d_free[b]])
        e2 = ph.op("act", lambda e, di=di, b=b, kl=kl: e.activation(out=dst[b][:, 1, kl, :], in_=pp[:, di, :], func=AF.Copy),
                   deps=[t_di, d_free[b]])
        pp_free[dr] = e1
        pp_free[di] = e2
        evs += [e1, e2]
        if kl == G - 1:
            x_free[b] = m4
            t = None
            for ri in range(2):
                t = ph.dma("sp", DD[ri][gi * G:(gi + 1) * G, :, :].rearrange("k n c -> n k c"), dst[b][:, ri, :, :], dsem_[b], deps=evs)
            d_free[b] = t
    ph.run()

    ph = Phase(nc, f"h2c{l}_{si}")
    ic = ph.sb("ic", [N1, N2, H1], BF16)
    isn = ph.sb("isn", [N1, N2, H1], BF16)
    ldt = ph.dsem("ldt")
    ph.dma("sp", ic[:], d[f"c_i2c{L}"][:], ldt)
    t_tab = ph.dma("sp", isn[:], d[f"c_i2s{L}"][:], ldt)
    yc = ph.sb("yc", [128, 4, L], BF16)
    dd = [ph.sb(f"dd{b}", [N1, 2, G, HW], BF16) for b in range(NB)]
    ddsem = [ph.dsem(f"dds{b}") for b in range(NB)]
    dd_free = [None] * NB
    dd_ld = [None] * NB
    NZ = 4
    pz = ph.ps("pz", [128, NZ, 512], F32)
    pz_free = [None] * NZ
    zc = 0

    def load2(gi):
        b = gi % NB
        for ri in range(2):
            dd_ld[b] = ph.dma("sp", dd[b][:, ri, :, :], DD[ri][:, gi * G:(gi + 1) * G, :], ddsem[b], deps=[dd_free[b]])

    ngrp = N2 // G
    load2(0)
    evs = []
    for gi in range(ngrp):
        b = gi % NB
        if gi + 1 < ngrp:
            load2(gi + 1)
        last = None
        for cc in range(4):
            z = zc % NZ
            zc += 1
            for n2l in range(G):
                n2 = gi * G + n2l
                ph.op("pe", lambda e, z=z, n2l=n2l, n2=n2, b=b, cc=cc: e.matmul(pz[:, z, n2l * H1:(n2l + 1) * H1], lhsT=dd[b][:, 0, n2l, cc * 128:(cc + 1) * 128],
                                                                              rhs=ic[:, n2, :], start=True, stop=False),
                      deps=[t_tab, dd_ld[b], pz_free[z]])
                last = ph.op("pe", lambda e, z=z, n2l=n2l, n2=n2, b=b, cc=cc: e.matmul(pz[:, z, n2l * H1:(n2l + 1) * H1], lhsT=dd[b][:, 1, n2l, cc * 128:(cc + 1) * 128],
                                                                                     rhs=isn[:, n2, :], start=False, stop=True))
            dstv = yc[:, cc, :].rearrange("p (a n) -> p n a", n=N2)[:, gi * G:(gi + 1) * G, :]
            if cc % 2 == 0:
                ev = ph.op("act", lambda e, z=z, dstv=dstv: e.activation(out=dstv, in_=pz[:, z, 0:G * H1].rearrange("p (g a) -> p g a", a=H1), func=AF.Copy), deps=[last])
            else:
                ev = ph.op("dve", lambda e, z=z, dstv=dstv: e.tensor_copy(out=dstv, in_=pz[:, z, 0:G * H1].rearrange("p (g a) -> p g a", a=H1)), deps=[last])
            pz_free[z] = ev
            evs.append(ev)
        dd_free[b] = last
    ysem = ph.dsem("ys")
    for cc in range(4):
        ph.dma("sp", d["YC"][cc * 128:(cc + 1) * 128, off:off + L], yc[:, cc, :], ysem, deps=evs[-8:])
    ph.run()


def phase_M(P, l):
    nc, d, g = P.nc, P.d, P.g
    T = P.T
    last_layer = l == DEPTH - 1
    xsrc = d["x"] if l == 0 else d["XR"]
    xdst = d["y"] if last_layer else d["XR"]
    ph = Phase(nc, f"M{l}")
    Wm = ph.sb("Wm", [128, 8, 3 * D], BF16)
    Wbr = ph.sb("Wbr", [128, 3, 4, D], BF16)
    Wo = ph.sb("Wo", [128, 8, D], BF16)
    bm = ph.sb("bm", [128, 24], F32)
    fg = ph.sb("fg", [128, D], F32)
    wl = P.gsem[0]
    t_w = None
    for k in range(8):
        t_w = ph.dma("pool", Wm[:, k, :], d["w_merge"][l, k * 128:(k + 1) * 128, :], wl)
    for j, nm in enumerate(["w_branch_hy", "w_branch_gqa", "w_branch_diff"]):
        for k in range(4):
            t_w = ph.dma("pool", Wbr[:, j, k, :], d[nm][l, k * 128:(k + 1) * 128, :], wl)
    for k in range(8):
        t_w = ph.dma("pool", Wo[:, k, :], d["w_out"][l, k * 128:(k + 1) * 128, :], wl)
    cl = ph.dsem("cl")
    ph.dma("sp", bm[:], d["b_merge"][l:l + 1, :].rearrange("o (j p) -> p (o j)", p=128), cl, slow=True)
    t_c = ph.dma("sp", fg[:], bcast_rows(d["final_g"].rearrange("(o f) -> o f", o=1), 128), cl)

    NB = 2
    hT = [ph.sb(f"hT{b}", [128, 8, 512], BF16) for b in range(NB)]
    Y = [ph.sb(f"Y{b}", [128, 4, 4, 512], BF16) for b in range(NB)]
    isem = [ph.dsem(f"is{b}") for b in range(NB)]
    in_free = [None] * NB
    in_ld = [None] * NB
    xt = [ph.sb(f"xt{s}", [128, D], F32) for s in range(4)]
    xsem = [ph.dsem(f"xs{s}") for s in range(4)]
    x_free = [None] * 4
    x_ld = [None] * 4
    gt = [ph.sb(f"gt{j}", [128, 512], F32) for j in range(3)]
    gt_free = [None] * 3
    acc = ph.sb("acc", [128, 512], F32)
    tm1 = ph.sb("tm1", [128, 512], F32)
    tm2 = ph.sb("tm2", [128, 512], F32)
    mg = ph.sb("mg", [128, 8, 512], BF16)
    mg_free = None
    xo = [ph.sb(f"xo{i}", [128, D], F32) for i in range(2)]
    xosem = [ph.dsem(f"xos{i}") for i in range(2)]
    xo_free = [None] * 2
    junk = ph.sb("junk", [128, D], BF16)
    ssq = ph.sb("ssq", [128, 2], F32)
    NS = 7
    pg = ph.ps("pg", [128, NS, 512], F32)
    slot_free = [None] * NS
    st = {"slot": 0, "xo": 0}

    def getslot():
        s = st["slot"] % NS
        st["slot"] += 1
        return s

    ntile = T // 512

    def load_in(tt):
        b = tt % NB
        t0 = tt * 512
        ph.dma("sp", hT[b][:], d["HT"][:, t0:t0 + 512].rearrange("(k p) t -> p k t", p=128), isem[b], deps=[in_free[b]])
        for j, nm in enumerate(["YC", "GG", "YG", "YD"]):
            in_ld[b] = ph.dma("sp", Y[b][:, j, :, :], d[nm][:, t0:t0 + 512].rearrange("(k p) t -> p k t", p=128), isem[b], deps=[in_free[b]])

    load_in(0)
    a_rd = None
    for tt in range(ntile):
        b = tt % NB
        t0 = tt * 512
        if tt + 1 < ntile:
            load_in(tt + 1)
        for s in range(4):
            x_ld[s] = ph.dma("sp", xt[s][:], xsrc[t0 + s * 128:t0 + (s + 1) * 128, :], xsem[s], deps=[x_free[s]])
        t_yh = ph.op("dve", lambda e, b=b: e.tensor_tensor(out=Y[b][:, 0, :, :], in0=Y[b][:, 0, :, :], in1=Y[b][:, 1, :, :], op=ALU.mult),
                     deps=[in_ld[b]])
        ysel = [0, 2, 3]
        last_pe = None
        for m in range(8):
            tg = []
            for j in range(3):
                su = getslot()
                for k in range(8):
                    last_pe = ph.op("pe", lambda e, su=su, k=k, j=j, m=m, b=b: e.matmul(pg[:, su, :], lhsT=Wm[:, k, j * D + m * 128:j * D + (m + 1) * 128],
                                                                                      rhs=hT[b][:, k, :], start=(k == 0), stop=(k == 7)),
                                    deps=[t_w, in_ld[b], slot_free[su]])
                t = ph.op("act", lambda e, su=su, j=j, m=m: e.activation(out=gt[j][:], in_=pg[:, su, :], func=AF.Sigmoid,
                                                                       bias=bm[:, j * 8 + m:j * 8 + m + 1], scale=1.0),
                          deps=[last_pe, gt_free[j], t_c])
                slot_free[su] = t
                tg.append(t)
            tb = []
            sb_ = []
            for j in range(3):
                su = getslot()
                sb_.append(su)
                for k in range(4):
                    last_pe = ph.op("pe", lambda e, su=su, k=k, j=j, m=m, b=b: e.matmul(pg[:, su, :], lhsT=Wbr[:, j, k, m * 128:(m + 1) * 128],
                                                                                      rhs=Y[b][:, ysel[j], k, :], start=(k == 0), stop=(k == 3)),
                                    deps=[t_yh, slot_free[su]])
                tb.append(last_pe)
            d0 = ph.op("dve", lambda e, su=sb_[0]: e.tensor_tensor(out=acc[:], in0=pg[:, su, :], in1=gt[0][:], op=ALU.mult),
                       deps=[tb[0], tg[0], a_rd])
            d1 = ph.op("dve", lambda e, su=sb_[1]: e.tensor_tensor(out=tm1[:], in0=pg[:, su, :], in1=gt[1][:], op=ALU.mult),
                       deps=[tb[1], tg[1]])
            d2 = ph.op("dve", lambda e, su=sb_[2]: e.tensor_tensor(out=tm2[:], in0=pg[:, su, :], in1=gt[2][:], op=ALU.mult),
                       deps=[tb[2], tg[2]])
            slot_free[sb_[0]] = d0
            slot_free[sb_[1]] = d1
            slot_free[sb_[2]] = d2
            gt_free[0] = d0
            gt_free[1] = d1
            gt_free[2] = d2
            p1 = ph.op("pool", lambda e: e.tensor_tensor(out=acc[:], in0=acc[:], in1=tm1[:], op=ALU.add), deps=[d0, d1])
            p2 = ph.op("pool", lambda e, m=m: e.tensor_tensor(out=mg[:, m, :], in0=acc[:], in1=tm2[:], op=ALU.add), deps=[p1, d2, mg_free])
            a_rd = p2
        in_free[b] = last_pe
        mg_rdy = a_rd
        for s in range(4):
            xi = st["xo"] % 2
            st["xo"] += 1
            tr = []
            for hh in range(2):
                su = getslot()
                for k in range(8):
                    last_pe = ph.op("pe", lambda e, su=su, k=k, s=s, hh=hh: e.matmul(pg[:, su, :], lhsT=mg[:, k, s * 128:(s + 1) * 128],
                                                                                   rhs=Wo[:, k, hh * 512:(hh + 1) * 512], start=(k == 0), stop=(k == 7)),
                                    deps=[mg_rdy, slot_free[su]])
                t = ph.op("dve", lambda e, su=su, s=s, hh=hh, xi=xi: e.tensor_tensor(out=xo[xi][:, hh * 512:(hh + 1) * 512], in0=pg[:, su, :],
                                                                                   in1=xt[s][:, hh * 512:(hh + 1) * 512], op=ALU.add),
                          deps=[last_pe, x_ld[s], xo_free[xi]])
                slot_free[su] = t
                tr.append(t)
            x_free[s] = tr[-1]
            r0 = t0 + s * 128
            if last_layer:
                c = xi
                t_ss = ph.op("act", lambda e, xi=xi, c=c: e.activation(out=junk[:], in_=xo[xi][:], func=AF.Square, accum_out=ssq[:, c:c + 1]),
                             deps=tr)
                t_sd = ph.op("act", lambda e, c=c: e.activation(out=ssq[:, c:c + 1], in_=ssq[:, c:c + 1], func=AF.Sqrt, bias=g["eps"][:], scale=1.0 / D),
                             deps=[t_ss])
                t_r = ph.op("dve", lambda e, c=c: e.reciprocal(out=ssq[:, c:c + 1], in_=ssq[:, c:c + 1]), deps=[t_sd])
                t_y = ph.op("dve", lambda e, xi=xi, c=c: e.scalar_tensor_tensor(out=xo[xi][:], in0=xo[xi][:], scalar=ssq[:, c:c + 1], in1=fg[:],
                                                                             op0=ALU.mult, op1=ALU.mult), deps=[t_r, t_c])
                xo_free[xi] = ph.dma("sp", xdst[r0:r0 + 128, :], xo[xi][:], xosem[xi], deps=[t_y])
            else:
                xo_free[xi] = ph.dma("sp", xdst[r0:r0 + 128, :], xo[xi][:], xosem[xi], deps=tr)
        mg_free = last_pe
    ph.run()


def build(stop=None):
    P = Prog()
    stop = stop or STOP_AFTER
    phase_prep(P)
    if stop == "prep":
        return P
    for l in range(DEPTH):
        for L in P.Ls:
            phase_filter(P, l, L)
        if stop == f"filter{l}":
            return P
        phase_A(P, l)
        if stop == f"A{l}":
            return P
        phase_H1(P, l)
        if stop == f"H1{l}":
            return P
        for si in range(len(SEQS)):
            phase_H2(P, l, si)
        if stop == f"H2{l}":
            return P
        phase_attn(P, l, "G")
        if stop == f"G{l}":
            return P
        phase_attn(P, l, "D")
        if stop == f"D{l}":
            return P
        phase_M(P, l)
        if stop == f"M{l}":
            return P
    return P


def make_in_maps(inputs, ncores):
    consts = host_consts()
    xp = np.asarray(inputs["x_prompt"], np.float32)
    xs = np.asarray(inputs["x_sample"], np.float32)
    maps = []
    for c in range(ncores):
        m = {}
        m["x"] = np.ascontiguousarray(np.concatenate([xp[c], xs[2 * c], xs[2 * c + 1]], axis=0))
        for k in PARAM_NAMES:
            m[k] = np.ascontiguousarray(np.asarray(inputs[k], np.float32))
        for k, v in consts.items():
            m["c_" + k] = v
        maps.append(m)
    return maps


def kernel(**inputs):
    P = build()
    maps = make_in_maps(inputs, NCORES)
    res = run_bass_kernel_spmd(P.nc, maps, core_ids=list(range(NCORES)))
    Lp, Ls = SEQS[0], SEQS[1]
    yp = np.stack([res.results[c]["y"][:Lp] for c in range(NCORES)], 0)
    ys = np.stack([res.results[c]["y"][Lp + j * Ls: Lp + (j + 1) * Ls] for c in range(NCORES) for j in range(2)], 0)
    return (np.ascontiguousarray(yp, dtype=np.float32), np.ascontiguousarray(ys, dtype=np.float32))
```

```python
import math
from contextlib import ExitStack

import numpy as np
import ml_dtypes

import concourse.bass as bass
import concourse.mybir as mybir
from concourse.bass_utils import run_bass_kernel_spmd

F32 = mybir.dt.float32
BF16 = mybir.dt.bfloat16
AF = mybir.ActivationFunctionType
ALU = mybir.AluOpType
AX = mybir.AxisListType

D = 1024
NCOL = 5376
HW = 512
EPS = 1e-6
DEPTH = 2
NCORES = 8
PI = math.pi

SEQS = [8192, 2048, 2048]
DEBUG_OUT = set()
STOP_AFTER = None

ENGS = ["pe", "act", "dve", "pool", "sp"]
BLK = {"pe": "tensor", "act": "scalar", "dve": "vector", "pool": "gpsimd", "sp": "sync"}


def fft_dims(L):
    n = 2 * L
    n1 = 1
    while n1 * n1 < n:
        n1 *= 2
    assert n1 * n1 == n, "2L must be a power of 4"
    return n1, n1


class DSem:
    def __init__(self, h, glob=False):
        self.h = h
        self.val = 0
        self.glob = glob


class Phase:
    def __init__(self, nc, name):
        self.nc = nc
        self.name = name
        self.es = ExitStack()
        self.q = {e: [] for e in ENGS}
        self.cnt = {e: 0 for e in ENGS}
        self.csem = {e: self.es.enter_context(nc.semaphore(f"{name}_{e}")) for e in ENGS}
        self.seen = {e: {} for e in ENGS}
        self.dsems = []
        self.gused = []
        self.n = 0

    def sb(self, name, shape, dt):
        return self.es.enter_context(self.nc.sbuf_tensor(f"{self.name}_{name}", list(shape), dt))

    def ps(self, name, shape, dt=F32):
        return self.es.enter_context(self.nc.psum_tensor(f"{self.name}_{name}", list(shape), dt))

    def dsem(self, name):
        d = DSem(self.es.enter_context(self.nc.semaphore(f"{self.name}_d_{name}")))
        self.dsems.append(d)
        return d

    def _waits(self, eng, deps):
        w = []
        for t in deps:
            if t is None:
                continue
            kind, key, val = t
            if kind == "c":
                if key == "pe" and eng == "pe":
                    continue
                h = self.csem[key]
                k = ("c", key)
            else:
                h = key.h
                k = ("d", id(key))
            if self.seen[eng].get(k, 0) >= val:
                continue
            self.seen[eng][k] = val
            w.append((h, val))
        return w

    def op(self, eng, fn, deps=()):
        w = self._waits(eng, deps)
        self.cnt[eng] += 1
        self.q[eng].append((w, fn, self.csem[eng], 1))
        self.n += 1
        return ("c", eng, self.cnt[eng])

    def dma(self, eng, out, in_, ds, deps=(), slow=False):
        assert (eng == "pool") == ds.glob, "pool-queue DMAs must use a global DSem (and only they)"
        if ds.glob and ds not in self.gused:
            self.gused.append(ds)
        w = self._waits(eng, deps)
        ds.val += 16
        if slow:
            fn = lambda e: e.dma_start(out=out, in_=in_, allow_slow_non_contiguous=True)
        else:
            fn = lambda e: e.dma_start(out=out, in_=in_)
        self.q[eng].append((w, fn, ds.h, 16))
        self.n += 1
        return ("d", ds, ds.val)

    def run(self):
        fin = []
        for d in self.dsems + self.gused:
            if d.val > 0 and self.seen["sp"].get(("d", id(d)), 0) < d.val:
                fin.append((d.h, d.val))
        q = self.q

        def emit(eng_name):
            def f(e):
                for (w, fn, sem, inc) in q[eng_name]:
                    for (h, v) in w:
                        e.wait_ge(h, v)
                    ins = fn(e)
                    ins.then_inc(sem, inc)
                if eng_name == "sp":
                    for (h, v) in fin:
                        e.wait_ge(h, v)
            return f

        with self.nc.Block() as block:
            for en in ENGS:
                if q[en] or (en == "sp" and fin):
                    getattr(block, BLK[en])(emit(en))
        allsems = [self.csem[e] for e in ENGS] + [d.h for d in self.dsems]

        def clr(e):
            for h in allsems:
                e.sem_clear(h)

        with self.nc.Block() as block:
            block.sync(clr)
        self.es.close()


_CONSTS = {}


def t5_bucket_np(rel):
    nb = 16
    max_exact = 8
    ret = (rel > 0).astype(np.int32) * nb
    n = np.abs(rel)
    nf = np.maximum(n, 1).astype(np.float32)
    large = max_exact + (np.log(nf / np.float32(max_exact)) / np.float32(math.log(128 / max_exact))
                         * np.float32(nb - max_exact)).astype(np.int32)
    large = np.minimum(large, nb - 1)
    return ret + np.where(n < max_exact, n, large)


def host_consts():
    key = tuple(SEQS)
    if key in _CONSTS:
        return _CONSTS[key]
    bf = ml_dtypes.bfloat16
    c = {}
    c["ident"] = np.eye(128, dtype=np.float32).astype(bf)
    ps = np.zeros((128, 128), np.float32)
    for m in range(128):
        if m % 64 < 32:
            ps[m + 32, m] = -1.0
        else:
            ps[m - 32, m] = 1.0
    c["pswap"] = ps.astype(bf)
    ob = np.zeros((128, 128), np.float32)
    ob[:64, :64] = 1.0 / 64
    ob[64:, 64:] = 1.0 / 64
    c["onesblk"] = ob.astype(bf)
    c["ones128"] = np.full((128, 128), 1.0 / 128, np.float32).astype(bf)
    c["ones1"] = np.ones((128, 128), np.float32).astype(bf)
    Lmax = max(SEQS)
    t = np.arange(Lmax)
    row = (t // 64).astype(np.float32)
    col = (t % 64).astype(np.float32)
    n_freq = 16
    inv = (np.float32(10000.0) ** (-np.arange(n_freq, dtype=np.float32) / np.float32(n_freq))).astype(np.float32)
    ang = np.concatenate([row[:, None] * inv, col[:, None] * inv], axis=-1).astype(np.float32)
    cs = np.cos(ang).astype(np.float32).T
    sn = np.sin(ang).astype(np.float32).T
    c["ropec"] = np.ascontiguousarray(np.tile(cs, (4, 1)))
    c["ropes"] = np.ascontiguousarray(np.tile(sn, (4, 1)))
    e = np.arange(1280)
    bk = t5_bucket_np((e - 639).astype(np.int32))
    oh = np.zeros((32, 1280), np.float32)
    oh[bk, e] = 1.0
    c["t5oh"] = oh
    for L in sorted(set(SEQS)):
        t01 = np.linspace(0.0, 1.0, L, dtype=np.float32)[:, None]
        bands = 16
        w = (np.float32(2.0 * math.pi) * np.arange(L, dtype=np.float32)[:, None] / np.float32(L)).astype(np.float32)
        f = np.linspace(1e-4, bands - 1, bands, dtype=np.float32)[None, :]
        z = np.concatenate([t01, np.cos(f * w), -np.sin(f * w)], axis=-1).astype(np.float32)
        c[f"zT{L}"] = np.ascontiguousarray(z.T)
        max_decay = math.log(1e-2) / 0.3
        min_decay = math.log(1e-2) / 1.5
        deltas = np.linspace(min_decay, max_decay, HW, dtype=np.float32)
        win = np.exp(-t01 * np.abs(deltas)[None, :]).astype(np.float32)
        wb = win.copy()
        wb[0, :] = 0.0
        c[f"win{L}"] = np.ascontiguousarray(np.stack([win, wb], 0))
        N1, N2 = fft_dims(L)
        N = N1 * N2
        n1 = np.arange(N1 // 2)[:, None, None]
        n2 = np.arange(N2)[None, :, None]
        k1 = np.arange(N1)[None, None, :]
        ph = (k1 * (N2 * n1 + n2)) % N
        th = 2.0 * np.pi * ph.astype(np.float64) / N
        f1r = np.cos(th).astype(np.float32).astype(bf)
        f1i = (-np.sin(th)).astype(np.float32).astype(bf)
        if N1 // 2 >= 32:
            f1r = np.ascontiguousarray(np.concatenate([f1r[:, 0::2, :], f1r[:, 1::2, :]], axis=0))
            f1i = np.ascontiguousarray(np.concatenate([f1i[:, 0::2, :], f1i[:, 1::2, :]], axis=0))
        c[f"f1r{L}"] = f1r
        c[f"f1i{L}"] = f1i
        a = np.arange(N2)[:, None]
        b = np.arange(N2)[None, :]
        th2 = 2.0 * np.pi * ((a * b) % N2).astype(np.float64) / N2
        c[f"f2{L}"] = np.stack([np.cos(th2), np.sin(th2), -np.sin(th2), -np.cos(th2)], 1).astype(np.float32).astype(bf)
        k1 = np.arange(N1)[:, None, None]
        n2 = np.arange(N2)[None, :, None]
        n1 = np.arange(N1 // 2)[None, None, :]
        ph = (k1 * (N2 * n1 + n2)) % N
        th = 2.0 * np.pi * ph.astype(np.float64) / N
        c[f"i2c{L}"] = (np.cos(th) / N).astype(np.float32).astype(bf)
        c[f"i2s{L}"] = (-np.sin(th) / N).astype(np.float32).astype(bf)
    _CONSTS[key] = c
    return c


PARAM_NAMES = ["rel_bias", "norm_g", "w_in", "hy_conv_w", "hy_conv_b", "hy_f_w1", "hy_f_b1", "hy_f_w2",
               "hy_f_b2", "hy_f_wout", "hy_f_freq", "hy_bias", "q_norm_g", "k_norm_g", "lam_q1", "lam_k1",
               "lam_q2", "lam_k2", "diff_subln_g", "w_branch_hy", "w_branch_gqa", "w_branch_diff",
               "w_merge", "b_merge", "w_out", "final_g"]

PARAM_SHAPES = {
    "rel_bias": (32, 4), "norm_g": (DEPTH, D), "w_in": (DEPTH, D, NCOL), "hy_conv_w": (DEPTH, 3, 1536),
    "hy_conv_b": (DEPTH, 1536), "hy_f_w1": (DEPTH, 33, 64), "hy_f_b1": (DEPTH, 64),
    "hy_f_w2": (DEPTH, 2, 64, 64), "hy_f_b2": (DEPTH, 2, 64), "hy_f_wout": (DEPTH, 64, 1024),
    "hy_f_freq": (DEPTH, 64), "hy_bias": (DEPTH, HW), "q_norm_g": (DEPTH, 64), "k_norm_g": (DEPTH, 64),
    "lam_q1": (DEPTH, 64), "lam_k1": (DEPTH, 64), "lam_q2": (DEPTH, 64), "lam_k2": (DEPTH, 64),
    "diff_subln_g": (DEPTH, 128), "w_branch_hy": (DEPTH, HW, D), "w_branch_gqa": (DEPTH, HW, D),
    "w_branch_diff": (DEPTH, HW, D), "w_merge": (DEPTH, D, 3 * D), "b_merge": (DEPTH, 3 * D),
    "w_out": (DEPTH, D, D), "final_g": (D,),
}


def bcast_rows(ap_1d_or_row, nparts):
    a = ap_1d_or_row
    return bass.AP(tensor=a.tensor, offset=a.offset, ap=[[0, nparts]] + [list(x) for x in a.ap[-1:]])


class Prog:
    def __init__(self):
        self.nc = bass.Bass("TRN2", target_bir_lowering=False)
        nc = self.nc
        self.T = sum(SEQS)
        self.offs = [sum(SEQS[:i]) for i in range(len(SEQS))]
        self.Ls = sorted(set(SEQS), reverse=True)
        self.consts = host_consts()
        self.d = {}
        self.d["x"] = nc.dram_tensor("x", [self.T, D], F32, kind="ExternalInput").ap()
        for k in PARAM_NAMES:
            self.d[k] = nc.dram_tensor(k, list(PARAM_SHAPES[k]), F32, kind="ExternalInput").ap()
        for k, v in self.consts.items():
            dt = BF16 if v.dtype == ml_dtypes.bfloat16 else F32
            self.d["c_" + k] = nc.dram_tensor("c_" + k, list(v.shape), dt, kind="ExternalInput").ap()
        self.d["y"] = nc.dram_tensor("y", [self.T, D], F32, kind="ExternalOutput").ap()
        self.ges = ExitStack()
        self.g = {}
        self.gsem = [DSem(self.ges.enter_context(nc.semaphore(f"gpool{i}")), glob=True) for i in range(4)]

    def scr(self, name, shape, dt):
        kind = "ExternalOutput" if name in DEBUG_OUT else "Internal"
        a = self.nc.dram_tensor(name, list(shape), dt, kind=kind).ap()
        self.d[name] = a
        return a

    def gsb(self, name, shape, dt):
        t = self.ges.enter_context(self.nc.sbuf_tensor("g_" + name, list(shape), dt))
        self.g[name] = t
        return t


def phase_prep(P):
    nc, d, g = P.nc, P.d, P.g
    for nm in ["ident", "pswap", "onesblk", "ones128", "ones1"]:
        P.gsb(nm, [128, 128], BF16)
    P.gsb("wstrip", [128, 4, 1152], BF16)
    P.gsb("farb", [128, 4, 2], F32)
    P.gsb("lamt", [128, DEPTH, 2], F32)
    P.gsb("eps", [128, 1], F32)
    P.gsb("onesf", [128, 128], F32)
    bvr = P.scr("BVR", [4, 1280], F32)
    ph = Phase(nc, "prep")
    ld = ph.dsem("ld")
    toks = []
    for nm in ["ident", "pswap", "onesblk", "ones128", "ones1"]:
        toks.append(ph.dma("sp", g[nm][:], d["c_" + nm][:], ld))
    t_eps = ph.op("pool", lambda e: e.memset(g["eps"][:], EPS))
    ph.op("pool", lambda e: e.memset(g["onesf"][:], 1.0))
    rb = ph.sb("rb", [32, 4], F32)
    oh = ph.sb("oh", [32, 1280], F32)
    bv = ph.sb("bv", [4, 1280], F32)
    pb = ph.ps("pb", [4, 3, 512], F32)
    l2 = ph.dsem("l2")
    ph.dma("sp", rb[:], d["rel_bias"][:], l2)
    t_oh = ph.dma("sp", oh[:], d["c_t5oh"][:], l2)
    widths = [512, 512, 256]
    tcp = []
    for i, wd in enumerate(widths):
        tm = ph.op("pe", lambda e, i=i, wd=wd: e.matmul(pb[:, i, 0:wd], lhsT=rb[:], rhs=oh[:, i * 512:i * 512 + wd],
                                                         start=True, stop=True), deps=[t_oh])
        tcp.append(ph.op("dve", lambda e, i=i, wd=wd: e.tensor_copy(out=bv[:, i * 512:i * 512 + wd], in_=pb[:, i, 0:wd]),
                         deps=[tm]))
    st = ph.dsem("st")
    t_st = ph.dma("sp", bvr[:], bv[:], st, deps=tcp)
    l3 = ph.dsem("l3")
    wrev = ph.sb("wrev", [128, 4, 1152], F32)
    for h in range(4):
        src = bass.AP(tensor=bvr.tensor, offset=h * 1280, ap=[[1, 128], [1, 1152]])
        t_wr = ph.dma("sp", wrev[:, h, :], src, l3, deps=[t_st])
        for side, e0 in enumerate([0, 1278]):
            src = bass.AP(tensor=bvr.tensor, offset=h * 1280 + e0, ap=[[0, 128], [1, 1]])
            t_wr = ph.dma("sp", g["farb"][:, h, side:side + 1], src, l3, deps=[t_st])
    for h in range(4):
        a = wrev[:, h, :]
        rev = bass.AP(tensor=a.tensor, offset=a.offset + 1151, ap=[list(a.ap[0]), [-1, 1152]])
        ph.op("dve", lambda e, h=h, rev=rev: e.tensor_scalar(out=g["wstrip"][:, h, :], in0=rev, scalar1=8.0, scalar2=None, op0=ALU.mult),
              deps=[t_wr])
    lv = ph.sb("lv", [128, DEPTH, 4, 64], F32)
    l4 = ph.dsem("l4")
    t_lv = None
    for l in range(DEPTH):
        for j, nm in enumerate(["lam_q1", "lam_k1", "lam_q2", "lam_k2"]):
            t_lv = ph.dma("sp", lv[:, l, j, :], bcast_rows(d[nm][l:l + 1, :], 128), l4)
    junk = ph.sb("junk", [128, 64], F32)
    ss = ph.sb("ss", [128, DEPTH, 2], F32)
    ee = ph.sb("ee", [128, DEPTH, 2], F32)
    df = ph.sb("df", [128, DEPTH], F32)
    for l in range(DEPTH):
        lam_init = 0.8 - 0.6 * math.exp(-0.3 * l)
        ts = []
        prev = None
        for j in range(2):
            tmul = ph.op("dve", lambda e, l=l, j=j: e.tensor_tensor(out=junk[:], in0=lv[:, l, 2 * j, :], in1=lv[:, l, 2 * j + 1, :],
                                                                 op=ALU.mult), deps=[t_lv, prev])
            prev = ph.op("dve", lambda e, l=l, j=j: e.tensor_reduce(out=ss[:, l, j:j + 1], in_=junk[:], axis=AX.X, op=ALU.add),
                         deps=[tmul])
            ts.append(prev)
        te = ph.op("act", lambda e, l=l: e.activation(out=ee[:, l, :], in_=ss[:, l, :], func=AF.Exp), deps=ts)
        t1 = ph.op("dve", lambda e, l=l: e.tensor_tensor(out=df[:, l:l + 1], in0=ee[:, l, 1:2], in1=ee[:, l, 0:1],
                                                         op=ALU.subtract), deps=[te, prev])
        ph.op("dve", lambda e, l=l, li=lam_init: e.tensor_scalar(out=g["lamt"][:, l, 0:1], in0=df[:, l:l + 1],
                                                                  scalar1=-li, scalar2=None, op0=ALU.add), deps=[t1])
    ph.run()


def fft_stage1(ph, P, L, srcs, combos, Ad, tagp):
    nc, d = P.nc, P.d
    N1, N2 = fft_dims(L)
    H1 = N1 // 2
    G = FG
    paired = H1 >= 32
    NP_ = 2 * H1 if paired else H1
    GH = G // 2 if paired else G
    ncol = N2 // 2 if paired else N2
    tr = ph.sb(tagp + "tr", [NP_, ncol, N1], BF16)
    ti = ph.sb(tagp + "ti", [NP_, ncol, N1], BF16)
    ldt = ph.dsem(tagp + "ldt")
    ph.dma("sp", tr[:], d[f"c_f1r{L}"][:], ldt)
    t_tab = ph.dma("sp", ti[:], d[f"c_f1i{L}"][:], ldt)
    ns = len(srcs)
    NB = 2
    xs = [ph.sb(f"{tagp}x{b}", [NP_, ns, GH, HW], BF16) for b in range(NB)]
    xsem = [ph.dsem(f"{tagp}xs{b}") for b in range(NB)]
    no = len(combos)
    stg = [ph.sb(f"{tagp}st{b}", [N1, no, 2, G, HW], BF16) for b in range(NB)]
    ssem = [ph.dsem(f"{tagp}ss{b}") for b in range(NB)]
    NSL = 4
    pp = ph.ps(tagp + "pp", [N1, NSL, HW], F32)
    ngrp = N2 // G
    x_free = [None] * NB
    st_free = [None] * NB
    pp_free = [None] * NSL
    x_ld = [None] * NB

    def load(gi):
        b = gi % NB
        t = None
        for si, s in enumerate(srcs):
            v = s.rearrange("(a n) c -> a n c", n=N2)
            if paired:
                for hf in range(2):
                    src = v[0:H1, gi * G + hf:(gi + 1) * G:2, :]
                    t = ph.dma("sp", xs[b][hf * H1:(hf + 1) * H1, si, :, :], src, xsem[b], deps=[x_free[b]])
            else:
                t = ph.dma("sp", xs[b][:, si, :, :], v[0:H1, gi * G:(gi + 1) * G, :], xsem[b], deps=[x_free[b]])
        x_ld[b] = t

    load(0)
    cnt = 0
    for gi in range(ngrp):
        b = gi % NB
        if gi + 1 < ngrp:
            load(gi + 1)
        evs = []
        last_mm = None
        for ml in range(GH):
            for o, combo in enumerate(combos):
                for ri, tab in enumerate([tr, ti]):
                    halves = [0, 1] if paired else [0]
                    slots = []
                    for hf in halves:
                        slots.append(cnt % NSL)
                        cnt += 1
                    col = (gi * G) // (2 if paired else 1) + ml
                    for ci, si in enumerate(combo):
                        for hf in halves:
                            slot = slots[hf]
                            p0, p1 = hf * H1, (hf + 1) * H1
                            kw = dict(tile_position=(p0, 0)) if paired else {}
                            last_mm = ph.op("pe", lambda e, slot=slot, tab=tab, col=col, b=b, si=si, ml=ml, ci=ci, nci=len(combo), p0=p0, p1=p1, kw=kw:
                                            e.matmul(pp[:, slot, :], lhsT=tab[p0:p1, col, :], rhs=xs[b][p0:p1, si, ml, :],
                                                     start=(ci == 0), stop=(ci == nci - 1), **kw),
                                            deps=[t_tab, x_ld[b], pp_free[slot]])
                    for hf in halves:
                        slot = slots[hf]
                        n2l = (2 * ml + hf) if paired else ml
                        eng = "act" if (cnt + hf) % 2 else "dve"
                        if eng == "act":
                            ev = ph.op("act", lambda e, slot=slot, b=b, o=o, ri=ri, n2l=n2l:
                                       e.activation(out=stg[b][:, o, ri, n2l, :], in_=pp[:, slot, :], func=AF.Copy),
                                       deps=[last_mm, st_free[b]])
                        else:
                            ev = ph.op("dve", lambda e, slot=slot, b=b, o=o, ri=ri, n2l=n2l:
                                       e.tensor_copy(out=stg[b][:, o, ri, n2l, :], in_=pp[:, slot, :]),
                                       deps=[last_mm, st_free[b]])
                        pp_free[slot] = ev
                        evs.append(ev)
        x_free[b] = last_mm
        t = None
        for o in range(no):
            for ri in range(2):
                dst = Ad[o][ri][gi * G:(gi + 1) * G, :, :].rearrange("n k c -> k n c")
                t = ph.dma("sp", dst, stg[b][:, o, ri, :, :], ssem[b], deps=evs)
        st_free[b] = t


def fft_stage2_tables(ph, P, L, tagp):
    d = P.d
    N1, N2 = fft_dims(L)
    f2 = ph.sb(tagp + "f2", [N2, 4, N2], BF16)
    ldt = ph.dsem(tagp + "ldf2")
    t = ph.dma("sp", f2[:], d[f"c_f2{L}"][:], ldt)
    return f2, t


FG = 4


def load_ktiles(ph, srcs, k0, G, dst, sem, deps, N2):
    t = None
    for si, s in enumerate(srcs):
        t = ph.dma("sp", dst[:, si, :, :], s[:, k0:k0 + G, :], sem, deps=deps)
    return t


def phase_filter(P, l, L):
    nc, d, g = P.nc, P.d, P.g
    N1, N2 = fft_dims(L)
    key = f"{L}"
    if ("FIL" + key) not in d:
        P.scr("FIL" + key, [3, L, HW], BF16)
        for o in range(2):
            for ri in range(2):
                P.scr(f"AF{key}_{o}{ri}", [N2, N1, HW], BF16)
        for ri in range(2):
            P.scr(f"H{key}_{ri}", [N2, N1, HW], BF16)
    FIL = d["FIL" + key]
    AFd = [[d[f"AF{key}_{o}{ri}"] for ri in range(2)] for o in range(2)]
    Hd = [d[f"H{key}_{ri}"] for ri in range(2)]

    ph = Phase(nc, f"fm{l}_{L}")
    w1 = ph.sb("w1", [33, 64], F32)
    w2 = ph.sb("w2", [64, 2, 64], F32)
    wo = ph.sb("wo", [64, 1024], F32)
    fr = ph.sb("fr", [64, 1], F32)
    bb = ph.sb("bb", [64, 3], F32)
    fb = ph.sb("fb", [64, 3], F32)
    zt = ph.sb("zt", [33, L], F32)
    ld = ph.dsem("ld")
    ph.dma("sp", w1[:], d["hy_f_w1"][l], ld)
    for i in range(2):
        ph.dma("sp", w2[:, i, :], d["hy_f_w2"][l, i], ld)
    ph.dma("sp", wo[:], d["hy_f_wout"][l], ld)
    ph.dma("sp", fr[:], d["hy_f_freq"][l:l + 1, :].rearrange("o f -> f o"), ld, slow=True)
    ph.dma("sp", bb[:, 0:1], d["hy_f_b1"][l:l + 1, :].rearrange("o f -> f o"), ld, slow=True)
    for i in range(2):
        ph.dma("sp", bb[:, 1 + i:2 + i], d["hy_f_b2"][l, i:i + 1, :].rearrange("o f -> f o"), ld, slow=True)
    t_ld = ph.dma("sp", zt[:], d[f"c_zT{L}"][:], ld)
    t_fb = ph.op("dve", lambda e: e.tensor_scalar(out=fb[:], in0=bb[:], scalar1=fr[:, 0:1], scalar2=None, op0=ALU.mult),
                 deps=[t_ld])
    pa = ph.ps("pa", [64, 4, 512], F32)
    pf = ph.ps("pf", [128, 4, 512], F32)
    ya = [[ph.sb(f"ya{p}_{i}", [64, 512], F32) for i in range(2)] for p in range(2)]
    aa = [[ph.sb(f"aa{p}_{i}", [64, 512], F32) for i in range(2)] for p in range(2)]
    m1 = [ph.sb(f"m1_{p}", [64, 512], F32) for p in range(2)]
    m2 = [ph.sb(f"m2_{p}", [64, 512], F32) for p in range(2)]
    NBW = 2
    wn = [ph.sb(f"wn{b}", [128, 2, 512], F32) for b in range(NBW)]
    wsem = [ph.dsem(f"ws{b}") for b in range(NBW)]
    fo = [ph.sb(f"fo{b}", [128, 3, 512], BF16) for b in range(NBW)]
    fsem = [ph.dsem(f"fs{b}") for b in range(NBW)]
    wn_free = [None] * NBW
    fo_free = [None] * NBW
    pa_free = [None] * 4
    pf_free = [None] * 4
    a_read = [[None, None], [None, None]]
    ya_read = [[None, None], [None, None]]
    m_read = [None, None]
    ntile = L // 512
    cnt = {"sub": 0, "pfc": 0}

    def tile_stages(j, par):
        cols = slice(j * 512, (j + 1) * 512)
        stt = {"prev_a": None}

        def layer_stage(layer):
            def run():
                pi = par * 2 + layer % 2
                if layer == 0:
                    tm = ph.op("pe", lambda e: e.matmul(pa[:, pi, :], lhsT=w1[:], rhs=zt[:, cols], start=True, stop=True),
                               deps=[t_ld, pa_free[pi]])
                else:
                    src = aa[par][(layer - 1) % 2]
                    tm = ph.op("pe", lambda e: e.matmul(pa[:, pi, :], lhsT=w2[:, layer - 1, :], rhs=src[:], start=True, stop=True),
                               deps=[t_ld, pa_free[pi], stt["prev_a"]])
                    a_read[par][(layer - 1) % 2] = tm
                yb = ya[par][layer % 2]
                t1 = ph.op("dve", lambda e: e.tensor_scalar(out=yb[:], in0=pa[:, pi, :], scalar1=fr[:, 0:1],
                                                            scalar2=fb[:, layer:layer + 1], op0=ALU.mult, op1=ALU.add),
                           deps=[tm, t_fb, ya_read[par][layer % 2]])
                pa_free[pi] = t1
                ta = ph.op("dve", lambda e: e.tensor_scalar(out=m1[par][:], in0=yb[:], scalar1=-PI, scalar2=2 * PI, op0=ALU.is_lt, op1=ALU.mult),
                           deps=[t1, m_read[par]])
                tb = ph.op("dve", lambda e: e.tensor_scalar(out=m2[par][:], in0=yb[:], scalar1=PI, scalar2=-2 * PI, op0=ALU.is_gt, op1=ALU.mult),
                           deps=[t1])
                tc = ph.op("dve", lambda e: e.tensor_tensor(out=yb[:], in0=yb[:], in1=m1[par][:], op=ALU.add), deps=[ta, tb])
                t2 = ph.op("dve", lambda e: e.tensor_tensor(out=yb[:], in0=yb[:], in1=m2[par][:], op=ALU.add), deps=[tc])
                m_read[par] = t2
                ab = aa[par][layer % 2]
                stt["prev_a"] = ph.op("act", lambda e: e.activation(out=ab[:], in_=yb[:], func=AF.Sin),
                                      deps=[t2, a_read[par][layer % 2]])
                ya_read[par][layer % 2] = stt["prev_a"]
            return run

        def tm_stage(s):
            def run():
                a3 = aa[par][0]
                b = cnt["sub"] % NBW
                cnt["sub"] += 1
                t0 = j * 512 + s * 128
                t_w = ph.dma("sp", wn[b][:], d[f"c_win{L}"][0:2, t0:t0 + 128, :].rearrange("w t c -> t w c"), wsem[b], deps=[wn_free[b]])
                outs = []
                for half in range(2):
                    slot = cnt["pfc"] % 4
                    cnt["pfc"] += 1
                    tm = ph.op("pe", lambda e, slot=slot, half=half: e.matmul(pf[:, slot, :], lhsT=a3[:, s * 128:(s + 1) * 128],
                                                                              rhs=wo[:, half * 512:(half + 1) * 512], start=True, stop=True),
                               deps=[stt["prev_a"], pf_free[slot]])
                    a_read[par][0] = tm
                    te = ph.op("dve", lambda e, slot=slot, half=half: e.tensor_tensor(out=fo[b][:, half, :], in0=pf[:, slot, :],
                                                                                      in1=wn[b][:, half, :], op=ALU.mult),
                               deps=[tm, t_w, fo_free[b]])
                    pf_free[slot] = te
                    outs.append(te)
                wn_free[b] = outs[-1]
                tn = ph.op("act", lambda e: e.activation(out=fo[b][:, 2, :], in_=fo[b][:, 1, :], func=AF.Copy, scale=-1.0),
                           deps=[outs[-1], fo_free[b]])
                fo_free[b] = ph.dma("sp", FIL[:, t0:t0 + 128, :].rearrange("w t c -> t w c"), fo[b][:], fsem[b], deps=[outs[0], tn])
            return run

        return [layer_stage(0), layer_stage(1), layer_stage(2)] + [tm_stage(s) for s in range(4)]

    for j0 in range(0, ntile, 2):
        lists = [tile_stages(j, j - j0) for j in range(j0, min(j0 + 2, ntile))]
        for k in range(7):
            for lst in lists:
                lst[k]()
    ph.run()

    ph = Phase(nc, f"ff1{l}_{L}")
    fft_stage1(ph, P, L, [FIL[0], FIL[1], FIL[2]], [[0, 1], [0, 2]], AFd, "a")
    ph.run()

    ph = Phase(nc, f"ff2{l}_{L}")
    f2, t_f2 = fft_stage2_tables(ph, P, L, "b")
    hb = ph.sb("hb", [128, HW], F32)
    lb = ph.dsem("lb")
    t_hb = ph.dma("sp", hb[:], bcast_rows(d["hy_bias"][l:l + 1, :], 128), lb)
    G = FG
    NB = 2
    xin = [ph.sb(f"xin{b}", [N2, 4, G, HW], BF16) for b in range(NB)]
    xsem = [ph.dsem(f"xs{b}") for b in range(NB)]
    hst = [ph.sb(f"hst{b}", [N2, 2, G, HW], BF16) for b in range(NB)]
    hsem = [ph.dsem(f"hs{b}") for b in range(NB)]
    pp = ph.ps("pp", [N2, 4, HW], F32)
    x_free = [None] * NB
    h_free = [None] * NB
    pp_free = [None] * 4
    x_ld = [None] * NB
    srcs = [AFd[0][0], AFd[0][1], AFd[1][0], AFd[1][1]]
    ngrp = N1 // G
    x_ld[0] = load_ktiles(ph, srcs, 0, G, xin[0], xsem[0], [], N2)
    cnt = 0
    for gi in range(ngrp):
        b = gi % NB
        if gi + 1 < ngrp:
            nb_ = (gi + 1) % NB
            x_ld[nb_] = load_ktiles(ph, srcs, (gi + 1) * G, G, xin[nb_], xsem[nb_], [x_free[nb_]], N2)
        evs = []
        last = None
        for kl in range(G):
            for ri, (ia, ib, tb) in enumerate([(0, 1, 1), (3, 2, 2)]):
                slot = cnt % 4
                cnt += 1
                ph.op("pe", lambda e, slot=slot, b=b, ia=ia, kl=kl: e.matmul(pp[:, slot, :], lhsT=f2[:, 0, :], rhs=xin[b][:, ia, kl, :],
                                                                           start=True, stop=False),
                      deps=[t_f2, x_ld[b], pp_free[slot]])
                last = ph.op("pe", lambda e, slot=slot, b=b, ib=ib, kl=kl, tb=tb: e.matmul(pp[:, slot, :], lhsT=f2[:, tb, :], rhs=xin[b][:, ib, kl, :],
                                                                                         start=False, stop=True))
                if ri == 0:
                    ev = ph.op("dve", lambda e, slot=slot, b=b, kl=kl: e.tensor_tensor(out=hst[b][:, 0, kl, :], in0=pp[:, slot, :], in1=hb[0:N2, :], op=ALU.add),
                               deps=[last, t_hb, h_free[b]])
                else:
                    ev = ph.op("act", lambda e, slot=slot, b=b, kl=kl: e.activation(out=hst[b][:, 1, kl, :], in_=pp[:, slot, :], func=AF.Copy),
                               deps=[last, h_free[b]])
                pp_free[slot] = ev
                evs.append(ev)
        x_free[b] = last
        t = None
        for ri in range(2):
            t = ph.dma("sp", Hd[ri][:, gi * G:(gi + 1) * G, :], hst[b][:, ri, :, :], hsem[b], deps=evs)
        h_free[b] = t
    ph.run()


def ensure_act_scratch(P):
    d, T = P.d, P.T
    if "UH" in d:
        return
    P.scr("UH", [2048, T], BF16)
    P.scr("GQ", [512, T], BF16)
    P.scr("GK2", [2, 128, T], BF16)
    P.scr("GV", [2, T, 64], BF16)
    P.scr("GS", [512, T], BF16)
    P.scr("DQ", [512, T], BF16)
    P.scr("DK", [512, T], BF16)
    P.scr("DV", [T, 512], BF16)
    P.scr("DS", [512, T], BF16)
    P.scr("HT", [1024, T], BF16)
    P.scr("XR", [T, D], F32)
    P.scr("HV", [T, HW], BF16)
    P.scr("GG", [512, T], BF16)
    P.scr("YC", [512, T], BF16)
    P.scr("YG", [512, T], BF16)
    P.scr("YD", [512, T], BF16)


def seq_of(P, t0):
    for i, o in enumerate(P.offs):
        if o <= t0 < o + SEQS[i]:
            return i, t0 - o
    raise ValueError


def phase_A(P, l):
    nc, d, g = P.nc, P.d, P.g
    T = P.T
    ensure_act_scratch(P)
    xsrc = d["x"] if l == 0 else d["XR"]
    ph = Phase(nc, f"A{l}")
    Wb = ph.sb("Wb", [128, 8, NCOL], BF16)
    gam = ph.sb("gam", [128, D], F32)
    qg = ph.sb("qg", [128, 2], F32)
    wl = P.gsem[0]
    t_w = None
    for k in range(8):
        t_w = ph.dma("pool", Wb[:, k, :], d["w_in"][l, k * 128:(k + 1) * 128, :], wl)
    cl = ph.dsem("cl")
    ph.dma("sp", gam[:], bcast_rows(d["norm_g"][l:l + 1, :], 128), cl)
    for hh in range(2):
        ph.dma("sp", qg[hh * 64:(hh + 1) * 64, 0:1], d["q_norm_g"][l:l + 1, :].rearrange("o f -> f o"), cl, slow=True)
        t_c = ph.dma("sp", qg[hh * 64:(hh + 1) * 64, 1:2], d["k_norm_g"][l:l + 1, :].rearrange("o f -> f o"), cl, slow=True)

    NXB = 3
    xb = [ph.sb(f"xb{i}", [128, D], F32) for i in range(NXB)]
    xsem = [ph.dsem(f"xs{i}") for i in range(NXB)]
    xb_free = [None] * NXB
    hb = [ph.sb(f"hb{i}", [128, D], BF16) for i in range(4)]
    hb_free = [None] * 4
    hb_rdy = [None] * 4
    junk = ph.sb("junk", [128, D], F32)
    ssq = ph.sb("ssq", [128, 4], F32)
    hT = [ph.sb(f"hT{i}", [128, 8, 512], BF16) for i in range(2)]
    hT_free = [None] * 2
    hT_st = [None] * 2
    hT_rdy = [None] * 2
    htsem = [ph.dsem(f"hts{i}") for i in range(2)]
    cs = [ph.sb(f"cs{i}", [128, 2, 512], F32) for i in range(2)]
    cssem = [ph.dsem(f"css{i}") for i in range(2)]
    cs_free = [None] * 2
    cs_ld = [None] * 2
    tp = ph.ps("tp", [128, 2, 8, 128], BF16)
    tp_free = [None] * 2
    NS = 6
    pg = ph.ps("pg", [128, NS, 512], F32)
    slot_free = [None] * NS
    st = {"slot": 0, "sub": 0, "og": 0, "tpi": 0, "qs": 0}
    NOG = 5
    og = [ph.sb(f"og{i}", [128, 4, 512], BF16) for i in range(NOG)]
    ogsem = [ph.dsem(f"ogs{i}") for i in range(NOG)]
    og_free = [None] * NOG
    vs = [ph.sb(f"vs{i}", [128, 640], BF16) for i in range(2)]
    vssem = [ph.dsem(f"vss{i}") for i in range(2)]
    vs_free = [None] * 2
    sqb = [ph.sb(f"sqb{i}", [128, 512], BF16) for i in range(2)]
    rt = [ph.sb(f"rt{i}", [128, 512], F32) for i in range(2)]
    qn = [ph.sb(f"qn{i}", [128, 512], BF16) for i in range(2)]
    t1b = [ph.sb(f"t1b{i}", [128, 512], F32) for i in range(2)]
    t2b = [ph.sb(f"t2b{i}", [128, 512], F32) for i in range(2)]
    lastuse = [dict() for _ in range(2)]
    ssm = ph.sb("ssm", [128, 4], F32)
    rsd = ph.sb("rsd", [128, 4], F32)
    ss_free = [None] * 4

    def getslot():
        s = st["slot"] % NS
        st["slot"] += 1
        return s

    ntile = T // 512

    def norm_part1(tt):
        b = tt % 2
        t0 = tt * 512
        si, pos = seq_of(P, t0)
        cs_ld[b] = None
        ph.dma("sp", cs[b][:, 0, :], d["c_ropec"][:, pos:pos + 512], cssem[b], deps=[cs_free[b]])
        cs_ld[b] = ph.dma("sp", cs[b][:, 1, :], d["c_ropes"][:, pos:pos + 512], cssem[b], deps=[cs_free[b]])
        for s in range(4):
            i = st["sub"] % NXB
            st["sub"] += 1
            j = s
            q4 = s
            r0 = t0 + s * 128
            t_x = ph.dma("sp", xb[i][:], xsrc[r0:r0 + 128, :], xsem[i], deps=[xb_free[i]])
            t_ss = ph.op("act", lambda e, i=i, q4=q4: e.activation(out=junk[:], in_=xb[i][:], func=AF.Square,
                                                                  accum_out=ssq[:, q4:q4 + 1]), deps=[t_x, ss_free[q4]])
            t_sd = ph.op("act", lambda e, q4=q4: e.activation(out=ssm[:, q4:q4 + 1], in_=ssq[:, q4:q4 + 1], func=AF.Sqrt,
                                                             bias=g["eps"][:], scale=1.0 / D), deps=[t_ss])
            t_r = ph.op("dve", lambda e, q4=q4: e.reciprocal(out=rsd[:, q4:q4 + 1], in_=ssm[:, q4:q4 + 1]), deps=[t_sd])
            t_h = ph.op("dve", lambda e, i=i, j=j, q4=q4: e.scalar_tensor_tensor(
                out=hb[j][:], in0=xb[i][:], scalar=rsd[:, q4:q4 + 1], in1=gam[:], op0=ALU.mult, op1=ALU.mult),
                deps=[t_r, t_c, hb_free[j]])
            ss_free[q4] = t_h
            xb_free[i] = t_h
            hb_rdy[j] = t_h

    def norm_part2(tt):
        b = tt % 2
        t0 = tt * 512
        evs = []
        for s in range(4):
            j = s
            tpi = st["tpi"] % 2
            st["tpi"] += 1
            last = None
            for k in range(8):
                last = ph.op("pe", lambda e, tpi=tpi, k=k, j=j: e.transpose(out=tp[:, tpi, k, :], in_=hb[j][:, k * 128:(k + 1) * 128],
                                                                           identity=g["ident"][:]),
                             deps=[hb_rdy[j], tp_free[tpi]])
            hb_free[j] = last
            ev = ph.op("act", lambda e, tpi=tpi, b=b, s=s: e.activation(out=hT[b][:, :, s * 128:(s + 1) * 128], in_=tp[:, tpi, :, :], func=AF.Copy),
                       deps=[last, hT_free[b], hT_st[b]])
            tp_free[tpi] = ev
            evs.append(ev)
        hT_rdy[b] = evs[-1]
        hT_st[b] = ph.dma("sp", d["HT"][:, t0:t0 + 512].rearrange("(k p) t -> p k t", p=128), hT[b][:], htsem[b], deps=evs)

    def grp4(c0, kind, name):
        return [(c0 + j, kind, name, j) for j in range(4)]
    plan = []
    plan += [(16, "qk", "GQ", 0)] + grp4(0, "copy", "UH0") + [(20, "qk", "GK", 0)] + grp4(4, "copy", "UH1")
    plan += [(17, "qk", "GQ", 1)] + grp4(8, "copy", "UH2") + grp4(26, "copy", "DQ")
    plan += [(18, "qk", "GQ", 2)] + grp4(30, "copy", "DK") + grp4(12, "silu", "UH3")
    plan += [(19, "qk", "GQ", 3)] + grp4(22, "silu", "GS") + grp4(38, "silu", "DS")
    dests = {"UH0": (d["UH"], 0), "UH1": (d["UH"], 512), "UH2": (d["UH"], 1024), "UH3": (d["UH"], 1536),
             "DQ": (d["DQ"], 0), "DK": (d["DK"], 0), "GS": (d["GS"], 0), "DS": (d["DS"], 0), "GQ": (d["GQ"], 0)}

    norm_part1(0)
    norm_part2(0)
    for tt in range(ntile):
        b = tt % 2
        t0 = tt * 512
        cur = {}
        gevs = {}
        last_pe = None
        deferred = []
        for ci, (c, kind, grp, j) in enumerate(plan):
            if ci == 5 and tt + 1 < ntile:
                norm_part1(tt + 1)
            if ci == 24 and tt + 1 < ntile:
                norm_part2(tt + 1)
            while deferred and deferred[0][0] <= ci:
                deferred.pop(0)[1]()
            if grp not in cur:
                if grp == "GQ":
                    cur[grp] = 3
                elif grp == "GK":
                    cur[grp] = 4
                else:
                    cur[grp] = st["og"] % 3
                    st["og"] += 1
                gevs[grp] = []
            cur_og = cur[grp]
            grp_evs = gevs[grp]
            su = getslot()
            for k in range(8):
                last_pe = ph.op("pe", lambda e, su=su, k=k, c=c, b=b: e.matmul(pg[:, su, :], lhsT=Wb[:, k, c * 128:(c + 1) * 128], rhs=hT[b][:, k, :],
                                                                             start=(k == 0), stop=(k == 7)),
                                deps=[t_w, hT_rdy[b], slot_free[su]])
            mm = last_pe
            o = cur_og
            if kind == "copy":
                ev = ph.op("act", lambda e, su=su, o=o, j=j: e.activation(out=og[o][:, j, :], in_=pg[:, su, :], func=AF.Copy),
                           deps=[mm, og_free[o]])
                slot_free[su] = ev
            elif kind == "silu":
                ev = ph.op("act", lambda e, su=su, o=o, j=j: e.activation(out=og[o][:, j, :], in_=pg[:, su, :], func=AF.Silu),
                           deps=[mm, og_free[o]])
                slot_free[su] = ev
            else:
                q = st["qs"] % 2
                st["qs"] += 1
                lu = lastuse[q]
                gcol = 0 if grp == "GQ" else 1
                t_sq = ph.op("act", lambda e, su=su, q=q: e.activation(out=sqb[q][:], in_=pg[:, su, :], func=AF.Square),
                             deps=[mm, lu.get("sqb")])
                box = {}

                def stepA(su=su, q=q, lu=lu, gcol=gcol, t_sq=t_sq, box=box):
                    sm = getslot()
                    t_ms = ph.op("pe", lambda e: e.matmul(pg[:, sm, :], lhsT=g["onesblk"][:], rhs=sqb[q][:], start=True, stop=True),
                                 deps=[t_sq, slot_free[sm]])
                    lu["sqb"] = t_ms
                    t_sd = ph.op("act", lambda e: e.activation(out=rt[q][:], in_=pg[:, sm, :], func=AF.Sqrt, bias=g["eps"][:], scale=1.0),
                                 deps=[t_ms, lu.get("rt")])
                    slot_free[sm] = t_sd
                    t_rs = ph.op("dve", lambda e: e.reciprocal(out=rt[q][:], in_=rt[q][:]), deps=[t_sd])
                    t_qn = ph.op("dve", lambda e: e.scalar_tensor_tensor(
                        out=qn[q][:], in0=pg[:, su, :], scalar=qg[:, gcol:gcol + 1], in1=rt[q][:], op0=ALU.mult, op1=ALU.mult),
                        deps=[t_rs, t_c, lu.get("qn")])
                    slot_free[su] = t_qn
                    lu["rt"] = t_qn
                    box["t_qn"] = t_qn

                def stepB(q=q, lu=lu, b=b, o=o, j=j, grp=grp, box=box, grp_evs=grp_evs, t0=t0):
                    t_qn = box["t_qn"]
                    sw = getslot()
                    t_sw = ph.op("pe", lambda e: e.matmul(pg[:, sw, :], lhsT=g["pswap"][:], rhs=qn[q][:], start=True, stop=True),
                                 deps=[t_qn, slot_free[sw]])
                    t_1 = ph.op("dve", lambda e: e.tensor_tensor(out=t1b[q][:], in0=qn[q][:], in1=cs[b][:, 0, :], op=ALU.mult),
                                deps=[t_qn, cs_ld[b], lu.get("t1b")])
                    t_2 = ph.op("dve", lambda e: e.tensor_tensor(out=t2b[q][:], in0=pg[:, sw, :], in1=cs[b][:, 1, :], op=ALU.mult),
                                deps=[t_sw, cs_ld[b], lu.get("t2b")])
                    slot_free[sw] = t_2
                    lu["qn"] = t_2
                    cs_free[b] = t_2
                    ev = ph.op("pool", lambda e: e.tensor_tensor(out=og[o][:, j, :], in0=t1b[q][:], in1=t2b[q][:], op=ALU.add),
                               deps=[t_1, t_2, og_free[o]])
                    lu["t1b"] = ev
                    lu["t2b"] = ev
                    grp_evs.append(ev)
                    if grp == "GK":
                        tk = None
                        for kv in range(2):
                            for dup in range(2):
                                tk = ph.dma("sp", d["GK2"][kv, dup * 64:(dup + 1) * 64, t0:t0 + 512], og[o][kv * 64:(kv + 1) * 64, 0, :], ogsem[o], deps=grp_evs)
                        og_free[o] = tk
                    elif j == 3:
                        dst, r0 = dests[grp]
                        og_free[o] = ph.dma("sp", dst[r0:r0 + 512, t0:t0 + 512].rearrange("(j p) t -> p j t", p=128), og[o][:], ogsem[o], deps=grp_evs)

                deferred.append((ci + 2, stepA))
                deferred.append((ci + 6, stepB))
                deferred.sort(key=lambda x: x[0])
                continue
            grp_evs.append(ev)
            if grp == "GK":
                tk = None
                for kv in range(2):
                    for dup in range(2):
                        tk = ph.dma("sp", d["GK2"][kv, dup * 64:(dup + 1) * 64, t0:t0 + 512], og[o][kv * 64:(kv + 1) * 64, 0, :], ogsem[o], deps=grp_evs)
                og_free[o] = tk
            elif j == 3:
                dst, r0 = dests[grp]
                og_free[o] = ph.dma("sp", dst[r0:r0 + 512, t0:t0 + 512].rearrange("(j p) t -> p j t", p=128), og[o][:], ogsem[o], deps=grp_evs)
        while deferred and deferred[0][0] <= len(plan) + 1:
            deferred.pop(0)[1]()
        for s in range(4):
            vb = (tt * 4 + s) % 2
            s1 = getslot()
            for k in range(8):
                last_pe = ph.op("pe", lambda e, s1=s1, k=k, s=s, b=b: e.matmul(pg[:, s1, 0:128], lhsT=hT[b][:, k, s * 128:(s + 1) * 128], rhs=Wb[:, k, 2688:2816],
                                                                             start=(k == 0), stop=(k == 7)),
                                deps=[t_w, hT_rdy[b], slot_free[s1]])
            e1 = ph.op("act", lambda e, s1=s1, vb=vb: e.activation(out=vs[vb][:, 0:128], in_=pg[:, s1, 0:128], func=AF.Copy),
                       deps=[last_pe, vs_free[vb]])
            slot_free[s1] = e1
            s2 = getslot()
            for k in range(8):
                last_pe = ph.op("pe", lambda e, s2=s2, k=k, s=s, b=b: e.matmul(pg[:, s2, :], lhsT=hT[b][:, k, s * 128:(s + 1) * 128], rhs=Wb[:, k, 4352:4864],
                                                                             start=(k == 0), stop=(k == 7)),
                                deps=[slot_free[s2]])
            e2 = ph.op("dve", lambda e, s2=s2, vb=vb: e.tensor_copy(out=vs[vb][:, 128:640], in_=pg[:, s2, :]),
                       deps=[last_pe, vs_free[vb]])
            slot_free[s2] = e2
            r0 = t0 + s * 128
            for kv in range(2):
                ph.dma("sp", d["GV"][kv, r0:r0 + 128, :], vs[vb][:, kv * 64:(kv + 1) * 64], vssem[vb], deps=[e1])
            vs_free[vb] = ph.dma("sp", d["DV"][r0:r0 + 128, :], vs[vb][:, 128:640], vssem[vb], deps=[e2])
        hT_free[b] = last_pe
        while deferred:
            deferred.pop(0)[1]()
    ph.run()


def phase_attn(P, l, mode):
    nc, d, g = P.nc, P.d, P.g
    isD = mode == "D"
    ph = Phase(nc, f"{mode}{l}")
    Lmax = max(SEQS)
    VW = 128 if isD else 64
    NKB = 2
    Kb = [ph.sb(f"K{i}", [128, Lmax], BF16) for i in range(NKB)]
    Vb = [ph.sb(f"V{i}", [128, Lmax // 128, VW], BF16) for i in range(NKB)]
    ksem = [ph.dsem(f"ks{i}") for i in range(NKB)]
    k_free = [None] * NKB
    k_ld = [None] * NKB
    NQB = 3
    Qb = [ph.sb(f"Q{i}", [128, 512], BF16) for i in range(NQB)]
    Gb = [ph.sb(f"Gt{i}", [128, 512], BF16) for i in range(NQB)]
    qsem = [ph.dsem(f"qs{i}") for i in range(NQB)]
    q_free = [None] * NQB
    q_ld = [None] * NQB
    NP = 4
    p_s = [ph.sb(f"p{i}", [128, 2, 512], BF16) for i in range(NP)]
    p_free = [None] * NP
    ps_s = ph.ps("s", [128, 2, 2, 512], F32)
    s_free = [None] * 2
    if isD:
        acc = ph.ps("acc", [128, 4, 512], F32)
        NACC = 1
        gsub = ph.sb("gsub", [128, 1], F32)
        gs0 = ph.sb("gs0", [128, 1], F32)
        cl = ph.dsem("cl")
        t_g0 = ph.dma("sp", gs0[:], d["diff_subln_g"][l:l + 1, :].rearrange("o f -> f o"), cl, slow=True)
        lam_init = 0.8 - 0.6 * math.exp(-0.3 * l)
        t_gs = ph.op("dve", lambda e: e.tensor_scalar(out=gsub[:], in0=gs0[:], scalar1=1.0 - lam_init, scalar2=None, op0=ALU.mult),
                     deps=[t_g0])
        sqd = [ph.sb(f"sqd{i}", [128, 512], BF16) for i in range(2)]
        dcp = [ph.sb(f"dcp{i}", [64, 512], F32) for i in range(2)]
        obA = [ph.sb(f"obA{i}", [128, 512], F32) for i in range(2)]
        obB = [ph.sb(f"obB{i}", [128, 512], F32) for i in range(2)]
        rbD = [ph.sb(f"rbD{i}", [128, 512], F32) for i in range(2)]
    else:
        acc = ph.ps("acc", [128, 2, 2, 512], F32)
        NACC = 2
    acc_free = [None] * NACC
    den_free = [None]
    rb1 = ph.sb("rb1", [128, 512], F32)
    ob1 = ph.sb("ob1", [128, 512], F32)
    NOS = 2
    ost = [ph.sb(f"ost{i}", [128, 512], BF16) for i in range(NOS)]
    osem = [ph.dsem(f"os{i}") for i in range(NOS)]
    o_free = [None] * NOS
    ones = g["ones1"]

    groups = []
    kvsets = []
    for si, L in enumerate(SEQS):
        nsets = 4 if isD else 2
        for a in range(nsets):
            kvsets.append((si, a))
            subs = [a] if isD else [2 * a, 2 * a + 1]
            for hp in subs:
                for qj in range(L // 512):
                    groups.append(dict(si=si, a=a, hp=hp, qj=qj, L=L, off=P.offs[si], ks=len(kvsets) - 1))
    Ksrc = d["DK"] if isD else None
    Qsrc = d["DQ"] if isD else d["GQ"]
    Ssrc = d["DS"] if isD else d["GS"]
    Ydst = d["YD"] if isD else d["YG"]

    def load_kv(ksi):
        si, a = kvsets[ksi]
        L, off = SEQS[si], P.offs[si]
        b = ksi % NKB
        if isD:
            ph.dma("sp", Kb[b][:, 0:L], d["DK"][a * 128:(a + 1) * 128, off:off + L], ksem[b], deps=[k_free[b]])
            src = d["DV"][off:off + L, a * 128:(a + 1) * 128].rearrange("(c p) e -> p c e", p=128)
        else:
            ph.dma("sp", Kb[b][:, 0:L], d["GK2"][a, :, off:off + L], ksem[b], deps=[k_free[b]])
            src = d["GV"][a, off:off + L, :].rearrange("(c p) e -> p c e", p=128)
        k_ld[b] = ph.dma("sp", Vb[b][:, 0:L // 128, :], src, ksem[b], deps=[k_free[b]])

    def load_q(gi):
        grp = groups[gi]
        b = gi % NQB
        r0 = grp["hp"] * 128
        c0 = grp["off"] + grp["qj"] * 512
        ph.dma("sp", Qb[b][:], Qsrc[r0:r0 + 128, c0:c0 + 512], qsem[b], deps=[q_free[b]])
        q_ld[b] = ph.dma("sp", Gb[b][:], Ssrc[r0:r0 + 128, c0:c0 + 512], qsem[b], deps=[q_free[b]])

    tiles = []
    for gi, grp in enumerate(groups):
        nkc = grp["L"] // 128
        for kc in range(nkc):
            tiles.append((gi, kc, nkc))
    nt = len(tiles)
    qk_tok = [None] * nt
    exp_tok = [None] * nt
    state = {"last_av": None}

    def emit_qk(i):
        gi, kc, nkc = tiles[i]
        grp = groups[gi]
        if kc == 0:
            flush_group(gi - NQB)
        kb = grp["ks"] % NKB
        qb = gi % NQB
        sb_ = i % 2
        near = False
        if isD:
            o = 128 * kc - 512 * grp["qj"]
            near = -256 < o < 640
        ph.op("pe", lambda e: e.matmul(ps_s[:, sb_, 0, :], lhsT=Kb[kb][0:64, kc * 128:(kc + 1) * 128], rhs=Qb[qb][0:64, :],
                                       start=True, stop=not near, tile_position=(0, 0)),
              deps=[k_ld[kb], q_ld[qb], s_free[sb_]])
        qk_tok[i] = ph.op("pe", lambda e: e.matmul(ps_s[:, sb_, 1, :], lhsT=Kb[kb][64:128, kc * 128:(kc + 1) * 128], rhs=Qb[qb][64:128, :],
                                                   start=True, stop=not near, tile_position=(64, 0)))
        if near:
            brhs = g["wstrip"][:, grp["a"], 512 - o:1024 - o]
            ph.op("pe", lambda e: e.matmul(ps_s[:, sb_, 0, :], lhsT=g["ident"][:], rhs=brhs, start=False, stop=True))
            qk_tok[i] = ph.op("pe", lambda e: e.matmul(ps_s[:, sb_, 1, :], lhsT=g["ident"][:], rhs=brhs, start=False, stop=True))

    def emit_exp(i):
        gi, kc, nkc = tiles[i]
        grp = groups[gi]
        sb_ = i % 2
        pb = i % NP
        if isD:
            h = grp["a"]
            o = 128 * kc - 512 * grp["qj"]
            if o <= -256 or o >= 640:
                side = 0 if o <= -256 else 1
                exp_tok[i] = ph.op("act", lambda e: e.activation(out=p_s[pb][:], in_=ps_s[:, sb_, :, :], func=AF.Exp,
                                                                bias=g["farb"][:, h, side:side + 1], scale=0.125),
                                   deps=[qk_tok[i], p_free[pb]])
                s_free[sb_] = exp_tok[i]
            else:
                exp_tok[i] = ph.op("act", lambda e: e.activation(out=p_s[pb][:], in_=ps_s[:, sb_, :, :], func=AF.Exp, scale=0.125),
                                   deps=[qk_tok[i], p_free[pb]])
                s_free[sb_] = exp_tok[i]
        else:
            exp_tok[i] = ph.op("act", lambda e: e.activation(out=p_s[pb][:], in_=ps_s[:, sb_, :, :], func=AF.Exp, scale=0.125),
                               deps=[qk_tok[i], p_free[pb]])
            s_free[sb_] = exp_tok[i]

    def emit_den(i):
        gi, kc, nkc = tiles[i]
        pb = i % NP
        first, last = kc == 0, kc == nkc - 1
        ph.op("pe", lambda e: e.matmul(acc[0:32, 2, :], lhsT=ones[:, 0:32], rhs=p_s[pb][:, 0, :], start=first, stop=last,
                                       tile_position=(0, 0)), deps=[exp_tok[i], den_free[0] if first else None])
        t = ph.op("pe", lambda e: e.matmul(acc[32:64, 2, :], lhsT=ones[:, 0:32], rhs=p_s[pb][:, 1, :], start=first, stop=last,
                                           tile_position=(0, 32)))
        p_free[pb] = t
        return t

    def emit_av(i):
        gi, kc, nkc = tiles[i]
        grp = groups[gi]
        kb = grp["ks"] % NKB
        pb = i % NP
        a_ = gi % NACC
        first, last = kc == 0, kc == nkc - 1
        deps = [exp_tok[i], acc_free[a_] if first else None]
        if isD:
            if i > 0 and not first:
                emit_den(i - 1)
            ph.op("pe", lambda e: e.matmul(acc[:, 0, :], lhsT=Vb[kb][:, kc, :], rhs=p_s[pb][:, 0, :], start=first, stop=last), deps=deps)
            t = ph.op("pe", lambda e: e.matmul(acc[:, 1, :], lhsT=Vb[kb][:, kc, :], rhs=p_s[pb][:, 1, :], start=first, stop=last))
            if last:
                t = emit_den(i)
        else:
            ph.op("pe", lambda e: e.matmul(acc[0:64, a_, 0, :], lhsT=Vb[kb][:, kc, :], rhs=p_s[pb][:, 0, :], start=first, stop=last,
                                           tile_position=(0, 0)), deps=deps)
            ph.op("pe", lambda e: e.matmul(acc[64:128, a_, 0, :], lhsT=Vb[kb][:, kc, :], rhs=p_s[pb][:, 1, :], start=first, stop=last,
                                           tile_position=(0, 64)))
            ph.op("pe", lambda e: e.matmul(acc[0:64, a_, 1, :], lhsT=ones[:, 0:64], rhs=p_s[pb][:, 0, :], start=first, stop=last,
                                           tile_position=(0, 0)))
            t = ph.op("pe", lambda e: e.matmul(acc[64:128, a_, 1, :], lhsT=ones[:, 0:64], rhs=p_s[pb][:, 1, :], start=first, stop=last,
                                               tile_position=(0, 64)))
        if not isD:
            p_free[pb] = t
        state["last_av"] = t
        return t

    pending = {}
    es_last = [None, None]
    sp_state = {"free": None}

    def flush_group(gq):
        for (_due, fn) in pending.pop(gq, []):
            fn()

    def run_due(i):
        for gq in sorted(pending.keys()):
            lst = pending[gq]
            while lst and lst[0][0] <= i:
                lst.pop(0)[1]()
            if not lst:
                pending.pop(gq)

    def finish_group(gi, t11):
        grp = groups[gi]
        qb = gi % NQB
        osl = gi % NOS
        r0 = grp["hp"] * 128
        c0 = grp["off"] + grp["qj"] * 512
        q_free[qb] = t11
        o_free[osl] = ph.dma("sp", Ydst[r0:r0 + 128, c0:c0 + 512], ost[osl][:], osem[osl], deps=[t11])
        if gi + NQB < len(groups):
            load_q(gi + NQB)

    def epilogue(gi, t_last, i_tile):
        grp = groups[gi]
        a_ = gi % NACC
        qb = gi % NQB
        osl = gi % NOS
        if isD:
            es_ = gi % 2
            flush_group(gi - 2)
            oA, oB, dc, sq_, rD = obA[es_], obB[es_], dcp[es_], sqd[es_], rbD[es_]
            c1 = ph.op("dve", lambda e: e.tensor_copy(out=oA[:], in_=acc[:, 0, :]), deps=[t_last, es_last[es_]])
            c2 = ph.op("dve", lambda e: e.tensor_copy(out=oB[:], in_=acc[:, 1, :]), deps=[t_last])
            c3 = ph.op("dve", lambda e: e.tensor_copy(out=dc[:], in_=acc[0:64, 2, :]), deps=[t_last])
            acc_free[a_] = c3
            den_free[0] = c3
            tr = ph.op("dve", lambda e: e.reciprocal(out=dc[:], in_=dc[:]), deps=[c3])
            stt = {}

            def step1():
                b1 = ph.op("pe", lambda e: e.matmul(acc[:, 3, :], lhsT=g["onesf"][0:1, :], rhs=dc[0:1, :], start=True, stop=True),
                           deps=[tr, sp_state["free"]])
                stt["o1"] = ph.op("dve", lambda e: e.tensor_tensor(out=oA[:], in0=oA[:], in1=acc[:, 3, :], op=ALU.mult), deps=[b1, c1])
                sp_state["free"] = stt["o1"]

            def step2():
                b2 = ph.op("pe", lambda e: e.matmul(acc[:, 3, :], lhsT=g["onesf"][32:33, :], rhs=dc[32:33, :], start=True, stop=True),
                           deps=[tr, sp_state["free"]])
                o2 = ph.op("dve", lambda e: e.tensor_tensor(out=oB[:], in0=oB[:], in1=acc[:, 3, :], op=ALU.mult), deps=[b2, c2])
                sp_state["free"] = o2
                t5 = ph.op("dve", lambda e: e.scalar_tensor_tensor(out=oA[:], in0=oB[:], scalar=g["lamt"][:, l, 0:1], in1=oA[:],
                                                                  op0=ALU.mult, op1=ALU.add), deps=[o2, stt["o1"]])
                stt["t5"] = t5
                stt["t6"] = ph.op("act", lambda e: e.activation(out=sq_[:], in_=oA[:], func=AF.Square), deps=[t5])

            def step3():
                t7 = ph.op("pe", lambda e: e.matmul(acc[:, 3, :], lhsT=g["ones128"][:], rhs=sq_[:], start=True, stop=True),
                           deps=[stt["t6"], sp_state["free"]])
                t8 = ph.op("act", lambda e: e.activation(out=rD[:], in_=acc[:, 3, :], func=AF.Sqrt, bias=g["eps"][:], scale=1.0), deps=[t7])
                sp_state["free"] = t8
                t9 = ph.op("dve", lambda e: e.reciprocal(out=rD[:], in_=rD[:]), deps=[t8])
                t10 = ph.op("dve", lambda e: e.scalar_tensor_tensor(out=oA[:], in0=oA[:], scalar=gsub[:, 0:1], in1=rD[:],
                                                                   op0=ALU.mult, op1=ALU.mult), deps=[t9, t_gs, stt["t5"]])
                t11 = ph.op("dve", lambda e: e.tensor_tensor(out=ost[osl][:], in0=oA[:], in1=Gb[qb][:], op=ALU.mult),
                            deps=[t10, q_ld[qb], o_free[osl]])
                es_last[es_] = t11
                finish_group(gi, t11)

            pending[gi] = [(i_tile + 5, step1), (i_tile + 7, step2), (i_tile + 10, step3)]
        else:
            t1 = ph.op("dve", lambda e: e.reciprocal(out=rb1[:], in_=acc[:, a_, 1, :]), deps=[t_last])
            t3 = ph.op("dve", lambda e: e.tensor_tensor(out=ob1[:], in0=acc[:, a_, 0, :], in1=rb1[:], op=ALU.mult), deps=[t1])
            acc_free[a_] = t3
            t11 = ph.op("dve", lambda e: e.tensor_tensor(out=ost[osl][:], in0=ob1[:], in1=Gb[qb][:], op=ALU.mult),
                        deps=[t3, q_ld[qb], o_free[osl]])
            finish_group(gi, t11)

    load_kv(0)
    for gq in range(min(NQB, len(groups))):
        load_q(gq)
    LA = 2
    for i0 in range(min(LA, nt)):
        emit_qk(i0)
    for i in range(nt):
        gi, kc, nkc = tiles[i]
        grp = groups[gi]
        if kc == 0:
            if (gi == 0 or groups[gi - 1]["ks"] != grp["ks"]) and grp["ks"] + 1 < len(kvsets):
                load_kv(grp["ks"] + 1)
        emit_exp(i)
        if i + LA < nt:
            emit_qk(i + LA)
        t = emit_av(i)
        run_due(i)
        if kc == nkc - 1:
            if gi + 1 >= len(groups) or groups[gi + 1]["ks"] != grp["ks"]:
                k_free[grp["ks"] % NKB] = t
            epilogue(gi, t, i)
    for gq in sorted(pending.keys()):
        flush_group(gq)
    ph.run()


def phase_H1(P, l):
    nc, d, g = P.nc, P.d, P.g
    ph = Phase(nc, f"H1{l}")
    cw = ph.sb("cw", [128, 12, 4], F32)
    cl = ph.dsem("cl")
    for j in range(3):
        ph.dma("sp", cw[:, :, j:j + 1], d["hy_conv_w"][l, j:j + 1, :].rearrange("o (c p) -> p c o", p=128), cl, slow=True)
    t_cw = ph.dma("sp", cw[:, :, 3:4], d["hy_conv_b"][l:l + 1, :].rearrange("o (c p) -> p c o", p=128), cl, slow=True)
    BWmax = min(2048, max(SEQS))
    NB = 2
    U = [[ph.sb(f"U{b}_{j}", [128, BWmax + 2], BF16) for j in range(3)] for b in range(NB)]
    SG = [ph.sb(f"SG{b}", [128, BWmax], BF16) for b in range(NB)]
    usem = [ph.dsem(f"us{b}") for b in range(NB)]
    u_free = [None] * NB
    x1c = [ph.sb(f"x1c{i}", [128, 512], F32) for i in range(2)]
    x1c_free = [None] * 2
    hvb = ph.sb("hvb", [128, BWmax], BF16)
    gb = ph.sb("gb", [128, BWmax], BF16)
    gsem = ph.dsem("gs")
    hvT = ph.sb("hvT", [128, BWmax // 128, 128], BF16)
    hsem = ph.dsem("hs")
    tpp = ph.ps("tpp", [128, BWmax // 128, 128], BF16)
    pc = ph.ps("pc", [128, 2, 3, 512], F32)
    pc_free = [[None] * 3 for _ in range(2)]
    dg = ph.sb("dg", [128, 12, 3, 128], BF16)
    t_dg = None
    for ch in range(12):
        for j in range(3):
            t_dg = ph.op("dve", lambda e, ch=ch, j=j: e.tensor_scalar(out=dg[:, ch, j, :], in0=g["ident"][:], scalar1=cw[:, ch, j:j + 1],
                                                                     scalar2=None, op0=ALU.mult), deps=[t_cw])
    blocks = []
    for si, L in enumerate(SEQS):
        BW = min(2048, L)
        for cc in range(4):
            for c0 in range(0, L, BW):
                blocks.append((si, L, P.offs[si], cc, c0, BW))
    ld_tok = [None] * NB
    ms_tok = [None] * NB

    def load(bi):
        si, L, off, cc, c0, BW = blocks[bi]
        b = bi % NB
        lo = 1 if c0 == 0 else 0
        hi = BW + 1 if c0 + BW == L else BW + 2
        mt = None
        for j in range(3):
            row0 = (j * 4 + cc) * 128
            if lo == 1:
                mt = ph.op("pool", lambda e, b=b, j=j: e.memset(U[b][j][:, 0:1], 0.0), deps=[u_free[b]])
            if hi == BW + 1:
                mt = ph.op("pool", lambda e, b=b, j=j, BW=BW: e.memset(U[b][j][:, BW + 1:BW + 2], 0.0), deps=[u_free[b]])
            ph.dma("sp", U[b][j][:, lo:hi], d["UH"][row0:row0 + 128, off + c0 - 1 + lo:off + c0 - 1 + hi], usem[b], deps=[u_free[b]])
        row0 = (12 + cc) * 128
        ld_tok[b] = ph.dma("sp", SG[b][:, 0:BW], d["UH"][row0:row0 + 128, off + c0:off + c0 + BW], usem[b], deps=[u_free[b]])
        ms_tok[b] = mt

    g_st = None
    h_st = None
    tp_free = None
    load(0)
    for bi, (si, L, off, cc, c0, BW) in enumerate(blocks):
        b = bi % NB
        if bi + 1 < len(blocks):
            load(bi + 1)
        t_hv = None
        t_g = None
        for ct in range(BW // 512):
            pb_ = (bi * 4 + ct) % 2
            c0c = ct * 512
            mm = []
            for j in range(3):
                ch = j * 4 + cc
                for tap in range(3):
                    t = ph.op("pe", lambda e, pb_=pb_, j=j, ch=ch, tap=tap, b=b, c0c=c0c: e.matmul(
                        pc[:, pb_, j, :], lhsT=dg[:, ch, tap, :], rhs=U[b][j][:, c0c + tap:c0c + tap + 512], start=(tap == 0), stop=(tap == 2)),
                        deps=[t_dg, ld_tok[b], ms_tok[b], pc_free[pb_][j]])
                mm.append(t)
            xi = (bi * 4 + ct) % 2
            t_x1 = ph.op("act", lambda e, pb_=pb_, xi=xi, cc=cc: e.activation(out=x1c[xi][:], in_=pc[:, pb_, 1, :], func=AF.Identity,
                                                                             bias=cw[:, 4 + cc, 3:4], scale=1.0),
                         deps=[mm[1], x1c_free[xi]])
            pc_free[pb_][1] = t_x1
            t_hv = ph.op("dve", lambda e, pb_=pb_, xi=xi, cc=cc, c0c=c0c: e.scalar_tensor_tensor(
                out=hvb[:, c0c:c0c + 512], in0=pc[:, pb_, 2, :], scalar=cw[:, 8 + cc, 3:4], in1=x1c[xi][:], op0=ALU.add, op1=ALU.mult),
                deps=[mm[2], t_x1, tp_free])
            pc_free[pb_][2] = t_hv
            x1c_free[xi] = t_hv
            t_g = ph.op("dve", lambda e, pb_=pb_, cc=cc, b=b, c0c=c0c: e.scalar_tensor_tensor(
                out=gb[:, c0c:c0c + 512], in0=pc[:, pb_, 0, :], scalar=cw[:, cc, 3:4], in1=SG[b][:, c0c:c0c + 512], op0=ALU.add, op1=ALU.mult),
                deps=[mm[0], g_st])
            pc_free[pb_][0] = t_g
        u_free[b] = t_g
        g_st = ph.dma("sp", d["GG"][cc * 128:(cc + 1) * 128, off + c0:off + c0 + BW], gb[:, 0:BW], gsem, deps=[t_g])
        ns = BW // 128
        last = None
        for s in range(ns):
            last = ph.op("pe", lambda e, s=s: e.transpose(out=tpp[:, s, :], in_=hvb[:, s * 128:(s + 1) * 128], identity=g["ident"][:]),
                         deps=[t_hv, h_ev if s == 0 and bi > 0 else None])
        tp_free = last
        h_ev = ph.op("act", lambda e, ns=ns: e.activation(out=hvT[:, 0:ns, :], in_=tpp[:, 0:ns, :], func=AF.Copy), deps=[last, h_st])
        h_st = ph.dma("sp", d["HV"][off + c0:off + c0 + BW, cc * 128:(cc + 1) * 128].rearrange("(s p) c -> p s c", p=128),
                      hvT[:, 0:ns, :], hsem, deps=[h_ev])
    ph.run()


def phase_H2(P, l, si):
    nc, d, g = P.nc, P.d, P.g
    L, off = SEQS[si], P.offs[si]
    N1, N2 = fft_dims(L)
    H1 = N1 // 2
    key = f"{L}"
    if f"AD{key}_0" not in d:
        for ri in range(2):
            P.scr(f"AD{key}_{ri}", [N2, N1, HW], BF16)
            P.scr(f"DD{key}_{ri}", [N1, N2, HW], BF16)
    AD = [d[f"AD{key}_{ri}"] for ri in range(2)]
    DD = [d[f"DD{key}_{ri}"] for ri in range(2)]
    Hd = [d[f"H{key}_{ri}"] for ri in range(2)]

    ph = Phase(nc, f"h2a{l}_{si}")
    fft_stage1(ph, P, L, [d["HV"][off:off + L, :]], [[0]], [AD], "a")
    ph.run()

    ph = Phase(nc, f"h2b{l}_{si}")
    f2, t_f2 = fft_stage2_tables(ph, P, L, "b")
    G = FG
    NB = 2
    xin = [ph.sb(f"xin{b}", [N2, 2, G, HW], BF16) for b in range(NB)]
    hin = [ph.sb(f"hin{b}", [N2, 2, G, HW], BF16) for b in range(NB)]
    xsem = [ph.dsem(f"xs{b}") for b in range(NB)]
    dst = [ph.sb(f"dst{b}", [N2, 2, G, HW], BF16) for b in range(NB)]
    dsem_ = [ph.dsem(f"ds{b}") for b in range(NB)]
    NT = 3
    tq = [ph.sb(f"tq{q}", [N2, 4, HW], BF16) for q in range(NT)]
    y_free = [None] * NT
    NS = 8
    pp = ph.ps("pp", [N2, NS, HW], F32)
    pp_free = [None] * NS
    x_free = [None] * NB
    d_free = [None] * NB
    x_ld = [None] * NB
    st = {"slot": 0, "q": 0}

    def getslot():
        s = st["slot"] % NS
        st["slot"] += 1
        return s

    def load(gi):
        b = gi % NB
        k0 = gi * G
        for ri in range(2):
            ph.dma("sp", xin[b][:, ri, :, :], AD[ri][:, k0:k0 + G, :], xsem[b], deps=[x_free[b]])
        for ri in range(2):
            x_ld[b] = ph.dma("sp", hin[b][:, ri, :, :], Hd[ri][:, k0:k0 + G, :], xsem[b], deps=[x_free[b]])

    ngrp = N1 // G
    items = [(gi, kl) for gi in range(ngrp) for kl in range(G)]
    f2tok = {}

    def emit_f2(i):
        gi, kl = items[i]
        b = gi % NB
        sr, si_ = getslot(), getslot()
        ph.op("pe", lambda e: e.matmul(pp[:, sr, :], lhsT=f2[:, 0, :], rhs=xin[b][:, 0, kl, :], start=True, stop=False),
              deps=[t_f2, x_ld[b], pp_free[sr]])
        t_br = ph.op("pe", lambda e: e.matmul(pp[:, sr, :], lhsT=f2[:, 1, :], rhs=xin[b][:, 1, kl, :], start=False, stop=True))
        ph.op("pe", lambda e: e.matmul(pp[:, si_, :], lhsT=f2[:, 0, :], rhs=xin[b][:, 1, kl, :], start=True, stop=False),
              deps=[pp_free[si_]])
        t_bi = ph.op("pe", lambda e: e.matmul(pp[:, si_, :], lhsT=f2[:, 2, :], rhs=xin[b][:, 0, kl, :], start=False, stop=True))
        f2tok[i] = (sr, si_, t_br, t_bi)

    load(0)
    emit_f2(0)
    evs = []
    for i, (gi, kl) in enumerate(items):
        b = gi % NB
        if kl == 0:
            evs = []
            if gi + 1 < ngrp:
                load(gi + 1)
        if i + 1 < len(items):
            emit_f2(i + 1)
        sr, si_, t_br, t_bi = f2tok.pop(i)
        q = i % NT
        m1 = ph.op("dve", lambda e, q=q, sr=sr, b=b, kl=kl: e.tensor_tensor(out=tq[q][:, 0, :], in0=pp[:, sr, :], in1=hin[b][:, 0, kl, :], op=ALU.mult),
                   deps=[t_br, x_ld[b], y_free[q]])
        m2 = ph.op("dve", lambda e, q=q, si_=si_, b=b, kl=kl: e.tensor_tensor(out=tq[q][:, 1, :], in0=pp[:, si_, :], in1=hin[b][:, 1, kl, :], op=ALU.mult),
                   deps=[t_bi])
        m3 = ph.op("dve", lambda e, q=q, sr=sr, b=b, kl=kl: e.tensor_tensor(out=tq[q][:, 2, :], in0=pp[:, sr, :], in1=hin[b][:, 1, kl, :], op=ALU.mult))
        m4 = ph.op("dve", lambda e, q=q, si_=si_, b=b, kl=kl: e.tensor_tensor(out=tq[q][:, 3, :], in0=pp[:, si_, :], in1=hin[b][:, 0, kl, :], op=ALU.mult))
        pp_free[sr] = m3
        pp_free[si_] = m4
        dr, di = getslot(), getslot()
        for n_, (tbl, src) in enumerate([(0, 0), (3, 1), (2, 2), (2, 3)]):
            t_dr = ph.op("pe", lambda e, dr=dr, q=q, tbl=tbl, src=src, n_=n_: e.matmul(pp[:, dr, :], lhsT=f2[:, tbl, :], rhs=tq[q][:, src, :],
                                                                                   start=(n_ == 0), stop=(n_ == 3)),
                         deps=[m4, pp_free[dr]] if n_ == 0 else [])
        for n_, (tbl, src) in enumerate([(0, 2), (0, 3), (1, 0), (2, 1)]):
            t_di = ph.op("pe", lambda e, di=di, q=q, tbl=tbl, src=src, n_=n_: e.matmul(pp[:, di, :], lhsT=f2[:, tbl, :], rhs=tq[q][:, src, :],
                                                                                   start=(n_ == 0), stop=(n_ == 3)),
                         deps=[pp_free[di]] if n_ == 0 else [])
        y_free[q] = t_di
        e1 = ph.op("act", lambda e, dr=dr, b=b, kl=kl: e.activation(out=dst[b][:, 0, kl, :], in_=pp[:, dr, :], func=AF.Copy),
                   deps=[t_dr, d_free[b]])
        e2 = ph.op("act", lambda e, di=di, b=b, kl=kl: e.activation(out=dst[b][:, 1, kl, :], in_=pp[:, di, :], func=AF.Copy),
                   deps=[t_di, d_free[b]])
        pp_free[dr] = e1
        pp_free[di] = e2
        evs += [e1, e2]
        if kl == G - 1:
            x_free[b] = m4
            t = None
            for ri in range(2):
                t = ph.dma("sp", DD[ri][gi * G:(gi + 1) * G, :, :].rearrange("k n c -> n k c"), dst[b][:, ri, :, :], dsem_[b], deps=evs)
            d_free[b] = t
    ph.run()

    ph = Phase(nc, f"h2c{l}_{si}")
    ic = ph.sb("ic", [N1, N2, H1], BF16)
    isn = ph.sb("isn", [N1, N2, H1], BF16)
    ldt = ph.dsem("ldt")
    ph.dma("sp", ic[:], d[f"c_i2c{L}"][:], ldt)
    t_tab = ph.dma("sp", isn[:], d[f"c_i2s{L}"][:], ldt)
    yc = ph.sb("yc", [128, 4, L], BF16)
    dd = [ph.sb(f"dd{b}", [N1, 2, G, HW], BF16) for b in range(NB)]
    ddsem = [ph.dsem(f"dds{b}") for b in range(NB)]
    dd_free = [None] * NB
    dd_ld = [None] * NB
    NZ = 4
    pz = ph.ps("pz", [128, NZ, 512], F32)
    pz_free = [None] * NZ
    zc = 0

    def load2(gi):
        b = gi % NB
        for ri in range(2):
            dd_ld[b] = ph.dma("sp", dd[b][:, ri, :, :], DD[ri][:, gi * G:(gi + 1) * G, :], ddsem[b], deps=[dd_free[b]])

    ngrp = N2 // G
    load2(0)
    evs = []
    for gi in range(ngrp):
        b = gi % NB
        if gi + 1 < ngrp:
            load2(gi + 1)
        last = None
        for cc in range(4):
            z = zc % NZ
            zc += 1
            for n2l in range(G):
                n2 = gi * G + n2l
                ph.op("pe", lambda e, z=z, n2l=n2l, n2=n2, b=b, cc=cc: e.matmul(pz[:, z, n2l * H1:(n2l + 1) * H1], lhsT=dd[b][:, 0, n2l, cc * 128:(cc + 1) * 128],
                                                                              rhs=ic[:, n2, :], start=True, stop=False),
                      deps=[t_tab, dd_ld[b], pz_free[z]])
                last = ph.op("pe", lambda e, z=z, n2l=n2l, n2=n2, b=b, cc=cc: e.matmul(pz[:, z, n2l * H1:(n2l + 1) * H1], lhsT=dd[b][:, 1, n2l, cc * 128:(cc + 1) * 128],
                                                                                     rhs=isn[:, n2, :], start=False, stop=True))
            dstv = yc[:, cc, :].rearrange("p (a n) -> p n a", n=N2)[:, gi * G:(gi + 1) * G, :]
            if cc % 2 == 0:
                ev = ph.op("act", lambda e, z=z, dstv=dstv: e.activation(out=dstv, in_=pz[:, z, 0:G * H1].rearrange("p (g a) -> p g a", a=H1), func=AF.Copy), deps=[last])
            else:
                ev = ph.op("dve", lambda e, z=z, dstv=dstv: e.tensor_copy(out=dstv, in_=pz[:, z, 0:G * H1].rearrange("p (g a) -> p g a", a=H1)), deps=[last])
            pz_free[z] = ev
            evs.append(ev)
        dd_free[b] = last
    ysem = ph.dsem("ys")
    for cc in range(4):
        ph.dma("sp", d["YC"][cc * 128:(cc + 1) * 128, off:off + L], yc[:, cc, :], ysem, deps=evs[-8:])
    ph.run()


def phase_M(P, l):
    nc, d, g = P.nc, P.d, P.g
    T = P.T
    last_layer = l == DEPTH - 1
    xsrc = d["x"] if l == 0 else d["XR"]
    xdst = d["y"] if last_layer else d["XR"]
    ph = Phase(nc, f"M{l}")
    Wm = ph.sb("Wm", [128, 8, 3 * D], BF16)
    Wbr = ph.sb("Wbr", [128, 3, 4, D], BF16)
    Wo = ph.sb("Wo", [128, 8, D], BF16)
    bm = ph.sb("bm", [128, 24], F32)
    fg = ph.sb("fg", [128, D], F32)
    wl = P.gsem[0]
    t_w = None
    for k in range(8):
        t_w = ph.dma("pool", Wm[:, k, :], d["w_merge"][l, k * 128:(k + 1) * 128, :], wl)
    for j, nm in enumerate(["w_branch_hy", "w_branch_gqa", "w_branch_diff"]):
        for k in range(4):
            t_w = ph.dma("pool", Wbr[:, j, k, :], d[nm][l, k * 128:(k + 1) * 128, :], wl)
    for k in range(8):
        t_w = ph.dma("pool", Wo[:, k, :], d["w_out"][l, k * 128:(k + 1) * 128, :], wl)
    cl = ph.dsem("cl")
    ph.dma("sp", bm[:], d["b_merge"][l:l + 1, :].rearrange("o (j p) -> p (o j)", p=128), cl, slow=True)
    t_c = ph.dma("sp", fg[:], bcast_rows(d["final_g"].rearrange("(o f) -> o f", o=1), 128), cl)

    NB = 2
    hT = [ph.sb(f"hT{b}", [128, 8, 512], BF16) for b in range(NB)]
    Y = [ph.sb(f"Y{b}", [128, 4, 4, 512], BF16) for b in range(NB)]
    isem = [ph.dsem(f"is{b}") for b in range(NB)]
    in_free = [None] * NB
    in_ld = [None] * NB
    xt = [ph.sb(f"xt{s}", [128, D], F32) for s in range(4)]
    xsem = [ph.dsem(f"xs{s}") for s in range(4)]
    x_free = [None] * 4
    x_ld = [None] * 4
    gt = [ph.sb(f"gt{j}", [128, 512], F32) for j in range(3)]
    gt_free = [None] * 3
    acc = ph.sb("acc", [128, 512], F32)
    tm1 = ph.sb("tm1", [128, 512], F32)
    tm2 = ph.sb("tm2", [128, 512], F32)
    mg = ph.sb("mg", [128, 8, 512], BF16)
    mg_free = None
    xo = [ph.sb(f"xo{i}", [128, D], F32) for i in range(2)]
    xosem = [ph.dsem(f"xos{i}") for i in range(2)]
    xo_free = [None] * 2
    junk = ph.sb("junk", [128, D], BF16)
    ssq = ph.sb("ssq", [128, 2], F32)
    NS = 7
    pg = ph.ps("pg", [128, NS, 512], F32)
    slot_free = [None] * NS
    st = {"slot": 0, "xo": 0}

    def getslot():
        s = st["slot"] % NS
        st["slot"] += 1
        return s

    ntile = T // 512

    def load_in(tt):
        b = tt % NB
        t0 = tt * 512
        ph.dma("sp", hT[b][:], d["HT"][:, t0:t0 + 512].rearrange("(k p) t -> p k t", p=128), isem[b], deps=[in_free[b]])
        for j, nm in enumerate(["YC", "GG", "YG", "YD"]):
            in_ld[b] = ph.dma("sp", Y[b][:, j, :, :], d[nm][:, t0:t0 + 512].rearrange("(k p) t -> p k t", p=128), isem[b], deps=[in_free[b]])

    load_in(0)
    a_rd = None
    for tt in range(ntile):
        b = tt % NB
        t0 = tt * 512
        if tt + 1 < ntile:
            load_in(tt + 1)
        for s in range(4):
            x_ld[s] = ph.dma("sp", xt[s][:], xsrc[t0 + s * 128:t0 + (s + 1) * 128, :], xsem[s], deps=[x_free[s]])
        t_yh = ph.op("dve", lambda e, b=b: e.tensor_tensor(out=Y[b][:, 0, :, :], in0=Y[b][:, 0, :, :], in1=Y[b][:, 1, :, :], op=ALU.mult),
                     deps=[in_ld[b]])
        ysel = [0, 2, 3]
        last_pe = None
        for m in range(8):
            tg = []
            for j in range(3):
                su = getslot()
                for k in range(8):
                    last_pe = ph.op("pe", lambda e, su=su, k=k, j=j, m=m, b=b: e.matmul(pg[:, su, :], lhsT=Wm[:, k, j * D + m * 128:j * D + (m + 1) * 128],
                                                                                      rhs=hT[b][:, k, :], start=(k == 0), stop=(k == 7)),
                                    deps=[t_w, in_ld[b], slot_free[su]])
                t = ph.op("act", lambda e, su=su, j=j, m=m: e.activation(out=gt[j][:], in_=pg[:, su, :], func=AF.Sigmoid,
                                                                       bias=bm[:, j * 8 + m:j * 8 + m + 1], scale=1.0),
                          deps=[last_pe, gt_free[j], t_c])
                slot_free[su] = t
                tg.append(t)
            tb = []
            sb_ = []
            for j in range(3):
                su = getslot()
                sb_.append(su)
                for k in range(4):
                    last_pe = ph.op("pe", lambda e, su=su, k=k, j=j, m=m, b=b: e.matmul(pg[:, su, :], lhsT=Wbr[:, j, k, m * 128:(m + 1) * 128],
                                                                                      rhs=Y[b][:, ysel[j], k, :], start=(k == 0), stop=(k == 3)),
                                    deps=[t_yh, slot_free[su]])
                tb.append(last_pe)
            d0 = ph.op("dve", lambda e, su=sb_[0]: e.tensor_tensor(out=acc[:], in0=pg[:, su, :], in1=gt[0][:], op=ALU.mult),
                       deps=[tb[0], tg[0], a_rd])
            d1 = ph.op("dve", lambda e, su=sb_[1]: e.tensor_tensor(out=tm1[:], in0=pg[:, su, :], in1=gt[1][:], op=ALU.mult),
                       deps=[tb[1], tg[1]])
            d2 = ph.op("dve", lambda e, su=sb_[2]: e.tensor_tensor(out=tm2[:], in0=pg[:, su, :], in1=gt[2][:], op=ALU.mult),
                       deps=[tb[2], tg[2]])
            slot_free[sb_[0]] = d0
            slot_free[sb_[1]] = d1
            slot_free[sb_[2]] = d2
            gt_free[0] = d0
            gt_free[1] = d1
            gt_free[2] = d2
            p1 = ph.op("pool", lambda e: e.tensor_tensor(out=acc[:], in0=acc[:], in1=tm1[:], op=ALU.add), deps=[d0, d1])
            p2 = ph.op("pool", lambda e, m=m: e.tensor_tensor(out=mg[:, m, :], in0=acc[:], in1=tm2[:], op=ALU.add), deps=[p1, d2, mg_free])
            a_rd = p2
        in_free[b] = last_pe
        mg_rdy = a_rd
        for s in range(4):
            xi = st["xo"] % 2
            st["xo"] += 1
            tr = []
            for hh in range(2):
                su = getslot()
                for k in range(8):
                    last_pe = ph.op("pe", lambda e, su=su, k=k, s=s, hh=hh: e.matmul(pg[:, su, :], lhsT=mg[:, k, s * 128:(s + 1) * 128],
                                                                                   rhs=Wo[:, k, hh * 512:(hh + 1) * 512], start=(k == 0), stop=(k == 7)),
                                    deps=[mg_rdy, slot_free[su]])
                t = ph.op("dve", lambda e, su=su, s=s, hh=hh, xi=xi: e.tensor_tensor(out=xo[xi][:, hh * 512:(hh + 1) * 512], in0=pg[:, su, :],
                                                                                   in1=xt[s][:, hh * 512:(hh + 1) * 512], op=ALU.add),
                          deps=[last_pe, x_ld[s], xo_free[xi]])
                slot_free[su] = t
                tr.append(t)
            x_free[s] = tr[-1]
            r0 = t0 + s * 128
            if last_layer:
                c = xi
                t_ss = ph.op("act", lambda e, xi=xi, c=c: e.activation(out=junk[:], in_=xo[xi][:], func=AF.Square, accum_out=ssq[:, c:c + 1]),
                             deps=tr)
                t_sd = ph.op("act", lambda e, c=c: e.activation(out=ssq[:, c:c + 1], in_=ssq[:, c:c + 1], func=AF.Sqrt, bias=g["eps"][:], scale=1.0 / D),
                             deps=[t_ss])
                t_r = ph.op("dve", lambda e, c=c: e.reciprocal(out=ssq[:, c:c + 1], in_=ssq[:, c:c + 1]), deps=[t_sd])
                t_y = ph.op("dve", lambda e, xi=xi, c=c: e.scalar_tensor_tensor(out=xo[xi][:], in0=xo[xi][:], scalar=ssq[:, c:c + 1], in1=fg[:],
                                                                             op0=ALU.mult, op1=ALU.mult), deps=[t_r, t_c])
                xo_free[xi] = ph.dma("sp", xdst[r0:r0 + 128, :], xo[xi][:], xosem[xi], deps=[t_y])
            else:
                xo_free[xi] = ph.dma("sp", xdst[r0:r0 + 128, :], xo[xi][:], xosem[xi], deps=tr)
        mg_free = last_pe
    ph.run()


def build(stop=None):
    P = Prog()
    stop = stop or STOP_AFTER
    phase_prep(P)
    if stop == "prep":
        return P
    for l in range(DEPTH):
        for L in P.Ls:
            phase_filter(P, l, L)
        if stop == f"filter{l}":
            return P
        phase_A(P, l)
        if stop == f"A{l}":
            return P
        phase_H1(P, l)
        if stop == f"H1{l}":
            return P
        for si in range(len(SEQS)):
            phase_H2(P, l, si)
        if stop == f"H2{l}":
            return P
        phase_attn(P, l, "G")
        if stop == f"G{l}":
            return P
        phase_attn(P, l, "D")
        if stop == f"D{l}":
            return P
        phase_M(P, l)
        if stop == f"M{l}":
            return P
    return P


def make_in_maps(inputs, ncores):
    consts = host_consts()
    xp = np.asarray(inputs["x_prompt"], np.float32)
    xs = np.asarray(inputs["x_sample"], np.float32)
    maps = []
    for c in range(ncores):
        m = {}
        m["x"] = np.ascontiguousarray(np.concatenate([xp[c], xs[2 * c], xs[2 * c + 1]], axis=0))
        for k in PARAM_NAMES:
            m[k] = np.ascontiguousarray(np.asarray(inputs[k], np.float32))
        for k, v in consts.items():
            m["c_" + k] = v
        maps.append(m)
    return maps


def kernel(**inputs):
    P = build()
    maps = make_in_maps(inputs, NCORES)
    res = run_bass_kernel_spmd(P.nc, maps, core_ids=list(range(NCORES)))
    Lp, Ls = SEQS[0], SEQS[1]
    yp = np.stack([res.results[c]["y"][:Lp] for c in range(NCORES)], 0)
    ys = np.stack([res.results[c]["y"][Lp + j * Ls: Lp + (j + 1) * Ls] for c in range(NCORES) for j in range(2)], 0)
    return (np.ascontiguousarray(yp, dtype=np.float32), np.ascontiguousarray(ys, dtype=np.float32))
```
